# Optimizing a Trainium2 kernel written in Bass

```python
import jax, jax.numpy as jnp
from jax import lax
import numpy as np

D_MODEL = 1024
BATCH = 1
SEQ = 16384
DEPTH = 4

N_A = DEPTH // 2
N_B = DEPTH - N_A
M_HEADS = 4
M_DQK = 128
M_DV = D_MODEL // M_HEADS
M_CHUNK = 64
M_SPLITS = [M_HEADS * M_DQK, 2 * M_HEADS * M_DQK, 2 * M_HEADS * M_DQK + M_HEADS * M_DV,
            2 * M_HEADS * M_DQK + 2 * M_HEADS * M_DV, 2 * M_HEADS * M_DQK + 2 * M_HEADS * M_DV + M_HEADS]
M_IN = 2 * M_HEADS * M_DQK + 2 * M_HEADS * M_DV + 2 * M_HEADS
A_HEADS = 8
A_DH = D_MODEL // A_HEADS
ROT_DIM = A_DH // 4
ROPE_THETA = 500000.0
MOBA_BLOCK = 256
MOBA_TOPK = 3
Q_BLOCK = 128
D_FF = 2816
CONV_W = 3
EPS = 1e-6

kernel_name = 'yoco_mlstm_moba_convffn'


def rmsnorm(x, g):
    xf = x.astype(jnp.float32)
    y = xf * lax.rsqrt(jnp.mean(xf * xf, axis=-1, keepdims=True) + EPS)
    return (y * g.astype(jnp.float32)).astype(x.dtype)


def rope_tables(s_len):
    pos = jnp.arange(s_len, dtype=jnp.float32)
    inv = ROPE_THETA ** (-jnp.arange(0, ROT_DIM, 2, dtype=jnp.float32) / ROT_DIM)
    ang = pos[:, None] * inv[None, :]
    return jnp.cos(ang), jnp.sin(ang)


def apply_partial_rope(t, cos, sin):
    half = ROT_DIM // 2
    t1, t2, rest = t[..., :half], t[..., half:ROT_DIM], t[..., ROT_DIM:]
    c = cos[None, :, None, :].astype(t.dtype)
    s = sin[None, :, None, :].astype(t.dtype)
    return jnp.concatenate([t1 * c - t2 * s, t2 * c + t1 * s, rest], axis=-1)


def mlstm_chunkwise(q, k, v, i_pre, f_pre):
    b_, s_, h_, dk = q.shape
    nc = s_ // M_CHUNK

    def to_chunks(t):
        t = t.astype(jnp.float32).reshape((b_, nc, M_CHUNK, h_) + t.shape[3:])
        return jnp.moveaxis(jnp.moveaxis(t, 1, 0), 3, 2)

    qc, kc, vc = to_chunks(q), to_chunks(k), to_chunks(v)
    ic = to_chunks(i_pre)
    lfc = to_chunks(jax.nn.log_sigmoid(f_pre.astype(jnp.float32)))
    tri = jnp.tril(jnp.ones((M_CHUNK, M_CHUNK), dtype=bool))

    def step(carry, inp):
        c_st, n_st, m_st = carry
        qb, kb, vb, ib, lfb = inp
        bcum = jnp.cumsum(lfb, axis=-1)
        dmat = jnp.where(tri, bcum[..., :, None] - bcum[..., None, :] + ib[..., None, :], -jnp.inf)
        inter = bcum + m_st[..., None]
        m_t = jnp.maximum(inter, jnp.max(dmat, axis=-1))
        w_inter = jnp.exp(inter - m_t)
        s = jnp.einsum('bhtd,bhsd->bhts', qb, kb) * jnp.exp(dmat - m_t[..., None])
        num = w_inter[..., None] * jnp.einsum('bhtd,bhde->bhte', qb, c_st) + jnp.einsum('bhts,bhse->bhte', s, vb)
        den = w_inter * jnp.einsum('bhtd,bhd->bht', qb, n_st) + jnp.sum(s, axis=-1)
        h = num / jnp.maximum(jnp.abs(den), jnp.exp(-m_t))[..., None]
        b_last = bcum[..., -1]
        g = b_last[..., None] - bcum + ib
        m_new = jnp.maximum(b_last + m_st, jnp.max(g, axis=-1))
        decay = jnp.exp(b_last + m_st - m_new)
        wk = jnp.exp(g - m_new[..., None])[..., None] * kb
        c_new = decay[..., None, None] * c_st + jnp.einsum('bhsd,bhse->bhde', wk, vb)
        n_new = decay[..., None] * n_st + jnp.sum(wk, axis=-2)
        return (c_new, n_new, m_new), h

    dv = v.shape[-1]
    init = (jnp.zeros((b_, h_, dk, dv), jnp.float32), jnp.zeros((b_, h_, dk), jnp.float32),
            jnp.zeros((b_, h_), jnp.float32))
    _, hs = lax.scan(step, init, (qc, kc, vc, ic, lfc))
    return jnp.transpose(hs, (1, 0, 3, 2, 4)).reshape(b_, s_, h_, dv)


def mlstm_mixer(x, norm_g, w_in, b_gates, h_norm_g, w_out):
    b_, s_, _ = x.shape
    u = rmsnorm(x, norm_g) @ w_in
    q, k, v, o, gi, gf = jnp.split(u, M_SPLITS, axis=-1)
    q = q.reshape(b_, s_, M_HEADS, M_DQK) * (M_DQK ** -0.5)
    k = k.reshape(b_, s_, M_HEADS, M_DQK)
    v = v.reshape(b_, s_, M_HEADS, M_DV)
    i_pre = gi.astype(jnp.float32) + b_gates[0]
    f_pre = gf.astype(jnp.float32) + b_gates[1]
    h = mlstm_chunkwise(q, k, v, i_pre, f_pre)
    h = rmsnorm(h, h_norm_g.reshape(M_HEADS, M_DV)).reshape(b_, s_, M_HEADS * M_DV)
    h = h * jax.nn.sigmoid(o.astype(jnp.float32))
    return h.astype(x.dtype) @ w_out


def conv_ffn(x, norm_g, w_up, conv_w, conv_b, w_down):
    s_ = x.shape[1]
    u = rmsnorm(x, norm_g) @ w_up
    up = jnp.pad(u, ((0, 0), (CONV_W - 1, 0), (0, 0)))
    c = conv_b + up[:, 0:s_] * conv_w[0]
    for j in range(1, CONV_W):
        c = c + up[:, j:j + s_] * conv_w[j]
    val, gate = jnp.split(c, 2, axis=-1)
    return (jax.nn.silu(gate) * val) @ w_down


def shared_kv(x, norm_g, w_kv, k_norm_g, cos, sin):
    b_, s_, _ = x.shape
    kv = rmsnorm(x, norm_g) @ w_kv
    k, v = jnp.split(kv, 2, axis=-1)
    k = apply_partial_rope(rmsnorm(k.reshape(b_, s_, A_HEADS, A_DH), k_norm_g), cos, sin)
    v = v.reshape(b_, s_, A_HEADS, A_DH)
    nb = -(-s_ // MOBA_BLOCK)
    pad = nb * MOBA_BLOCK - s_
    k = jnp.pad(k, ((0, 0), (0, pad), (0, 0), (0, 0)))
    v = jnp.pad(v, ((0, 0), (0, pad), (0, 0), (0, 0)))
    kb = k.reshape(b_, nb, MOBA_BLOCK, A_HEADS, A_DH).transpose(0, 3, 1, 2, 4)
    vb = v.reshape(b_, nb, MOBA_BLOCK, A_HEADS, A_DH).transpose(0, 3, 1, 2, 4)
    kmean = jnp.mean(kb.astype(jnp.float32), axis=3).astype(kb.dtype)
    return kb, vb, kmean


def moba_mixer(x, norm_g, w_q, q_norm_g, w_o, kb, vb, kmean, cos, sin):
    b_, s_, _ = x.shape
    q = (rmsnorm(x, norm_g) @ w_q).reshape(b_, s_, A_HEADS, A_DH)
    q = apply_partial_rope(rmsnorm(q, q_norm_g), cos, sin) * (A_DH ** -0.5)
    q = q.transpose(0, 2, 1, 3)
    nb = kb.shape[2]
    n_topk = min(MOBA_TOPK, nb)
    bidx = jnp.arange(b_)[:, None, None, None]
    hidx = jnp.arange(A_HEADS)[None, :, None, None]
    n_sel = n_topk * MOBA_BLOCK

    def attend_block(qi):
        start = qi * Q_BLOCK
        cur = start // MOBA_BLOCK
        qb = lax.dynamic_slice_in_dim(q, start, Q_BLOCK, axis=2)
        gate = jnp.einsum('bhqd,bhnd->bhqn', qb, kmean).astype(jnp.float32)
        gate = jnp.where(jnp.arange(nb) < cur, gate, -jnp.inf)
        _, sel = lax.top_k(gate, n_topk)
        sel_valid = jnp.arange(n_topk) < cur
        kg = kb[bidx, hidx, sel]
        vg = vb[bidx, hidx, sel]
        l_sel = jnp.einsum('bhqd,bhqjkd->bhqjk', qb, kg).astype(jnp.float32)
        l_sel = jnp.where(sel_valid[:, None], l_sel, -jnp.inf).reshape(b_, A_HEADS, Q_BLOCK, n_sel)
        ko = lax.dynamic_index_in_dim(kb, cur, axis=2, keepdims=False)
        vo = lax.dynamic_index_in_dim(vb, cur, axis=2, keepdims=False)
        l_own = jnp.einsum('bhqd,bhkd->bhqk', qb, ko).astype(jnp.float32)
        qpos = start + jnp.arange(Q_BLOCK)
        kpos = cur * MOBA_BLOCK + jnp.arange(MOBA_BLOCK)
        l_own = jnp.where(kpos[None, :] <= qpos[:, None], l_own, -jnp.inf)
        p = jax.nn.softmax(jnp.concatenate([l_sel, l_own], axis=-1), axis=-1).astype(vb.dtype)
        p_sel = p[..., :n_sel].reshape(b_, A_HEADS, Q_BLOCK, n_topk, MOBA_BLOCK)
        p_own = p[..., n_sel:]
        return jnp.einsum('bhqjk,bhqjkd->bhqd', p_sel, vg) + jnp.einsum('bhqk,bhkd->bhqd', p_own, vo)

    o = lax.map(attend_block, jnp.arange(s_ // Q_BLOCK))
    o = o.transpose(1, 0, 3, 2, 4).reshape(b_, s_, A_HEADS * A_DH)
    return o @ w_o


def setup_inputs(seed: int = 0) -> dict:
    key = jax.random.key(seed)
    ks = jax.random.split(key, 20)
    f32 = jnp.float32

    def w(k, shape, fan_in, scale=1.0):
        return jax.random.normal(k, shape, f32) * (scale * fan_in ** -0.5)

    def gain(k, shape):
        return 1.0 + 0.05 * jax.random.normal(k, shape, f32)

    res_scale = (2 * DEPTH) ** -0.5
    gate_base = jnp.stack([jnp.zeros((M_HEADS,), f32), jnp.linspace(3.0, 6.0, M_HEADS, dtype=f32)])
    return {
        'x': jax.random.normal(ks[0], (BATCH, SEQ, D_MODEL), f32),
        'a_norm': gain(ks[1], (N_A, D_MODEL)),
        'a_w_in': w(ks[2], (N_A, D_MODEL, M_IN), D_MODEL),
        'a_b_gates': gate_base[None] + 0.1 * jax.random.normal(ks[3], (N_A, 2, M_HEADS), f32),
        'a_h_norm': gain(ks[4], (N_A, M_HEADS * M_DV)),
        'a_w_out': w(ks[5], (N_A, D_MODEL, D_MODEL), D_MODEL, res_scale),
        'kv_norm': gain(ks[6], (D_MODEL,)),
        'w_kv': w(ks[7], (D_MODEL, 2 * D_MODEL), D_MODEL),
        'k_norm': gain(ks[8], (A_DH,)),
        'b_norm': gain(ks[9], (N_B, D_MODEL)),
        'b_w_q': w(ks[10], (N_B, D_MODEL, D_MODEL), D_MODEL),
        'b_q_norm': gain(ks[11], (N_B, A_DH)),
        'b_w_o': w(ks[12], (N_B, D_MODEL, D_MODEL), D_MODEL, res_scale),
        'f_norm': gain(ks[13], (DEPTH, D_MODEL)),
        'f_w_up': w(ks[14], (DEPTH, D_MODEL, 2 * D_FF), D_MODEL),
        'f_conv_w': w(ks[15], (DEPTH, CONV_W, 2 * D_FF), CONV_W),
        'f_conv_b': 0.02 * jax.random.normal(ks[16], (DEPTH, 2 * D_FF), f32),
        'f_w_down': w(ks[17], (DEPTH, D_FF, D_MODEL), D_FF, res_scale),
    }


def reference(x, a_norm, a_w_in, a_b_gates, a_h_norm, a_w_out, kv_norm, w_kv, k_norm,
              b_norm, b_w_q, b_q_norm, b_w_o, f_norm, f_w_up, f_conv_w, f_conv_b, f_w_down):
    cos, sin = rope_tables(x.shape[1])
    h = x
    kb = vb = kmean = None
    for l in range(DEPTH):
        if l < N_A:
            h = h + mlstm_mixer(h, a_norm[l], a_w_in[l], a_b_gates[l], a_h_norm[l], a_w_out[l])
        else:
            j = l - N_A
            h = h + moba_mixer(h, b_norm[j], b_w_q[j], b_q_norm[j], b_w_o[j], kb, vb, kmean, cos, sin)
        h = h + conv_ffn(h, f_norm[l], f_w_up[l], f_conv_w[l], f_conv_b[l], f_w_down[l])
        if l == N_A - 1:
            kb, vb, kmean = shared_kv(h, kv_norm, w_kv, k_norm, cos, sin)
    return h
```

```python
import numpy as np
from contextlib import ExitStack
import ml_dtypes
import concourse.bass as bass
import concourse.mybir as mybir
from concourse.bass_utils import run_bass_kernel_spmd

F32 = mybir.dt.float32
BF16 = mybir.dt.bfloat16
AF = mybir.ActivationFunctionType
ALU = mybir.AluOpType
AX = mybir.AxisListType

NCORES = 8
S = 16384
D = 1024
EPS = 1e-6
DFF = 2816
NEG = -30000.0


class Prog:
    def __init__(self, nc, es, es_sem=None):
        self.nc = nc
        self.es = es
        es_sem = es if es_sem is None else es_sem
        self.es_sem = es_sem
        self.eng = {"pe": nc.tensor, "act": nc.scalar, "dve": nc.vector, "pool": nc.gpsimd, "sp": nc.sync}
        self.esem = {}
        self.ecnt = {}
        for e in ("pe", "act", "dve", "pool"):
            self.esem[e] = es_sem.enter_context(nc.semaphore("s_" + e))
            self.ecnt[e] = 0
        self.waited = {}
        self.lastw = {}
        self.readers = {}
        self.dsem = {}
        self.dcnt = {}
        self.n_ins = 0
        self.phase = 0

    def sb(self, name, shape, dt):
        return self.es.enter_context(self.nc.sbuf_tensor(f"{name}_p{self.phase}", list(shape), dt))

    def ps(self, name, shape, dt):
        return self.es.enter_context(self.nc.psum_tensor(f"{name}_p{self.phase}", list(shape), dt))

    def _waits(self, e, r, w, dma=False):
        deps = []
        for k in r:
            t = self.lastw.get(k)
            if t is not None:
                deps.append((t, True))
        for k in w:
            t = self.lastw.get(k)
            if t is not None:
                deps.append((t, False))
            for t in self.readers.get(k, ()):
                deps.append((t, False))
        eng = self.eng[e]
        for (t, raw) in deps:
            sem, val, owner = t
            if owner == e and e == "pe" and not raw and not dma:
                continue
            key = (e, sem.name)
            if self.waited.get(key, 0) >= val:
                continue
            eng.wait_ge(sem, val)
            self.waited[key] = val

    def _commit(self, tok, r, w):
        for k in w:
            self.lastw[k] = tok
            self.readers[k] = []
        for k in r:
            if k in w:
                continue
            self.readers.setdefault(k, []).append(tok)
            if len(self.readers[k]) > 24:
                best = {}
                for t in self.readers[k]:
                    if t[0].name not in best or best[t[0].name][1] < t[1]:
                        best[t[0].name] = t
                self.readers[k] = list(best.values())

    def op(self, e, fn, r=(), w=()):
        self._waits(e, r, w)
        ins = fn(self.eng[e])
        self.ecnt[e] += 1
        ins.then_inc(self.esem[e], 1)
        self._commit((self.esem[e], self.ecnt[e], e), r, w)
        self.n_ins += 1

    def dma(self, q, chan, pairs, r=(), w=()):
        if chan not in self.dsem:
            self.dsem[chan] = self.es_sem.enter_context(self.nc.semaphore("d_" + chan))
            self.dcnt[chan] = 0
        sem = self.dsem[chan]
        prev = self.dcnt[chan]
        self._waits(q, r, w, dma=True)
        if prev and self.waited.get((q, sem.name), 0) < prev:
            self.eng[q].wait_ge(sem, prev)
            self.waited[(q, sem.name)] = prev
        for (o, i) in pairs:
            self.eng[q].dma_start(out=o, in_=i).then_inc(sem, 16)
            self.dcnt[chan] += 16
            self.n_ins += 1
        self._commit((sem, self.dcnt[chan], "dma"), r, w)

    def coll(self, kind, src_ap, dst_ap, r=(), w=()):
        self.dma_like_wait("pool", r, w)
        if "cc" not in self.dsem:
            self.dsem["cc"] = self.es_sem.enter_context(self.nc.semaphore("d_cc"))
            self.dcnt["cc"] = 0
        sem = self.dsem["cc"]
        self.nc.gpsimd.collective_compute(kind, ALU.bypass, replica_groups=[list(range(NCORES))],
                                          ins=[src_ap], outs=[dst_ap]).then_inc(sem, 1)
        self.dcnt["cc"] += 1
        self.n_ins += 1
        self._commit((sem, self.dcnt["cc"], "dma"), r, w)

    def dma_like_wait(self, q, r, w):
        self._waits(q, r, w, dma=True)

    def begin_phase(self, es):
        self.es = es
        self.phase += 1
        self.lastw = {}
        self.readers = {}
        return self

    def barrier(self):
        for e in ("pe", "act", "dve", "pool", "sp"):
            for o in ("pe", "act", "dve", "pool"):
                if o != e and self.ecnt[o] and self.waited.get((e, self.esem[o].name), 0) < self.ecnt[o]:
                    self.eng[e].wait_ge(self.esem[o], self.ecnt[o])
                    self.waited[(e, self.esem[o].name)] = self.ecnt[o]
            for ch, sem in self.dsem.items():
                if self.dcnt[ch] and self.waited.get((e, sem.name), 0) < self.dcnt[ch]:
                    self.eng[e].wait_ge(sem, self.dcnt[ch])
                    self.waited[(e, sem.name)] = self.dcnt[ch]

    def end_phase(self):
        self.barrier()
        self.lastw = {}
        self.readers = {}

    def finish(self, keys):
        self._waits("sp", keys, ())
        for e in ("pe", "act", "dve", "pool"):
            if self.ecnt[e]:
                self.eng["sp"].wait_ge(self.esem[e], self.ecnt[e])


def new_nc():
    return bass.Bass("TRN2", target_bir_lowering=False)


def din(nc, name, shape, dt=F32):
    return nc.dram_tensor(name, list(shape), dt, kind="ExternalInput").ap()


def dout(nc, name, shape, dt=F32):
    return nc.dram_tensor(name, list(shape), dt, kind="ExternalOutput").ap()


class Ctx:
    def __init__(self, nc=None, P=None, aps=None):
        self.fused = nc is not None
        self.nc = nc if self.fused else new_nc()
        self.P = P
        self.aps = aps

    def I(self, name, shape, dt=F32):
        if self.fused:
            ap = self.aps[name]
            assert list(ap.shape) == list(shape), (name, ap.shape, shape)
            return ap
        return din(self.nc, name, shape, dt)

    def O(self, name, shape, dt=F32):
        if self.fused:
            ap = self.aps[name]
            assert list(ap.shape) == list(shape), (name, ap.shape, shape)
            return ap
        return dout(self.nc, name, shape, dt)

    def prog(self, es):
        if self.fused:
            return self.P.begin_phase(es)
        return Prog(self.nc, es)

    def done(self, P, keys):
        if self.fused:
            P.end_phase()
        else:
            P.finish(keys)


def bf16_np(a):
    return a.astype(ml_dtypes.bfloat16)


IDENT_BF = np.eye(128, dtype=np.float32).astype(ml_dtypes.bfloat16)
IDENT_F = np.eye(128, dtype=np.float32)


def emit_rstd(P, ssq, rstd, n, key_ssq, key_rstd, tmp, dim=D):
    P.op("act", lambda e: e.activation(out=tmp[:, 0:n], in_=ssq[:, 0:n], func=AF.Ln, scale=1.0 / dim, bias=EPS_AP[0][:, 0:1]),
         r=[key_ssq, "epsc"], w=[key_rstd + "_t"])
    P.op("act", lambda e: e.activation(out=rstd[:, 0:n], in_=tmp[:, 0:n], func=AF.Exp, scale=-0.5),
         r=[key_rstd + "_t"], w=[key_rstd])


EPS_AP = [None]


def make_eps(P):
    t = P.sb("epsc", [128, 1], F32)
    P.op("pool", lambda e: e.memset(t[:], EPS), w=["epsc"])
    EPS_AP[0] = t


def build_ffn(ctx=None):
    CX = ctx if ctx is not None else Ctx()
    nc = CX.nc
    x_d = CX.I("x", [16, 128, D])
    xh_d = CX.I("xh", [32, D])
    g_d = CX.I("g", [128, D])
    wup_d = CX.I("wup", [D, 2 * DFF])
    cw_d = CX.I("cw", [128, 44, 3])
    cb_d = CX.I("cb", [128, 44])
    wdn_d = CX.I("wdn", [DFF, D])
    id_d = CX.I("ident", [128, 128], BF16)
    y_d = CX.O("y", [16, 128, D])
    with ExitStack() as es:
        P = CX.prog(es)
        xres = P.sb("xres", [128, 16, D], F32)
        g_b = P.sb("g_b", [128, D], F32)
        ident = P.sb("ident_s", [128, 128], BF16)
        cw = P.sb("cw_s", [128, 44, 3], F32)
        cb = P.sb("cb_s", [128, 44], F32)
        wdn = P.sb("wdn_s", [128, 22, D], BF16)
        ssq = P.sb("ssq", [128, 20], F32)
        rstd = P.sb("rstd", [128, 20], F32)
        rtmp = P.sb("rtmp", [128, 20], F32)
        xh = P.sb("xh_s", [32, D], F32)
        junk = P.sb("junk", [128, D], BF16)
        xn = [P.sb(f"xn{i}", [128, D], BF16) for i in range(2)]
        xnT = P.sb("xnT", [128, 8, 2, 260], BF16)
        wv = [P.sb(f"wv{i}", [128, 8, 256], BF16) for i in range(2)]
        wg = [P.sb(f"wg{i}", [128, 8, 256], BF16) for i in range(2)]
        Uv = P.sb("Uv", [128, 4, 130], F32)
        Ug = P.sb("Ug", [128, 4, 130], F32)
        tv = [P.sb(f"tv{i}", [128, 4, 128], F32) for i in range(2)]
        tg = [P.sb(f"tg{i}", [128, 4, 128], F32) for i in range(2)]
        sg = P.sb("sg", [128, 4, 128], F32)
        aT = P.sb("aT", [128, 22, 512], BF16)
        pT = [P.ps(f"pT{i}", [128, 8, 128], BF16) for i in range(2)]
        pU = [P.ps(f"pU{i}", [128, 512], F32) for i in range(4)]
        pD = [P.ps(f"pD{i}", [128, 512], F32) for i in range(2)]

        P.dma("sp", "c0", [(g_b[:], g_d[:, :]), (ident[:], id_d[:, :]), (cw[:], cw_d[:, :, :]),
                           (cb[:], cb_d[:, :]), (xh[:], xh_d[:, :])], w=["g_b", "ident", "cw", "cb", "xh"])
        for t in range(16):
            P.dma("sp", f"x{t % 4}", [(xres[:, t, :], x_d[t, :, :])], w=[f"x{t}"])
        P.dma("pool", "wdn", [(wdn[:, c * 11:(c + 1) * 11, :],
                               wdn_d[c * 11 * 128:(c + 1) * 11 * 128, :].rearrange("(c p) n -> p c n", p=128))
                              for c in range(2)], w=["wdn"])

        make_eps(P)
        P.op("pool", lambda e: e.memset(ssq[:], 1.0), w=["ssq"])
        for t in range(16):
            P.op("act", lambda e, t=t: e.activation(out=junk[:], in_=xres[:, t, :], func=AF.Square,
                                                    accum_out=ssq[:, t:t + 1]), r=[f"x{t}"], w=["junk", "ssq"])
        P.op("act", lambda e: e.activation(out=junk[0:32, :], in_=xh[:], func=AF.Square,
                                           accum_out=ssq[0:32, 16:17]), r=["xh"], w=["junk", "ssq"])
        emit_rstd(P, ssq, rstd, 17, "ssq", "rstd", rtmp)

        wcount = [0]

        def load_w(b):
            i = wcount[0] % 2
            wcount[0] += 1
            P.dma("pool", f"wup{i}", [
                (wv[i][:], wup_d[:, b * 256:(b + 1) * 256].rearrange("(k p) n -> p k n", p=128)),
                (wg[i][:], wup_d[:, DFF + b * 256:DFF + (b + 1) * 256].rearrange("(k p) n -> p k n", p=128)),
            ], w=[f"wup{i}"])
            return i

        nT = [0]
        for q in range(4):
            for tt in range(4):
                t = q * 4 + tt
                xb = xn[t % 2]
                P.op("dve", lambda e, t=t, xb=xb: e.scalar_tensor_tensor(
                    out=xb[:], in0=xres[:, t, :], scalar=rstd[:, t:t + 1], in1=g_b[:],
                    op0=ALU.mult, op1=ALU.mult), r=[f"x{t}", "rstd", "g_b"], w=[f"xn{t % 2}"])
                pt = pT[nT[0] % 2]
                pk = f"pT{nT[0] % 2}"
                nT[0] += 1
                for k in range(8):
                    P.op("pe", lambda e, k=k, pt=pt, xb=xb: e.transpose(pt[:, k, :], xb[:, k * 128:(k + 1) * 128],
                                                                       ident[:]),
                         r=[f"xn{t % 2}", "ident"], w=[pk])
                hh, cc = tt // 2, (tt % 2) * 128
                P.op("act", lambda e, pt=pt, hh=hh, cc=cc: e.copy(out=xnT[:, :, hh, cc:cc + 128], in_=pt[:, :, :]),
                     r=[pk], w=["xnT"])
            xb = xn[0]
            P.op("dve", lambda e, xb=xb: e.scalar_tensor_tensor(
                out=xb[0:32, :], in0=xh[:], scalar=rstd[0:32, 16:17], in1=g_b[0:32, :],
                op0=ALU.mult, op1=ALU.mult), r=["xh", "rstd", "g_b"], w=["xn0"])
            pt = pT[nT[0] % 2]
            pk = f"pT{nT[0] % 2}"
            nT[0] += 1
            for k in range(8):
                P.op("pe", lambda e, k=k, pt=pt, xb=xb: e.transpose(pt[:, k, 0:32], xb[0:32, k * 128:(k + 1) * 128],
                                                                   ident[0:32, 0:32]),
                     r=["xn0", "ident"], w=[pk])
            for hh in range(2):
                P.op("act", lambda e, pt=pt, hh=hh: e.copy(out=xnT[:, :, hh, 256:260],
                                                           in_=pt[:, :, 8 * q + 4 * hh:8 * q + 4 * hh + 4]),
                     r=[pk], w=["xnT"])

            for b in range(11):
                wi = load_w(b)
                for sub in range(2):
                    c = 2 * b + sub
                    for (wt, U, nm, pbase) in ((wv[wi], Uv, "Uv", 0), (wg[wi], Ug, "Ug", 2)):
                        for hh in range(2):
                            pu = pU[pbase + hh]
                            puk = f"pU{pbase + hh}"
                            for k in range(8):
                                P.op("pe", lambda e, pu=pu, wt=wt, k=k, hh=hh, sub=sub: e.matmul(
                                    pu[:, 0:260], wt[:, k, sub * 128:(sub + 1) * 128], xnT[:, k, hh, :],
                                    start=(k == 0), stop=(k == 7)),
                                     r=[f"wup{wi}", "xnT"], w=[puk])
                            P.op("act", lambda e, pu=pu, U=U, hh=hh: e.copy(
                                out=U[:, 2 * hh:2 * hh + 2, 2:130],
                                in_=pu[:, 0:256].rearrange("p (a b) -> p a b", a=2)), r=[puk], w=[nm])
                            P.op("dve", lambda e, pu=pu, U=U, hh=hh: e.tensor_copy(
                                out=U[:, 2 * hh:2 * hh + 2, 0:2],
                                in_=pu[:, 256:260].rearrange("p (a b) -> p a b", a=2)), r=[puk], w=[nm])
                    for (eng, U, nm, tb, tn, cidx) in (("dve", Uv, "Uv", tv, "tv", c), ("pool", Ug, "Ug", tg, "tg", 22 + c)):
                        P.op(eng, lambda e, U=U, tb=tb, cidx=cidx: e.tensor_scalar(
                            out=tb[0][:], in0=U[:, :, 0:128], scalar1=cw[:, cidx, 0:1], scalar2=cb[:, cidx:cidx + 1],
                            op0=ALU.mult, op1=ALU.add), r=[nm, "cw", "cb"], w=[tn + "0"])
                        P.op(eng, lambda e, U=U, tb=tb, cidx=cidx: e.tensor_scalar(
                            out=tb[1][:], in0=U[:, :, 1:129], scalar1=cw[:, cidx, 1:2], scalar2=None,
                            op0=ALU.mult), r=[nm, "cw"], w=[tn + "1"])
                        P.op(eng, lambda e, tb=tb: e.tensor_tensor(out=tb[0][:], in0=tb[0][:], in1=tb[1][:], op=ALU.add),
                             r=[tn + "0", tn + "1"], w=[tn + "0"])
                        P.op(eng, lambda e, U=U, tb=tb, cidx=cidx: e.tensor_scalar(
                            out=tb[1][:], in0=U[:, :, 2:130], scalar1=cw[:, cidx, 2:3], scalar2=None,
                            op0=ALU.mult), r=[nm, "cw"], w=[tn + "1"])
                        P.op(eng, lambda e, tb=tb: e.tensor_tensor(out=tb[0][:], in0=tb[0][:], in1=tb[1][:], op=ALU.add),
                             r=[tn + "0", tn + "1"], w=[tn + "0"])
                    P.op("act", lambda e: e.activation(out=sg[:], in_=tg[0][:], func=AF.Silu), r=["tg0"], w=["sg"])
                    P.op("dve", lambda e, c=c: e.tensor_tensor(
                        out=aT[:, c, :].rearrange("p (a b) -> p a b", a=4), in0=sg[:], in1=tv[0][:], op=ALU.mult),
                        r=["sg", "tv0"], w=["aT"])
            for tt in range(4):
                t = q * 4 + tt
                for cg in range(2):
                    pd = pD[cg]
                    for c in range(22):
                        P.op("pe", lambda e, pd=pd, c=c, tt=tt, cg=cg: e.matmul(
                            pd[:], aT[:, c, tt * 128:(tt + 1) * 128], wdn[:, c, cg * 512:(cg + 1) * 512],
                            start=(c == 0), stop=(c == 21)), r=["aT", "wdn"], w=[f"pD{cg}"])
                    P.op("dve", lambda e, pd=pd, t=t, cg=cg: e.tensor_tensor(
                        out=xres[:, t, cg * 512:(cg + 1) * 512], in0=xres[:, t, cg * 512:(cg + 1) * 512], in1=pd[:],
                        op=ALU.add), r=[f"pD{cg}", f"x{t}"], w=[f"x{t}"])
                P.dma("sp", f"y{t % 4}", [(y_d[t, :, :], xres[:, t, :])], r=[f"x{t}"], w=[f"yo{t}"])
        CX.done(P, [f"yo{t}" for t in range(16)])
        print("ffn instructions:", P.n_ins)
    return nc


def run_ffn(nc, xt, xh, g, wup, cw, cb, wdn):
    cwl = np.ascontiguousarray(cw.reshape(3, 44, 128).transpose(2, 1, 0))
    cbl = np.ascontiguousarray(cb.reshape(44, 128).T)
    gb = np.ascontiguousarray(np.broadcast_to(g[None, :], (128, D)))
    maps = [{"x": xt[c], "xh": xh[c], "g": gb, "wup": wup, "cw": cwl, "cb": cbl, "wdn": wdn, "ident": IDENT_BF}
            for c in range(NCORES)]
    res = run_bass_kernel_spmd(nc, maps, core_ids=list(range(NCORES)))
    return [r["y"] for r in res.results]


def halos_from(hfull, tile_ids):
    out = np.zeros((len(tile_ids) * 2, D), np.float32)
    for i, j in enumerate(tile_ids):
        if j > 0:
            out[2 * i:2 * i + 2] = hfull[j * 128 - 2:j * 128]
    return out


MIN_ = 3080


def build_mlstm(state_only, ctx=None):
    CX = ctx if ctx is not None else Ctx()
    nc = CX.nc
    x_d = CX.I("x", [16, 128, D])
    g_d = CX.I("g", [128, D])
    win_d = CX.I("win", [D, MIN_])
    bg_d = CX.I("bg", [128, 8])
    id_d = CX.I("ident", [128, 128], BF16)
    tri_d = CX.I("tri", [128, 128])
    if state_only:
        cst_o = CX.O("cst", [128, 4 * 257])
        fst_o = CX.O("fst", [128, 4])
    else:
        gh_d = CX.I("gh", [128, D])
        wout_d = CX.I("wout", [D, D])
        cst_d = CX.I("cst_in", [8, 128, 4 * 257])
        fst_d = CX.I("fst_in", [128, 8, 4])
        cm_d = CX.I("cmask", [128, 8])
        y_d = CX.O("y", [16, 128, D])
    with ExitStack() as es:
        P = CX.prog(es)
        make_eps(P)
        g_b = P.sb("g_b", [128, D], F32)
        ident = P.sb("ident_s", [128, 128], BF16)
        tri = P.sb("tri_s", [128, 128], F32)
        ones = P.sb("ones_s", [128, 128], F32)
        bg = P.sb("bg_s", [128, 8], F32)
        win = P.sb("win_s", [128, 8, MIN_], BF16)
        xin = [P.sb(f"xin{i}", [128, D], F32) for i in range(2)]
        junk = P.sb("junk", [128, D], BF16)
        ssq = P.sb("ssq", [128, 4], F32)
        rstd = P.sb("rstd", [128, 4], F32)
        rtmp = P.sb("rtmp", [128, 4], F32)
        xn = [P.sb(f"xn{i}", [128, D], BF16) for i in range(2)]
        xnT = P.sb("xnT", [128, 8, 512], BF16)
        qkT = P.sb("qkT", [128, 8, 512], BF16)
        ktok = P.sb("ktok", [128, 4, 512], BF16)
        vaug = P.sb("vaug", [128, 4, 4, 258], BF16)
        gts = P.sb("gts", [128, 4, 8], F32)
        C = P.sb("C", [128, 4, 257], F32)
        Cb = P.sb("Cb", [128, 4, 258], BF16)
        Facc = P.sb("Facc", [128, 4], F32)
        ig = P.sb("ig", [128, 4], F32)
        fp = P.sb("fp", [128, 4], F32)
        lf = P.sb("lf", [128, 4], F32)
        lfb = P.sb("lfb", [128, 4, 128], F32)
        bcs = P.sb("bcs", [128, 4], F32)
        av = P.sb("av", [128, 4], F32)
        eW = P.sb("eW", [128, 4], F32)
        eB = P.sb("eB", [128, 4], F32)
        E1 = P.sb("E1", [128, 128], F32)
        DT = P.sb("DT", [128, 128], F32)
        DTm = P.sb("DTm", [128, 128], F32)
        Sp = P.sb("Sp", [128, 128], BF16)
        qs = P.sb("qs", [128, 128], BF16)
        wk = P.sb("wk", [128, 128], BF16)
        pT = P.ps("pT", [128, 8, 128], BF16)
        pA = P.ps("pA", [128, 1024], F32)
        pB = P.ps("pB", [128, 512], F32)
        pM = P.ps("pM", [128, 512], F32)
        pPB = pM[:, 0:128]
        pG = pM[:, 128:144]
        pSd = P.ps("pSd", [128, 512], F32)
        pS = pSd[:, 0:128]
        pdC = pSd[:, 128:385]
        pOut = [P.ps(f"pOut{i}", [128, 512], F32) for i in range(2)]
        if not state_only:
            gh_b = P.sb("gh_b", [128, D], F32)
            wout = P.sb("wout_s", [128, 8, D], BF16)
            so = P.sb("so", [128, 4, D], BF16)
            gso = P.sb("gso", [128, D], F32)
            hg = P.sb("hg", [128, D], BF16)
            hgT = P.sb("hgT", [128, 8, 128], BF16)
            cstb = [P.sb(f"cstb{i}", [128, 4, 257], F32) for i in range(2)]
            fstb = P.sb("fstb", [128, 8, 4], F32)
            cm = P.sb("cm_s", [128, 8], F32)
            dec = P.sb("dec", [128, 4], F32)
            den = P.sb("den", [128, 4], F32)
            ssqh = P.sb("ssqh", [128, 4], F32)
            rsh = P.sb("rsh", [128, 4], F32)
            rth = P.sb("rth", [128, 4], F32)
            sc = P.sb("sc", [128, 4], F32)
            xres = [P.sb(f"xr{i}", [128, D], F32) for i in range(2)]

        P.dma("sp", "c0", [(g_b[:], g_d[:, :]), (ident[:], id_d[:, :]), (tri[:], tri_d[:, :]), (bg[:], bg_d[:, :])],
              w=["g_b", "ident", "tri", "bg"])
        P.op("pool", lambda e: e.memset(ones[:], 1.0), w=["ones"])
        P.op("pool", lambda e: e.memset(vaug[:], 1.0), w=["vaug"])
        P.op("pool", lambda e: e.memset(C[:], 0.0), w=["C"])
        P.op("pool", lambda e: e.memset(Cb[:], 0.0), w=["Cb"])
        P.op("pool", lambda e: e.memset(Facc[:], 0.0), w=["Facc"])
        for k in range(8):
            P.dma("pool", "win", [(win[:, k, :], win_d[k * 128:(k + 1) * 128, :])], w=[f"win{k}"])
        WIN = [f"win{k}" for k in range(8)]
        if not state_only:
            P.dma("sp", "c1", [(gh_b[:], gh_d[:, :]), (fstb[:], fst_d[:, :, :]), (cm[:], cm_d[:, :])],
                  w=["gh_b", "fstb", "cm"])
            P.dma("pool", "wout", [(wout[:], wout_d.rearrange("(k p) n -> p k n", p=128))], w=["wout"])
            for cp in range(8):
                cb_ = cstb[cp % 2]
                ck = f"cstb{cp % 2}"
                P.dma("sp", ck, [(cb_[:], cst_d[cp, :, :].rearrange("p (h e) -> p h e", h=4))], w=[ck])
                P.op("act", lambda e, cp=cp: e.activation(out=dec[:], in_=fstb[:, cp, :], func=AF.Exp), r=["fstb"], w=["dec"])
                P.op("dve", lambda e, cp=cp: e.tensor_scalar(out=dec[:], in0=dec[:], scalar1=-1.0, scalar2=cm[:, cp:cp + 1],
                                                             op0=ALU.add, op1=ALU.mult), r=["dec", "cm"], w=["dec"])
                P.op("dve", lambda e: e.tensor_scalar(out=dec[:], in0=dec[:], scalar1=1.0, scalar2=None, op0=ALU.add),
                     r=["dec"], w=["dec"])
                P.op("dve", lambda e, cp=cp, cb_=cb_: e.tensor_scalar(out=cb_[:], in0=cb_[:], scalar1=cm[:, cp:cp + 1], scalar2=None,
                                                                     op0=ALU.mult), r=[ck, "cm"], w=[ck])
                for h in range(4):
                    P.op("dve", lambda e, h=h, cb_=cb_: e.scalar_tensor_tensor(
                        out=C[:, h, :], in0=C[:, h, :], scalar=dec[:, h:h + 1], in1=cb_[:, h, :],
                        op0=ALU.mult, op1=ALU.add), r=["C", "dec", ck], w=["C"])
            P.op("act", lambda e: e.copy(out=Cb[:, :, 0:257], in_=C[:, :, :]), r=["C"], w=["Cb"])

        def proj_fm(gi_, col0, scale):
            for k in range(8):
                P.op("pe", lambda e, k=k: e.matmul(pB[:], win[:, k, col0:col0 + 128], xnT[:, k, :],
                                                  start=(k == 0), stop=(k == 7)), r=[f"win{k}", "xnT"], w=["pB"])
            if scale == 1.0:
                P.op("act", lambda e: e.copy(out=qkT[:, gi_, :], in_=pB[:]), r=["pB"], w=["qkT"])
            else:
                P.op("act", lambda e: e.mul(out=qkT[:, gi_, :], in_=pB[:], mul=scale), r=["pB"], w=["qkT"])

        for grp in range(4):
            for tt in range(4):
                t = grp * 4 + tt
                xi = xin[t % 2]
                xk = f"xin{t % 2}"
                P.dma("sp", xk, [(xi[:], x_d[t, :, :])], w=[xk])
                P.op("act", lambda e, xi=xi, tt=tt: e.activation(out=junk[:], in_=xi[:], func=AF.Square,
                                                                 accum_out=ssq[:, tt:tt + 1]), r=[xk], w=["junk", "ssq"])
                emit_rstd(P, ssq[:, tt:tt + 1], rstd[:, tt:tt + 1], 1, "ssq", "rstd", rtmp[:, tt:tt + 1])
                xb = xn[t % 2]
                P.op("dve", lambda e, xi=xi, xb=xb, tt=tt: e.scalar_tensor_tensor(
                    out=xb[:], in0=xi[:], scalar=rstd[:, tt:tt + 1], in1=g_b[:], op0=ALU.mult, op1=ALU.mult),
                    r=[xk, "rstd", "g_b"], w=[f"xn{t % 2}"])
                for k in range(8):
                    P.op("pe", lambda e, k=k, xb=xb: e.transpose(pT[:, k, :], xb[:, k * 128:(k + 1) * 128], ident[:]),
                         r=[f"xn{t % 2}", "ident"], w=["pT"])
                P.op("act", lambda e, tt=tt: e.copy(out=xnT[:, :, tt * 128:(tt + 1) * 128], in_=pT[:, :, :]),
                     r=["pT"], w=["xnT"])
            if not state_only:
                for h in range(4):
                    proj_fm(h, h * 128, 128.0 ** -0.5)
                for h in range(4):
                    proj_fm(4 + h, 512 + h * 128, 1.0)
            for tt in range(4):
                lhs = lambda k, tt=tt: xnT[:, k, tt * 128:(tt + 1) * 128]
                for k in range(8):
                    P.op("pe", lambda e, k=k: e.matmul(pB[:], lhs(k), win[:, k, 512:1024], start=(k == 0), stop=(k == 7)),
                         r=[f"win{k}", "xnT"], w=["pB"])
                P.op("act", lambda e, tt=tt: e.copy(out=ktok[:, tt, :], in_=pB[:]), r=["pB"], w=["ktok"])
                for cg in range(2):
                    for k in range(8):
                        P.op("pe", lambda e, k=k, cg=cg: e.matmul(pA[:, cg * 512:(cg + 1) * 512], lhs(k),
                                                                 win[:, k, 1024 + cg * 512:1536 + cg * 512],
                                                                 start=(k == 0), stop=(k == 7)), r=[f"win{k}", "xnT"], w=["pA"])
                P.op("act", lambda e, tt=tt: e.copy(out=vaug[:, tt, :, 0:256], in_=pA[:].rearrange("p (h e) -> p h e", h=4)),
                     r=["pA"], w=["vaug"])
                if not state_only:
                    for cg in range(2):
                        for k in range(8):
                            P.op("pe", lambda e, k=k, cg=cg: e.matmul(pA[:, cg * 512:(cg + 1) * 512], lhs(k),
                                                                     win[:, k, 2048 + cg * 512:2560 + cg * 512],
                                                                     start=(k == 0), stop=(k == 7)), r=[f"win{k}", "xnT"], w=["pA"])
                    P.op("act", lambda e, tt=tt: e.activation(out=so[:, tt, :], in_=pA[:], func=AF.Sigmoid), r=["pA"], w=["so"])
                for k in range(8):
                    P.op("pe", lambda e, k=k: e.matmul(pG[:, 0:8], lhs(k), win[:, k, 3072:3080], start=(k == 0), stop=(k == 7)),
                         r=[f"win{k}", "xnT"], w=["pM"])
                P.op("dve", lambda e, tt=tt: e.tensor_tensor(out=gts[:, tt, :], in0=pG[:, 0:8], in1=bg[:], op=ALU.add),
                     r=["pM", "bg"], w=["gts"])
            for tt in range(4):
                t = grp * 4 + tt
                P.op("act", lambda e, tt=tt: e.activation(out=fp[:], in_=gts[:, tt, 4:8], func=AF.Exp, scale=-1.0), r=["gts"], w=["fp"])
                P.op("act", lambda e: e.activation(out=fp[:], in_=fp[:], func=AF.Ln, scale=1.0, bias=ones[:, 0:1]), r=["fp", "ones"], w=["fp"])
                P.op("dve", lambda e: e.tensor_scalar(out=lf[:], in0=fp[:], scalar1=-1.0, scalar2=None, op0=ALU.mult), r=["fp"], w=["lf"])
                P.op("pe", lambda e: e.matmul(pG[:, 8:12], tri[:], lf[:], start=True, stop=True), r=["tri", "lf"], w=["pM"])
                P.op("pe", lambda e: e.matmul(pG[:, 12:16], ones[:], lf[:], start=True, stop=True), r=["ones", "lf"], w=["pM"])
                P.op("dve", lambda e, tt=tt: e.tensor_tensor(out=av[:], in0=gts[:, tt, 0:4], in1=pG[:, 8:12], op=ALU.subtract),
                     r=["gts", "pM"], w=["av"])
                P.op("dve", lambda e: e.tensor_tensor(out=eW[:], in0=av[:], in1=pG[:, 12:16], op=ALU.add), r=["av", "pM"], w=["eW"])
                P.op("act", lambda e: e.activation(out=eW[:], in_=eW[:], func=AF.Exp), r=["eW"], w=["eW"])
                P.op("act", lambda e: e.activation(out=eB[:], in_=pG[:, 12:16], func=AF.Exp), r=["pM"], w=["eB"])
                P.op("dve", lambda e: e.tensor_tensor(out=Facc[:], in0=Facc[:], in1=pG[:, 12:16], op=ALU.add), r=["Facc", "pM"], w=["Facc"])
                if not state_only:
                    P.op("dve", lambda e: e.tensor_copy(out=lfb[:], in_=lf[:, :].unsqueeze(2).to_broadcast([128, 4, 128])),
                         r=["lf"], w=["lfb"])
                    P.op("pool", lambda e, tt=tt: e.tensor_tensor(out=gso[:], in0=so[:, tt, :], in1=gh_b[:], op=ALU.mult),
                         r=["so", "gh_b"], w=["gso"])
                for h in range(4):
                    cs = slice(tt * 128, (tt + 1) * 128)
                    if not state_only:
                        P.op("pe", lambda e, h=h: e.matmul(pPB[:], lfb[:, h, :], tri[:], start=True, stop=True),
                             r=["lfb", "tri"], w=["pM"])
                        P.op("act", lambda e: e.activation(out=E1[:], in_=pPB[:], func=AF.Exp), r=["pM"], w=["E1"])
                        P.op("act", lambda e, h=h: e.activation(out=DT[:], in_=pPB[:], func=AF.Exp, bias=av[:, h:h + 1]),
                             r=["pM", "av"], w=["DT"])
                        P.op("pool", lambda e: e.tensor_tensor(out=DTm[:], in0=DT[:], in1=tri[:], op=ALU.mult), r=["DT", "tri"], w=["DTm"])
                        P.op("pool", lambda e, h=h, cs=cs: e.tensor_tensor(out=qs[:], in0=qkT[:, h, cs], in1=E1[:], op=ALU.mult),
                             r=["qkT", "E1"], w=["qs"])
                        P.op("pe", lambda e, h=h, cs=cs: e.matmul(pS[:], qkT[:, 4 + h, cs], qkT[:, h, cs], start=True, stop=True),
                             r=["qkT"], w=["pSd"])
                        P.op("dve", lambda e: e.tensor_tensor(out=Sp[:], in0=pS[:], in1=DTm[:], op=ALU.mult), r=["pSd", "DTm"], w=["Sp"])
                        po = pOut[h % 2]
                        pk = f"pOut{h % 2}"
                        P.op("pe", lambda e, h=h, po=po: e.matmul(po[:, 0:257], qs[:], Cb[:, h, 0:257], start=True, stop=False),
                             r=["qs", "Cb"], w=[pk])
                        P.op("pe", lambda e, h=h, po=po, tt=tt: e.matmul(po[:, 0:257], Sp[:], vaug[:, tt, h, 0:257], start=False, stop=True),
                             r=["Sp", "vaug"], w=[pk])
                    P.op("pool", lambda e, h=h, tt=tt: e.tensor_scalar(out=wk[:], in0=ktok[:, tt, h * 128:(h + 1) * 128],
                                                                       scalar1=eW[:, h:h + 1], scalar2=None, op0=ALU.mult),
                         r=["ktok", "eW"], w=["wk"])
                    P.op("pe", lambda e, h=h, tt=tt: e.matmul(pdC, wk[:], vaug[:, tt, h, 0:257], start=True, stop=True),
                         r=["wk", "vaug"], w=["pSd"])
                    P.op("dve", lambda e, h=h: e.scalar_tensor_tensor(out=C[:, h, :], in0=C[:, h, :], scalar=eB[:, h:h + 1],
                                                                      in1=pdC, op0=ALU.mult, op1=ALU.add),
                         r=["C", "eB", "pSd"], w=["C"])
                    if not state_only:
                        P.op("act", lambda e, h=h: e.copy(out=Cb[:, h, 0:257], in_=C[:, h, :]), r=["C"], w=["Cb"])
                        P.op("act", lambda e, h=h, po=po: e.activation(out=den[:, h:h + 1], in_=po[:, 256:257], func=AF.Abs), r=[pk], w=["den"])
                        P.op("dve", lambda e, h=h: e.tensor_scalar(out=den[:, h:h + 1], in0=den[:, h:h + 1], scalar1=1.0, scalar2=None,
                                                                   op0=ALU.max), r=["den"], w=["den"])
                        P.op("dve", lambda e, h=h: e.reciprocal(out=den[:, h:h + 1], in_=den[:, h:h + 1]), r=["den"], w=["den"])
                        P.op("act", lambda e, h=h, po=po: e.activation(out=junk[:, 0:256], in_=po[:, 0:256], func=AF.Square,
                                                                       scale=den[:, h:h + 1], accum_out=ssqh[:, h:h + 1]),
                             r=[pk, "den"], w=["junk", "ssqh"])
                        emit_rstd(P, ssqh[:, h:h + 1], rsh[:, h:h + 1], 1, "ssqh", "rsh", rth[:, h:h + 1], dim=256)
                        P.op("dve", lambda e, h=h: e.tensor_tensor(out=sc[:, h:h + 1], in0=den[:, h:h + 1], in1=rsh[:, h:h + 1], op=ALU.mult),
                             r=["den", "rsh"], w=["sc"])
                        P.op("dve", lambda e, h=h, po=po: e.scalar_tensor_tensor(
                            out=hg[:, h * 256:(h + 1) * 256], in0=po[:, 0:256], scalar=sc[:, h:h + 1],
                            in1=gso[:, h * 256:(h + 1) * 256], op0=ALU.mult, op1=ALU.mult), r=[pk, "sc", "gso"], w=["hg"])
                if not state_only:
                    for k in range(8):
                        P.op("pe", lambda e, k=k: e.transpose(pT[:, k, :], hg[:, k * 128:(k + 1) * 128], ident[:]),
                             r=["hg", "ident"], w=["pT"])
                    P.op("act", lambda e: e.copy(out=hgT[:], in_=pT[:, :, :]), r=["pT"], w=["hgT"])
                    xr = xres[t % 2]
                    xrk = f"xr{t % 2}"
                    P.dma("sp", xrk, [(xr[:], x_d[t, :, :])], w=[xrk])
                    for cg in range(2):
                        for k in range(8):
                            P.op("pe", lambda e, k=k, cg=cg: e.matmul(pA[:, cg * 512:(cg + 1) * 512], hgT[:, k, :],
                                                                     wout[:, k, cg * 512:(cg + 1) * 512],
                                                                     start=(k == 0), stop=(k == 7)), r=["hgT", "wout"], w=["pA"])
                    P.op("dve", lambda e, xr=xr: e.tensor_tensor(out=xr[:], in0=xr[:], in1=pA[:], op=ALU.add), r=[xrk, "pA"], w=[xrk])
                    P.dma("sp", f"y{t % 2}", [(y_d[t, :, :], xr[:])], r=[xrk], w=[f"yo{t}"])
        if state_only:
            P.dma("sp", "so", [(cst_o[:, :].rearrange("p (h e) -> p h e", h=4), C[:]), (fst_o[:, :], Facc[:])],
                  r=["C", "Facc"], w=["outs"])
            CX.done(P, ["outs"])
        else:
            CX.done(P, [f"yo{t}" for t in range(16)])
        print("mlstm instructions:", P.n_ins, "state_only", state_only)
    return nc


TRI = np.triu(np.ones((128, 128), np.float32))


def mlstm_common_inputs(xt, g, win, bgates):
    gb = np.ascontiguousarray(np.broadcast_to(g[None, :], (128, D)))
    bgb = np.ascontiguousarray(np.broadcast_to(bgates.reshape(1, 8), (128, 8)))
    return {"x": xt, "g": gb, "win": win, "bg": bgb, "ident": IDENT_BF, "tri": TRI}


def run_mlstm_state(nc, xts, g, win, bgates):
    maps = [mlstm_common_inputs(xts[c], g, win, bgates) for c in range(NCORES)]
    res = run_bass_kernel_spmd(nc, maps, core_ids=list(range(NCORES)))
    return [r["cst"] for r in res.results], [r["fst"] for r in res.results]


def run_mlstm_full(nc, xts, g, win, bgates, gh, wout, csts, fsts):
    cst_all = np.ascontiguousarray(np.stack(csts, 0))
    fst_all = np.ascontiguousarray(np.stack(fsts, 1))
    ghb = np.ascontiguousarray(np.broadcast_to(gh[None, :], (128, D)))
    maps = []
    for c in range(NCORES):
        m = mlstm_common_inputs(xts[c], g, win, bgates)
        cmask = np.zeros((128, 8), np.float32)
        cmask[:, :c] = 1.0
        m.update({"gh": ghb, "wout": wout, "cst_in": cst_all, "fst_in": fst_all, "cmask": cmask})
        maps.append(m)
    res = run_bass_kernel_spmd(nc, maps, core_ids=list(range(NCORES)))
    return [r["y"] for r in res.results]


def rope_tables_np():
    pos = np.arange(S, dtype=np.float32)
    inv = (np.float32(500000.0) ** (-(np.arange(0, 32, 2, dtype=np.float32) / np.float32(32)))).astype(np.float32)
    ang = (pos[:, None] * inv[None, :]).astype(np.float32)
    return np.cos(ang).astype(np.float32), np.sin(ang).astype(np.float32)


class NormT:
    def __init__(self, P, pT, ident, g_b):
        self.P, self.pT, self.ident, self.g_b = P, pT, ident, g_b
        self.xin = [P.sb(f"nt_xin{i}", [128, D], F32) for i in range(2)]
        self.xn = [P.sb(f"nt_xn{i}", [128, D], BF16) for i in range(2)]
        self.junk = P.sb("nt_junk", [128, D], BF16)
        self.ssq = P.sb("nt_ssq", [128, 2], F32)
        self.rstd = P.sb("nt_rstd", [128, 2], F32)
        self.rtmp = P.sb("nt_rtmp", [128, 2], F32)
        self.xnT = P.sb("nt_xnT", [128, 8, 128], BF16)
        self.n = 0

    def emit(self, x_ap):
        P = self.P
        i = self.n % 2
        self.n += 1
        xi, xb = self.xin[i], self.xn[i]
        xk, bk = f"nt_xin{i}", f"nt_xn{i}"
        P.dma("sp", xk, [(xi[:], x_ap)], w=[xk])
        P.op("act", lambda e: e.activation(out=self.junk[:], in_=xi[:], func=AF.Square, accum_out=self.ssq[:, i:i + 1]),
             r=[xk], w=["nt_junk", f"nt_ssq{i}"])
        emit_rstd(P, self.ssq[:, i:i + 1], self.rstd[:, i:i + 1], 1, f"nt_ssq{i}", f"nt_rstd{i}", self.rtmp[:, i:i + 1])
        P.op("dve", lambda e: e.scalar_tensor_tensor(out=xb[:], in0=xi[:], scalar=self.rstd[:, i:i + 1], in1=self.g_b[:],
                                                     op0=ALU.mult, op1=ALU.mult), r=[xk, f"nt_rstd{i}", "g_b"], w=[bk])
        for k in range(8):
            P.op("pe", lambda e, k=k: e.transpose(self.pT[:, k, :], xb[:, k * 128:(k + 1) * 128], self.ident[:]),
                 r=[bk, "ident"], w=["pT"])
        P.op("act", lambda e: e.copy(out=self.xnT[:], in_=self.pT[:, :, :]), r=["pT"], w=["nt_xnT"])
        return self.xnT


class HeadNormRope:
    def __init__(self, P, gh_b, cos, sin, scale):
        self.P, self.gh_b, self.cos, self.sin, self.scale = P, gh_b, cos, sin, scale
        self.sq = P.sb("hn_sq", [128, D], F32)
        self.ssq = P.sb("hn_ssq", [128, 8], F32)
        self.rstd = P.sb("hn_rstd", [128, 8], F32)
        self.rtmp = P.sb("hn_rtmp", [128, 8], F32)
        self.rt = P.sb("hn_rt", [128, 4, 8, 16], F32)

    def emit(self, ps, pkeys, kf, kfkey, t):
        P = self.P
        P.op("act", lambda e: e.activation(out=self.sq[:], in_=ps, func=AF.Square), r=pkeys, w=["hn_sq"])
        P.op("dve", lambda e: e.tensor_reduce(out=self.ssq[:], in_=self.sq[:].rearrange("p (h e) -> p h e", h=8),
                                              axis=AX.X, op=ALU.add), r=["hn_sq"], w=["hn_ssq"])
        emit_rstd(P, self.ssq, self.rstd, 8, "hn_ssq", "hn_rstd", self.rtmp, dim=128)
        kf3 = kf.rearrange("p (h e) -> p h e", h=8)
        P.op("dve", lambda e: e.scalar_tensor_tensor(out=kf3, in0=ps.rearrange("p (h e) -> p h e", h=8), scalar=self.scale,
                                                     in1=self.rstd[:, :].unsqueeze(2).to_broadcast([128, 8, 128]),
                                                     op0=ALU.mult, op1=ALU.mult),
             r=pkeys + ["hn_rstd"], w=[kfkey])
        P.op("pool", lambda e: e.tensor_tensor(out=kf3, in0=kf3, in1=self.gh_b[:, :].unsqueeze(1).to_broadcast([128, 8, 128]),
                                               op=ALU.mult), r=[kfkey, "gh_b"], w=[kfkey])
        t1, t2 = kf3[:, :, 0:16], kf3[:, :, 16:32]
        cb = self.cos[:, t, :].unsqueeze(1).to_broadcast([128, 8, 16])
        sb_ = self.sin[:, t, :].unsqueeze(1).to_broadcast([128, 8, 16])
        rt = self.rt
        for (j, a, b) in ((0, t1, cb), (1, t2, sb_), (2, t2, cb), (3, t1, sb_)):
            P.op("pool", lambda e, j=j, a=a, b=b: e.tensor_tensor(out=rt[:, j, :, :], in0=a, in1=b, op=ALU.mult),
                 r=[kfkey, "cs"], w=[f"hn_rt{j}"])
        P.op("pool", lambda e: e.tensor_tensor(out=t1, in0=rt[:, 0, :, :], in1=rt[:, 1, :, :], op=ALU.subtract),
             r=["hn_rt0", "hn_rt1"], w=[kfkey])
        P.op("pool", lambda e: e.tensor_tensor(out=t2, in0=rt[:, 2, :, :], in1=rt[:, 3, :, :], op=ALU.add),
             r=["hn_rt2", "hn_rt3"], w=[kfkey])


def build_kv(ctx=None):
    CX = ctx if ctx is not None else Ctx()
    nc = CX.nc
    x_d = CX.I("x", [16, 128, D])
    g_d = CX.I("g", [128, D])
    w_d = CX.I("wkv", [D, 2 * D])
    gk_d = CX.I("gk", [128, 128])
    cos_d = CX.I("cos", [128, 16, 16])
    sin_d = CX.I("sin", [128, 16, 16])
    id_d = CX.I("ident", [128, 128], BF16)
    kt_o = CX.O("kt", [8, 128, 2048], BF16)
    v_o = CX.O("v", [8, 128, 16, 130], BF16)
    km_o = CX.O("km", [128, 8, 8])
    with ExitStack() as es:
        P = CX.prog(es)
        make_eps(P)
        g_b = P.sb("g_b", [128, D], F32)
        gk_b = P.sb("gk_b", [128, 128], F32)
        cos = P.sb("cos_s", [128, 16, 16], F32)
        sin = P.sb("sin_s", [128, 16, 16], F32)
        ident = P.sb("ident_s", [128, 128], BF16)
        onesf = P.sb("onesf", [128, 1], F32)
        w = P.sb("w_s", [128, 8, 2 * D], BF16)
        kTall = P.sb("kTall", [128, 8, 2048], BF16)
        vall = P.sb("vall", [128, 8, 16, 130], BF16)
        kf = [P.sb(f"kf{i}", [128, D], F32) for i in range(2)]
        kb = P.sb("kb", [128, D], BF16)
        kms = P.sb("kms", [128, 8, 8], F32)
        pT = P.ps("pT", [128, 8, 128], BF16)
        pK = P.ps("pK", [128, 1024], F32)
        pV = P.ps("pV", [128, 1024], F32)
        pKM = P.ps("pKM", [128, 64], F32)
        P.dma("sp", "c0", [(g_b[:], g_d[:, :]), (gk_b[:], gk_d[:, :]), (cos[:], cos_d[:, :, :]), (sin[:], sin_d[:, :, :]),
                           (ident[:], id_d[:, :])], w=["g_b", "gh_b", "cs", "ident"])
        P.op("pool", lambda e: e.memset(onesf[:], 1.0), w=["onesf"])
        P.op("pool", lambda e: e.memset(vall[:], 1.0), w=["vall"])
        for k in range(8):
            P.dma("pool", "w", [(w[:, k, :], w_d[k * 128:(k + 1) * 128, :])], w=[f"w{k}"])
        nt = NormT(P, pT, ident, g_b)
        hn = HeadNormRope(P, gk_b, cos, sin, 1.0)
        for t in range(16):
            xnT = nt.emit(x_d[t, :, :])
            for (ps, pk, c0) in ((pK, "pK", 0), (pV, "pV", D)):
                for cg in range(2):
                    for k in range(8):
                        P.op("pe", lambda e, k=k, cg=cg, ps=ps, c0=c0: e.matmul(
                            ps[:, cg * 512:(cg + 1) * 512], xnT[:, k, :], w[:, k, c0 + cg * 512:c0 + (cg + 1) * 512],
                            start=(k == 0), stop=(k == 7)), r=["nt_xnT", f"w{k}"], w=[pk])
            P.op("act", lambda e, t=t: e.copy(out=vall[:, :, t, 0:128], in_=pV[:].rearrange("p (h e) -> p h e", h=8)),
                 r=["pV"], w=["vall"])
            kfi = kf[t % 2]
            kk = f"kf{t % 2}"
            hn.emit(pK[:], ["pK"], kfi[:], kk, t)
            P.op("act", lambda e, kfi=kfi: e.copy(out=kb[:], in_=kfi[:]), r=[kk], w=["kb"])
            for h in range(8):
                P.op("pe", lambda e, h=h: e.transpose(pT[:, h, :], kb[:, h * 128:(h + 1) * 128], ident[:]), r=["kb", "ident"], w=["pT"])
            P.op("act", lambda e, t=t: e.copy(out=kTall[:, :, t * 128:(t + 1) * 128], in_=pT[:, :, :]), r=["pT"], w=["kTall"])
            if t % 2 == 1:
                blk = t // 2
                for h in range(8):
                    for j in range(2):
                        P.op("pe", lambda e, h=h, j=j, blk=blk: e.matmul(
                            pKM[:, h * 8 + blk:h * 8 + blk + 1], kf[j][:, h * 128:(h + 1) * 128], onesf[:],
                            start=(j == 0), stop=(j == 1)), r=[f"kf{j}", "onesf"], w=["pKM"])
        P.op("act", lambda e: e.mul(out=kms[:].rearrange("p h b -> p (h b)"), in_=pKM[:], mul=1.0 / 256.0), r=["pKM"], w=["kms"])
        P.dma("sp", "o0", [(kt_o[h, :, :], kTall[:, h, :]) for h in range(8)], r=["kTall"], w=["o_kt"])
        P.dma("sp", "o1", [(v_o[h, :, :, :], vall[:, h, :, :]) for h in range(8)], r=["vall"], w=["o_v"])
        P.dma("sp", "o2", [(km_o[:, :, :], kms[:])], r=["kms"], w=["o_km"])
        CX.done(P, ["o_kt", "o_v", "o_km"])
        print("kv instructions:", P.n_ins)
    return nc


def run_kv(nc, xts, g, wkv, gk, cos_t, sin_t, tile_ids):
    gb = np.ascontiguousarray(np.broadcast_to(g[None, :], (128, D)))
    gkb = np.ascontiguousarray(np.broadcast_to(gk[None, :], (128, 128)))
    maps = []
    for c in range(NCORES):
        cs = np.stack([cos_t[j * 128:(j + 1) * 128] for j in tile_ids[c]], 1)
        sn = np.stack([sin_t[j * 128:(j + 1) * 128] for j in tile_ids[c]], 1)
        maps.append({"x": xts[c], "g": gb, "wkv": wkv, "gk": gkb, "cos": np.ascontiguousarray(cs),
                     "sin": np.ascontiguousarray(sn), "ident": IDENT_BF})
    res = run_bass_kernel_spmd(nc, maps, core_ids=list(range(NCORES)))
    kt = np.concatenate([r["kt"] for r in res.results], 2)
    v = np.concatenate([r["v"] for r in res.results], 2)
    km = np.concatenate([r["km"] for r in res.results], 2)
    return kt, v, km


def build_moba(ctx=None):
    CX = ctx if ctx is not None else Ctx()
    nc = CX.nc
    x_d = CX.I("x", [16, 128, D])
    g_d = CX.I("g", [128, D])
    wq_d = CX.I("wq", [D, D])
    wo_d = CX.I("wo", [D, D])
    gq_d = CX.I("gq", [128, 128])
    cos_d = CX.I("cos", [128, 16, 16])
    sin_d = CX.I("sin", [128, 16, 16])
    id_d = CX.I("ident", [128, 128], BF16)
    kt_d = CX.I("kt", [8, 8, 128, 2048], BF16)
    v_d = CX.I("v", [8, 8, 128, 16, 130], BF16)
    km_d = CX.I("km", [8, 128, 8, 8])
    gb_d = CX.I("gbias", [128, 16, 64])
    own_d = CX.I("own", [128, 16, 64])
    dg_d = CX.I("diag", [128, 8, 128], BF16)
    y_d = CX.O("y", [16, 128, D])
    with ExitStack() as es:
        P = CX.prog(es)
        make_eps(P)
        g_b = P.sb("g_b", [128, D], F32)
        gq_b = P.sb("gq_b", [128, 128], F32)
        cos = P.sb("cos_s", [128, 16, 16], F32)
        sin = P.sb("sin_s", [128, 16, 16], F32)
        ident = P.sb("ident_s", [128, 128], BF16)
        w = P.sb("w_s", [128, 8, D], BF16)
        gbias = P.sb("gbias_s", [128, 16, 64], F32)
        own = P.sb("own_s", [128, 16, 64], F32)
        diag = P.sb("diag_s", [128, 8, 128], BF16)
        kmf = P.sb("kmf", [128, 8, 64], F32)
        kmb = P.sb("kmb", [128, 8, 64], BF16)
        qT = P.sb("qT", [128, 8, 2048], BF16)
        Kh = P.sb("Kh", [128, S], BF16)
        Vh = P.sb("Vh", [128, 128, 130], BF16)
        Oall = P.sb("Oall", [128, 16, D], BF16)
        sbT = [P.sb(f"sbT{i}", [64, 2048], BF16) for i in range(2)]
        kf = P.sb("kf", [128, D], F32)
        qb = P.sb("qb", [128, D], BF16)
        gm = P.sb("gm", [128, 64], F32)
        top8 = P.sb("top8", [128, 8], F32)
        thr = P.sb("thr", [128, 1], F32)
        sel = P.sb("sel", [128, 64], F32)
        selb = P.sb("selb", [128, 64], BF16)
        PT = [P.sb(f"PT{i}", [128, 512], BF16) for i in range(2)]
        rden = P.sb("rden", [128, 4], F32)
        oT = P.sb("oT", [128, 8, 128], BF16)
        pT = P.ps("pT", [128, 8, 128], BF16)
        pQ = P.ps("pQ", [128, 1024], F32)
        pS = [P.ps(f"pS{i}", [128, 512], F32) for i in range(2)]
        pO23 = [P.ps(f"pO{i}", [128, 512], F32) for i in (2, 3)]
        pOb = [pQ[:, 0:512], pQ[:, 512:1024], pO23[0][:], pO23[1][:]]
        pOk = ["pQa", "pQb", "pO2", "pO3"]
        pGt = P.ps("pGt", [128, 512], F32)

        P.dma("sp", "c0", [(g_b[:], g_d[:, :]), (gq_b[:], gq_d[:, :]), (cos[:], cos_d[:, :, :]), (sin[:], sin_d[:, :, :]),
                           (ident[:], id_d[:, :]), (gbias[:], gb_d[:, :, :]), (own[:], own_d[:, :, :]), (diag[:], dg_d[:, :, :])] +
              [(kmf[:, :, r * 8:(r + 1) * 8], km_d[r, :, :, :]) for r in range(8)],
              w=["g_b", "gh_b", "cs", "ident", "gbias", "own", "diag", "kmf"])
        P.op("dve", lambda e: e.tensor_copy(out=kmb[:], in_=kmf[:]), r=["kmf"], w=["kmb"])
        P.dma("pool", "w", [(w[:], wq_d.rearrange("(k p) n -> p k n", p=128))], w=["w"])

        nt = NormT(P, pT, ident, g_b)
        hn = HeadNormRope(P, gq_b, cos, sin, 128.0 ** -0.5)
        for t in range(16):
            xnT = nt.emit(x_d[t, :, :])
            for cg in range(2):
                for k in range(8):
                    P.op("pe", lambda e, k=k, cg=cg: e.matmul(pQ[:, cg * 512:(cg + 1) * 512], xnT[:, k, :],
                                                             w[:, k, cg * 512:(cg + 1) * 512], start=(k == 0), stop=(k == 7)),
                         r=["nt_xnT", "w"], w=["pQa", "pQb"])
            hn.emit(pQ[:], ["pQa", "pQb"], kf[:], "kf", t)
            P.op("act", lambda e: e.copy(out=qb[:], in_=kf[:]), r=["kf"], w=["qb"])
            for h in range(8):
                P.op("pe", lambda e, h=h: e.transpose(pT[:, h, :], qb[:, h * 128:(h + 1) * 128], ident[:]), r=["qb", "ident"], w=["pT"])
            P.op("act", lambda e, t=t: e.copy(out=qT[:, :, t * 128:(t + 1) * 128], in_=pT[:, :, :]), r=["pT"], w=["qT"])
        P.dma("pool", "w", [(w[:], wo_d.rearrange("(k p) n -> p k n", p=128))], r=[], w=["w"])

        nS = [0]
        nO = [0]
        for h in range(8):
            P.dma("sp", "kh", [(Kh[:, r * 2048:(r + 1) * 2048], kt_d[r, h, :, :]) for r in range(8)], w=["Kh"])
            P.dma("sp", "vh", [(Vh[:, r * 16:(r + 1) * 16, :], v_d[r, h, :, :, :]) for r in range(8)], w=["Vh"])
            sT = sbT[h % 2]
            sk = f"sbT{h % 2}"
            for i in range(16):
                P.op("pe", lambda e, i=i, h=h: e.matmul(pGt[:, 0:64], qT[:, h, i * 128:(i + 1) * 128], kmb[:, h, :], start=True, stop=True),
                     r=["qT", "kmb"], w=["pGt"])
                P.op("dve", lambda e, i=i: e.tensor_tensor(out=gm[:], in0=pGt[:, 0:64], in1=gbias[:, i, :], op=ALU.add),
                     r=["pGt", "gbias"], w=["gm"])
                P.op("dve", lambda e: e.max(out=top8[:], in_=gm[:]), r=["gm"], w=["top8"])
                P.op("dve", lambda e: e.tensor_scalar(out=thr[:], in0=top8[:, 2:3], scalar1=-1e29, scalar2=None, op0=ALU.max),
                     r=["top8"], w=["thr"])
                P.op("dve", lambda e: e.tensor_scalar(out=sel[:], in0=gm[:], scalar1=thr[:, 0:1], scalar2=None, op0=ALU.is_ge),
                     r=["gm", "thr"], w=["sel"])
                P.op("dve", lambda e, i=i: e.tensor_tensor(out=sel[:], in0=sel[:], in1=own[:, i, :], op=ALU.add), r=["sel", "own"], w=["sel"])
                P.op("dve", lambda e: e.tensor_scalar(out=selb[:], in0=sel[:], scalar1=-NEG, scalar2=NEG, op0=ALU.mult, op1=ALU.add),
                     r=["sel"], w=["selb"])
                pst = pGt[:, 64:128].bitcast(BF16)
                P.op("pe", lambda e, pst=pst: e.transpose(pst[0:64, :], selb[:], ident[:]), r=["selb", "ident"], w=["pGt"])
                P.op("act", lambda e, i=i, sT=sT, pst=pst: e.copy(out=sT[:, i * 128:(i + 1) * 128], in_=pst[0:64, :]), r=["pGt"], w=[sk])
            for g in range(4):
                qcols = slice(g * 512, (g + 1) * 512)
                first = [True] * 4
                nkt = 32 * (g + 1)
                for kt in range(nkt):
                    ps = pS[nS[0] % 2]
                    psk = f"pS{nS[0] % 2}"
                    pt = PT[nS[0] % 2]
                    ptk = f"PT{nS[0] % 2}"
                    nS[0] += 1
                    n = kt // 2
                    dtile = None
                    for a in range(4):
                        i = 4 * g + a
                        if 8 * i <= kt < 8 * i + 8:
                            dtile = a
                    P.op("pe", lambda e, ps=ps, kt=kt, h=h: e.matmul(ps[:], Kh[:, kt * 128:(kt + 1) * 128], qT[:, h, qcols],
                                                                    start=True, stop=False), r=["Kh", "qT"], w=[psk])
                    P.op("pe", lambda e, ps=ps, n=n, sT=sT: e.matmul(ps[:], ident[0:64, n:n + 1].to_broadcast([64, 128]), sT[:, qcols],
                                                                    start=False, stop=(dtile is None)), r=["ident", sk], w=[psk])
                    if dtile is not None:
                        a = dtile
                        r_ = kt - 8 * (4 * g + a)
                        P.op("pe", lambda e, ps=ps, a=a, r_=r_: e.matmul(ps[:, a * 128:(a + 1) * 128], ident[:], diag[:, r_, :],
                                                                        start=False, stop=True), r=["ident", "diag"], w=[psk])
                    P.op("act", lambda e, ps=ps, pt=pt: e.activation(out=pt[:], in_=ps[:], func=AF.Exp), r=[psk], w=[ptk])
                    for a in range(4):
                        i = 4 * g + a
                        if kt >= 8 * i + 8:
                            continue
                        last = (kt == 8 * i + 7)
                        bank = pOb[a]
                        P.op("pe", lambda e, bank=bank, a=a, pt=pt, kt=kt, f=first[a], last=last: e.matmul(
                            bank[:, 0:130], pt[:, a * 128:(a + 1) * 128], Vh[:, kt, :],
                            start=f, stop=last), r=[ptk, "Vh"], w=[pOk[a]])
                        first[a] = False
                for a in range(4):
                    i = 4 * g + a
                    bank = pOb[a]
                    c0 = 0
                    P.op("dve", lambda e, bank=bank, c0=c0, a=a: e.reciprocal(out=rden[:, a:a + 1], in_=bank[:, c0 + 128:c0 + 129]),
                         r=[pOk[a]], w=["rden"])
                    P.op("dve", lambda e, bank=bank, c0=c0, a=a, i=i, h=h: e.tensor_scalar(
                        out=Oall[:, i, h * 128:(h + 1) * 128], in0=bank[:, c0:c0 + 128], scalar1=rden[:, a:a + 1], scalar2=None,
                        op0=ALU.mult), r=[pOk[a], "rden"], w=[f"Oall{i}"])
        for t in range(16):
            for k in range(8):
                P.op("pe", lambda e, k=k, t=t: e.transpose(pT[:, k, :], Oall[:, t, k * 128:(k + 1) * 128], ident[:]),
                     r=[f"Oall{t}", "ident"], w=["pT"])
            P.op("act", lambda e: e.copy(out=oT[:], in_=pT[:, :, :]), r=["pT"], w=["oT"])
            xi = nt.xin[t % 2]
            xk = f"nt_xin{t % 2}"
            P.dma("sp", xk, [(xi[:], x_d[t, :, :])], w=[xk])
            for cg in range(2):
                for k in range(8):
                    P.op("pe", lambda e, k=k, cg=cg: e.matmul(pQ[:, cg * 512:(cg + 1) * 512], oT[:, k, :],
                                                             w[:, k, cg * 512:(cg + 1) * 512], start=(k == 0), stop=(k == 7)),
                         r=["oT", "w"], w=["pQa", "pQb"])
            P.op("dve", lambda e, xi=xi: e.tensor_tensor(out=xi[:], in0=xi[:], in1=pQ[:], op=ALU.add), r=[xk, "pQa", "pQb"], w=[xk])
            P.dma("sp", f"y{t % 2}", [(y_d[t, :, :], xi[:])], r=[xk], w=[f"yo{t}"])
        CX.done(P, [f"yo{t}" for t in range(16)])
        print("moba instructions:", P.n_ins)
    return nc


def moba_consts(c):
    gb = np.full((16, 64), -1e30, np.float32)
    own = np.zeros((16, 64), np.float32)
    for i in range(16):
        cur = (8 * i + c) // 2
        gb[i, :cur] = 0.0
        own[i, cur] = 1.0
    diag = np.zeros((8, 128, 128), np.float32)
    r0 = c - (c % 2)
    kk = np.arange(128)[:, None]
    qq = np.arange(128)[None, :]
    for r in (r0, r0 + 1):
        kpos = r * 128 + kk
        qpos = c * 128 + qq
        diag[r] = np.where(kpos <= qpos, 0.0, NEG)
    gbb = np.ascontiguousarray(np.broadcast_to(gb[None], (128, 16, 64)))
    ownb = np.ascontiguousarray(np.broadcast_to(own[None], (128, 16, 64)))
    diagb = np.ascontiguousarray(diag.transpose(1, 0, 2)).astype(ml_dtypes.bfloat16)
    return gbb, ownb, diagb


def moba_inputs(c, xt, g, wq, wo, gq, cos_t, sin_t, kt, v, km):
    tiles = [8 * i + c for i in range(16)]
    gbb, ownb, diagb = moba_consts(c)
    return {"x": xt, "g": np.ascontiguousarray(np.broadcast_to(g[None, :], (128, D))), "wq": wq, "wo": wo,
            "gq": np.ascontiguousarray(np.broadcast_to(gq[None, :], (128, 128))),
            "cos": np.ascontiguousarray(np.stack([cos_t[j * 128:(j + 1) * 128] for j in tiles], 1)),
            "sin": np.ascontiguousarray(np.stack([sin_t[j * 128:(j + 1) * 128] for j in tiles], 1)),
            "ident": IDENT_BF, "kt": kt, "v": v, "km": km, "gbias": gbb, "own": ownb, "diag": diagb}


def run_moba(nc, xts, g, wq, wo, gq, cos_t, sin_t, kt, v, km):
    maps = [moba_inputs(c, xts[c], g, wq, wo, gq, cos_t, sin_t, kt, v, km) for c in range(NCORES)]
    res = run_bass_kernel_spmd(nc, maps, core_ids=list(range(NCORES)))
    return [r["y"] for r in res.results]


def emit_tails(ctx):
    CX = ctx
    x_d = CX.I("x", [16, 128, D])
    t_d = CX.O("tails", [32, D])
    with ExitStack() as es:
        P = CX.prog(es)
        P.dma("sp", "tl", [(t_d.rearrange("(t r) n -> t r n", r=2), x_d[:, 126:128, :])], w=["tails"])
        CX.done(P, ["tails"])


def emit_halo(ctx, lay):
    CX = ctx
    tg_d = CX.I("tails_g", [8 * 32, D])
    hs_d = CX.I("hsel", [128, 6, 8])
    xh_d = CX.O("xh", [32, D])
    with ExitStack() as es:
        P = CX.prog(es)
        G = P.sb("G", [32, 3, 8, D], F32)
        hs = P.sb("hs", [128, 6, 8], F32)
        acc = P.sb("acc", [32, D], F32)
        tg3 = tg_d.rearrange("(r q) n -> q r n", q=32)
        P.op("pool", lambda e: e.memset(G[:, 1:3, :, :], 0.0), w=["G"])
        P.dma("sp", "hl", [(hs[:], hs_d[:, :, :]), (G[:, 0, :, :], tg3),
                           (G[2:32, 1, :, :], tg3[0:30, :, :]), (G[0:2, 2, :, :], tg3[30:32, :, :])], w=["G", "hs"])
        first = True
        for v in range(3):
            for r in range(8):
                sc = hs[0:32, lay * 3 + v, r:r + 1]
                if first:
                    P.op("dve", lambda e, v=v, r=r, sc=sc: e.tensor_scalar(out=acc[:], in0=G[:, v, r, :], scalar1=sc, scalar2=None,
                                                                           op0=ALU.mult), r=["G", "hs"], w=["acc"])
                    first = False
                else:
                    P.op("dve", lambda e, v=v, r=r, sc=sc: e.scalar_tensor_tensor(out=acc[:], in0=G[:, v, r, :], scalar=sc, in1=acc[:],
                                                                                  op0=ALU.mult, op1=ALU.add), r=["G", "hs", "acc"], w=["acc"])
        P.dma("sp", "ho", [(xh_d[:, :], acc[:])], r=["acc"], w=["xh"])
        CX.done(P, ["xh"])


def emit_relayout(ctx):
    CX = ctx
    hg_d = CX.I("hgath", [8 * 2048, D])
    sc_d = CX.I("selc", [128, 8])
    y_d = CX.O("y", [16, 128, D])
    with ExitStack() as es:
        P = CX.prog(es)
        cand = [P.sb(f"cand{i}", [128, 8, D], F32) for i in range(2)]
        acc = [P.sb(f"acc{i}", [128, D], F32) for i in range(2)]
        sc = P.sb("sc", [128, 8], F32)
        P.dma("sp", "rs", [(sc[:], sc_d[:, :])], w=["sc"])
        for i in range(16):
            cb, ck = cand[i % 2], f"cand{i % 2}"
            ab, ak = acc[i % 2], f"acc{i % 2}"
            row0 = ((i // 2) * 16 + 8 * (i % 2)) * 128
            P.dma("sp", ck, [(cb[:], hg_d[row0:row0 + 1024, :].rearrange("(c p) n -> p c n", p=128))], w=[ck])
            P.op("dve", lambda e, cb=cb, ab=ab: e.tensor_scalar(out=ab[:], in0=cb[:, 0, :], scalar1=sc[:, 0:1], scalar2=None, op0=ALU.mult),
                 r=[ck, "sc"], w=[ak])
            for c2 in range(1, 8):
                P.op("dve", lambda e, cb=cb, ab=ab, c2=c2: e.scalar_tensor_tensor(out=ab[:], in0=cb[:, c2, :], scalar=sc[:, c2:c2 + 1], in1=ab[:],
                                                                                  op0=ALU.mult, op1=ALU.add), r=[ck, "sc", ak], w=[ak])
            P.dma("sp", f"ro{i % 2}", [(y_d[i, :, :], ab[:])], r=[ak], w=[f"yo{i}"])
        CX.done(P, [f"yo{i}" for i in range(16)])


def build_fused():
    nc = new_nc()
    E = lambda name, shape, dt=F32: din(nc, name, shape, dt)
    x_d = E("x", [16, 128, D])
    a_norm = E("a_norm_b", [2, 128, D]); a_win = E("a_w_in", [2, D, MIN_]); a_bg = E("a_bg_b", [2, 128, 8])
    a_hn = E("a_h_norm_b", [2, 128, D]); a_wout = E("a_w_out", [2, D, D])
    kvn = E("kv_norm_b", [128, D]); wkv = E("w_kv", [D, 2 * D]); kn = E("k_norm_b", [128, 128])
    b_norm = E("b_norm_b", [2, 128, D]); b_wq = E("b_w_q", [2, D, D]); b_qn = E("b_q_norm_b", [2, 128, 128]); b_wo = E("b_w_o", [2, D, D])
    f_norm = E("f_norm_b", [4, 128, D]); f_wup = E("f_w_up", [4, D, 2 * DFF]); f_cw = E("f_cw", [4, 128, 44, 3])
    f_cb = E("f_cb", [4, 128, 44]); f_wdn = E("f_w_down", [4, DFF, D])
    ident = E("ident", [128, 128], BF16); tri = E("tri", [128, 128])
    cmask = E("cmask", [128, 8]); hsel = E("hsel", [128, 6, 8]); selc = E("selc", [128, 8])
    cosC = E("cosC", [128, 16, 16]); sinC = E("sinC", [128, 16, 16]); cosI = E("cosI", [128, 16, 16]); sinI = E("sinI", [128, 16, 16])
    gbias = E("gbias", [128, 16, 64]); own = E("own", [128, 16, 64]); diag = E("diag", [128, 8, 128], BF16)
    y_d = dout(nc, "y", [16, 128, D])
    T = lambda name, shape, dt=F32: nc.dram_tensor(name, list(shape), dt)
    hA = T("hA", [2048, D]); hB = T("hB", [2048, D])
    st_src = T("st_src", [128, 1032]); st_g = T("st_g", [8 * 128, 1032])
    tl_src = T("tl_src", [32, D]); tl_g = T("tl_g", [8 * 32, D]); xh = T("xh_d", [32, D])
    kt_src = T("kt_src", [8 * 128, 2048], BF16); kt_g = T("kt_g", [64 * 128, 2048], BF16)
    v_src = T("v_src", [8 * 128, 16 * 130], BF16); v_g = T("v_g", [64 * 128, 16 * 130], BF16)
    km_src = T("km_src", [128, 64]); km_g = T("km_g", [8 * 128, 64])
    hgath = T("hgath", [8 * 2048, D])
    t3 = lambda t: t.ap().rearrange("(t p) n -> t p n", p=128)
    hA3, hB3 = t3(hA), t3(hB)
    with ExitStack() as es_sem:
        P = Prog(nc, None, es_sem)
        ctx = lambda aps: Ctx(nc, P, aps)

        def gather(src, dst):
            P.begin_phase(None)
            P.coll("AllGather", src.ap().opt(), dst.ap().opt())
            P.end_phase()

        def ffn(l, lay, xin, yout):
            emit_tails(ctx({"x": xin, "tails": tl_src.ap()}))
            gather(tl_src, tl_g)
            emit_halo(ctx({"tails_g": tl_g.ap(), "hsel": hsel, "xh": xh.ap()}), lay)
            build_ffn(ctx({"x": xin, "xh": xh.ap(), "g": f_norm[l, :, :], "wup": f_wup[l, :, :], "cw": f_cw[l, :, :, :],
                           "cb": f_cb[l, :, :], "wdn": f_wdn[l, :, :], "ident": ident, "y": yout}))

        cur = x_d
        for l in range(2):
            base = {"x": cur, "g": a_norm[l, :, :], "win": a_win[l, :, :], "bg": a_bg[l, :, :], "ident": ident, "tri": tri}
            build_mlstm(True, ctx(dict(base, cst=st_src.ap()[:, 0:1028], fst=st_src.ap()[:, 1028:1032])))
            gather(st_src, st_g)
            build_mlstm(False, ctx(dict(base, gh=a_hn[l, :, :], wout=a_wout[l, :, :],
                                        cst_in=st_g.ap().rearrange("(r p) n -> r p n", p=128)[:, :, 0:1028],
                                        fst_in=st_g.ap().rearrange("(r p) n -> p r n", p=128)[:, :, 1028:1032],
                                        cmask=cmask, y=hA3)))
            ffn(l, 0, hA3, hB3)
            cur = hB3
        build_kv(ctx({"x": hB3, "g": kvn, "wkv": wkv, "gk": kn, "cos": cosC, "sin": sinC, "ident": ident,
                      "kt": kt_src.ap().rearrange("(h p) n -> h p n", p=128),
                      "v": v_src.ap().rearrange("(h p) (t e) -> h p t e", p=128, e=130),
                      "km": km_src.ap().rearrange("p (h b) -> p h b", b=8)}))
        gather(kt_src, kt_g)
        gather(v_src, v_g)
        gather(km_src, km_g)
        gather(hB, hgath)
        emit_relayout(ctx({"hgath": hgath.ap(), "selc": selc, "y": hA3}))
        for j in range(2):
            build_moba(ctx({"x": hA3, "g": b_norm[j, :, :], "wq": b_wq[j, :, :], "wo": b_wo[j, :, :], "gq": b_qn[j, :, :],
                            "cos": cosI, "sin": sinI, "ident": ident,
                            "kt": kt_g.ap().rearrange("(r h p) n -> r h p n", h=8, p=128),
                            "v": v_g.ap().rearrange("(r h p) (t e) -> r h p t e", h=8, p=128, e=130),
                            "km": km_g.ap().rearrange("(r p) (h b) -> r p h b", p=128, b=8),
                            "gbias": gbias, "own": own, "diag": diag, "y": hB3}))
            ffn(2 + j, 1, hB3, hA3 if j == 0 else y_d)
        print("fused instructions:", P.n_ins)
    return nc


_NC = [None]


def fused_inputs(c, x, a_norm, a_w_in, a_b_gates, a_h_norm, a_w_out, kv_norm, w_kv, k_norm,
                 b_norm, b_w_q, b_q_norm, b_w_o, f_norm, f_w_up, f_conv_w, f_conv_b, f_w_down, cos_t, sin_t):
    rep = lambda a, n=128: np.ascontiguousarray(np.broadcast_to(a[..., None, :], a.shape[:-1] + (n, a.shape[-1])))
    contig = list(range(16 * c, 16 * c + 16))
    inter = [8 * i + c for i in range(16)]
    tab = lambda t, ids: np.ascontiguousarray(np.stack([t[j * 128:(j + 1) * 128] for j in ids], 1))
    cmask = np.zeros((128, 8), np.float32); cmask[:, :c] = 1.0
    hsel = np.zeros((128, 6, 8), np.float32)
    hsel[:, 1, c] = 1.0
    if c >= 1:
        hsel[:, 2, c - 1] = 1.0
        hsel[:, 3, c - 1] = 1.0
    else:
        hsel[:, 4, 7] = 1.0
    selc = np.zeros((128, 8), np.float32); selc[:, c] = 1.0
    gbb, ownb, diagb = moba_consts(c)
    return {
        "x": np.ascontiguousarray(x[c * 2048:(c + 1) * 2048].reshape(16, 128, D)),
        "a_norm_b": rep(a_norm), "a_w_in": a_w_in, "a_bg_b": rep(a_b_gates.reshape(2, 8)), "a_h_norm_b": rep(a_h_norm), "a_w_out": a_w_out,
        "kv_norm_b": rep(kv_norm), "w_kv": w_kv, "k_norm_b": rep(k_norm),
        "b_norm_b": rep(b_norm), "b_w_q": b_w_q, "b_q_norm_b": rep(b_q_norm), "b_w_o": b_w_o,
        "f_norm_b": rep(f_norm), "f_w_up": f_w_up,
        "f_cw": np.ascontiguousarray(f_conv_w.reshape(4, 3, 44, 128).transpose(0, 3, 2, 1)),
        "f_cb": np.ascontiguousarray(f_conv_b.reshape(4, 44, 128).transpose(0, 2, 1)), "f_w_down": f_w_down,
        "ident": IDENT_BF, "tri": TRI, "cmask": cmask, "hsel": hsel, "selc": selc,
        "cosC": tab(cos_t, contig), "sinC": tab(sin_t, contig), "cosI": tab(cos_t, inter), "sinI": tab(sin_t, inter),
        "gbias": gbb, "own": ownb, "diag": diagb,
    }


def kernel(x, a_norm, a_w_in, a_b_gates, a_h_norm, a_w_out, kv_norm, w_kv, k_norm,
           b_norm, b_w_q, b_q_norm, b_w_o, f_norm, f_w_up, f_conv_w, f_conv_b, f_w_down):
    f32 = lambda a: np.ascontiguousarray(np.asarray(a, dtype=np.float32))
    args = [f32(a) for a in (x, a_norm, a_w_in, a_b_gates, a_h_norm, a_w_out, kv_norm, w_kv, k_norm,
                             b_norm, b_w_q, b_q_norm, b_w_o, f_norm, f_w_up, f_conv_w, f_conv_b, f_w_down)]
    args[0] = args[0].reshape(S, D)
    cos_t, sin_t = rope_tables_np()
    if _NC[0] is None:
        _NC[0] = build_fused()
    maps = [fused_inputs(c, *args, cos_t, sin_t) for c in range(NCORES)]
    res = run_bass_kernel_spmd(_NC[0], maps, core_ids=list(range(NCORES)))
    out = np.empty((S, D), np.float32)
    for c in range(NCORES):
        y = res.results[c]["y"]
        for i in range(16):
            j = 8 * i + c
            out[j * 128:(j + 1) * 128] = y[i]
    return out.reshape(1, S, D)
```

```python
import numpy as np
from contextlib import ExitStack
import ml_dtypes
import concourse.bass as bass
import concourse.mybir as mybir
from concourse.bass_utils import run_bass_kernel_spmd

F32 = mybir.dt.float32
BF16 = mybir.dt.bfloat16
AF = mybir.ActivationFunctionType
ALU = mybir.AluOpType
AX = mybir.AxisListType

NCORES = 8
S = 16384
D = 1024
EPS = 1e-6
DFF = 2816
NEG = -30000.0


class Prog:
    def __init__(self, nc, es, es_sem=None):
        self.nc = nc
        self.es = es
        es_sem = es if es_sem is None else es_sem
        self.es_sem = es_sem
        self.eng = {"pe": nc.tensor, "act": nc.scalar, "dve": nc.vector, "pool": nc.gpsimd, "sp": nc.sync}
        self.esem = {}
        self.ecnt = {}
        for e in ("pe", "act", "dve", "pool"):
            self.esem[e] = es_sem.enter_context(nc.semaphore("s_" + e))
            self.ecnt[e] = 0
        self.waited = {}
        self.lastw = {}
        self.readers = {}
        self.dsem = {}
        self.dcnt = {}
        self.n_ins = 0
        self.phase = 0

    def sb(self, name, shape, dt):
        return self.es.enter_context(self.nc.sbuf_tensor(f"{name}_p{self.phase}", list(shape), dt))

    def ps(self, name, shape, dt):
        return self.es.enter_context(self.nc.psum_tensor(f"{name}_p{self.phase}", list(shape), dt))

    def _waits(self, e, r, w, dma=False):
        deps = []
        for k in r:
            t = self.lastw.get(k)
            if t is not None:
                deps.append((t, True))
        for k in w:
            t = self.lastw.get(k)
            if t is not None:
                deps.append((t, False))
            for t in self.readers.get(k, ()):
                deps.append((t, False))
        eng = self.eng[e]
        for (t, raw) in deps:
            sem, val, owner = t
            if owner == e and e == "pe" and not raw and not dma:
                continue
            key = (e, sem.name)
            if self.waited.get(key, 0) >= val:
                continue
            eng.wait_ge(sem, val)
            self.waited[key] = val

    def _commit(self, tok, r, w):
        for k in w:
            self.lastw[k] = tok
            self.readers[k] = []
        for k in r:
            if k in w:
                continue
            self.readers.setdefault(k, []).append(tok)
            if len(self.readers[k]) > 24:
                best = {}
                for t in self.readers[k]:
                    if t[0].name not in best or best[t[0].name][1] < t[1]:
                        best[t[0].name] = t
                self.readers[k] = list(best.values())

    def op(self, e, fn, r=(), w=()):
        self._waits(e, r, w)
        ins = fn(self.eng[e])
        self.ecnt[e] += 1
        ins.then_inc(self.esem[e], 1)
        self._commit((self.esem[e], self.ecnt[e], e), r, w)
        self.n_ins += 1

    def dma(self, q, chan, pairs, r=(), w=()):
        if chan not in self.dsem:
            self.dsem[chan] = self.es_sem.enter_context(self.nc.semaphore("d_" + chan))
            self.dcnt[chan] = 0
        sem = self.dsem[chan]
        prev = self.dcnt[chan]
        self._waits(q, r, w, dma=True)
        if prev and self.waited.get((q, sem.name), 0) < prev:
            self.eng[q].wait_ge(sem, prev)
            self.waited[(q, sem.name)] = prev
        for (o, i) in pairs:
            self.eng[q].dma_start(out=o, in_=i).then_inc(sem, 16)
            self.dcnt[chan] += 16
            self.n_ins += 1
        self._commit((sem, self.dcnt[chan], "dma"), r, w)

    def coll(self, kind, src_ap, dst_ap, r=(), w=()):
        self.dma_like_wait("pool", r, w)
        if "cc" not in self.dsem:
            self.dsem["cc"] = self.es_sem.enter_context(self.nc.semaphore("d_cc"))
            self.dcnt["cc"] = 0
        sem = self.dsem["cc"]
        self.nc.gpsimd.collective_compute(kind, ALU.bypass, replica_groups=[list(range(NCORES))],
                                          ins=[src_ap], outs=[dst_ap]).then_inc(sem, 1)
        self.dcnt["cc"] += 1
        self.n_ins += 1
        self._commit((sem, self.dcnt["cc"], "dma"), r, w)

    def dma_like_wait(self, q, r, w):
        self._waits(q, r, w, dma=True)

    def begin_phase(self, es):
        self.es = es
        self.phase += 1
        self.lastw = {}
        self.readers = {}
        return self

    def barrier(self):
        for e in ("pe", "act", "dve", "pool", "sp"):
            for o in ("pe", "act", "dve", "pool"):
                if o != e and self.ecnt[o] and self.waited.get((e, self.esem[o].name), 0) < self.ecnt[o]:
                    self.eng[e].wait_ge(self.esem[o], self.ecnt[o])
                    self.waited[(e, self.esem[o].name)] = self.ecnt[o]
            for ch, sem in self.dsem.items():
                if self.dcnt[ch] and self.waited.get((e, sem.name), 0) < self.dcnt[ch]:
                    self.eng[e].wait_ge(sem, self.dcnt[ch])
                    self.waited[(e, sem.name)] = self.dcnt[ch]

    def end_phase(self):
        self.barrier()
        self.lastw = {}
        self.readers = {}

    def finish(self, keys):
        self._waits("sp", keys, ())
        for e in ("pe", "act", "dve", "pool"):
            if self.ecnt[e]:
                self.eng["sp"].wait_ge(self.esem[e], self.ecnt[e])


def new_nc():
    return bass.Bass("TRN2", target_bir_lowering=False)


def din(nc, name, shape, dt=F32):
    return nc.dram_tensor(name, list(shape), dt, kind="ExternalInput").ap()


def dout(nc, name, shape, dt=F32):
    return nc.dram_tensor(name, list(shape), dt, kind="ExternalOutput").ap()


class Ctx:
    def __init__(self, nc=None, P=None, aps=None):
        self.fused = nc is not None
        self.nc = nc if self.fused else new_nc()
        self.P = P
        self.aps = aps

    def I(self, name, shape, dt=F32):
        if self.fused:
            ap = self.aps[name]
            assert list(ap.shape) == list(shape), (name, ap.shape, shape)
            return ap
        return din(self.nc, name, shape, dt)

    def O(self, name, shape, dt=F32):
        if self.fused:
            ap = self.aps[name]
            assert list(ap.shape) == list(shape), (name, ap.shape, shape)
            return ap
        return dout(self.nc, name, shape, dt)

    def prog(self, es):
        if self.fused:
            return self.P.begin_phase(es)
        return Prog(self.nc, es)

    def done(self, P, keys):
        if self.fused:
            P.end_phase()
        else:
            P.finish(keys)


def bf16_np(a):
    return a.astype(ml_dtypes.bfloat16)


IDENT_BF = np.eye(128, dtype=np.float32).astype(ml_dtypes.bfloat16)
IDENT_F = np.eye(128, dtype=np.float32)


def emit_rstd(P, ssq, rstd, n, key_ssq, key_rstd, tmp, dim=D):
    P.op("act", lambda e: e.activation(out=tmp[:, 0:n], in_=ssq[:, 0:n], func=AF.Ln, scale=1.0 / dim, bias=EPS_AP[0][:, 0:1]),
         r=[key_ssq, "epsc"], w=[key_rstd + "_t"])
    P.op("act", lambda e: e.activation(out=rstd[:, 0:n], in_=tmp[:, 0:n], func=AF.Exp, scale=-0.5),
         r=[key_rstd + "_t"], w=[key_rstd])


EPS_AP = [None]


def make_eps(P):
    t = P.sb("epsc", [128, 1], F32)
    P.op("pool", lambda e: e.memset(t[:], EPS), w=["epsc"])
    EPS_AP[0] = t


def build_ffn(ctx=None):
    CX = ctx if ctx is not None else Ctx()
    nc = CX.nc
    x_d = CX.I("x", [16, 128, D])
    xh_d = CX.I("xh", [32, D])
    g_d = CX.I("g", [128, D])
    wup_d = CX.I("wup", [D, 2 * DFF])
    cw_d = CX.I("cw", [128, 44, 3])
    cb_d = CX.I("cb", [128, 44])
    wdn_d = CX.I("wdn", [DFF, D])
    id_d = CX.I("ident", [128, 128], BF16)
    y_d = CX.O("y", [16, 128, D])
    with ExitStack() as es:
        P = CX.prog(es)
        xres = P.sb("xres", [128, 16, D], F32)
        g_b = P.sb("g_b", [128, D], F32)
        ident = P.sb("ident_s", [128, 128], BF16)
        cw = P.sb("cw_s", [128, 44, 3], F32)
        cb = P.sb("cb_s", [128, 44], F32)
        wdn = P.sb("wdn_s", [128, 22, D], BF16)
        ssq = P.sb("ssq", [128, 20], F32)
        rstd = P.sb("rstd", [128, 20], F32)
        rtmp = P.sb("rtmp", [128, 20], F32)
        xh = P.sb("xh_s", [32, D], F32)
        junk = P.sb("junk", [128, D], BF16)
        xn = [P.sb(f"xn{i}", [128, D], BF16) for i in range(2)]
        xnT = P.sb("xnT", [128, 8, 2, 260], BF16)
        wv = [P.sb(f"wv{i}", [128, 8, 256], BF16) for i in range(2)]
        wg = [P.sb(f"wg{i}", [128, 8, 256], BF16) for i in range(2)]
        Uv = P.sb("Uv", [128, 4, 130], F32)
        Ug = P.sb("Ug", [128, 4, 130], F32)
        tv = [P.sb(f"tv{i}", [128, 4, 128], F32) for i in range(2)]
        tg = [P.sb(f"tg{i}", [128, 4, 128], F32) for i in range(2)]
        sg = P.sb("sg", [128, 4, 128], F32)
        aT = P.sb("aT", [128, 22, 512], BF16)
        pT = [P.ps(f"pT{i}", [128, 8, 128], BF16) for i in range(2)]
        pU = [P.ps(f"pU{i}", [128, 512], F32) for i in range(4)]
        pD = [P.ps(f"pD{i}", [128, 512], F32) for i in range(2)]

        P.dma("sp", "c0", [(g_b[:], g_d[:, :]), (ident[:], id_d[:, :]), (cw[:], cw_d[:, :, :]),
                           (cb[:], cb_d[:, :]), (xh[:], xh_d[:, :])], w=["g_b", "ident", "cw", "cb", "xh"])
        for t in range(16):
            P.dma("sp", f"x{t % 4}", [(xres[:, t, :], x_d[t, :, :])], w=[f"x{t}"])
        P.dma("pool", "wdn", [(wdn[:, c * 11:(c + 1) * 11, :],
                               wdn_d[c * 11 * 128:(c + 1) * 11 * 128, :].rearrange("(c p) n -> p c n", p=128))
                              for c in range(2)], w=["wdn"])

        make_eps(P)
        P.op("pool", lambda e: e.memset(ssq[:], 1.0), w=["ssq"])
        for t in range(16):
            P.op("act", lambda e, t=t: e.activation(out=junk[:], in_=xres[:, t, :], func=AF.Square,
                                                    accum_out=ssq[:, t:t + 1]), r=[f"x{t}"], w=["junk", "ssq"])
        P.op("act", lambda e: e.activation(out=junk[0:32, :], in_=xh[:], func=AF.Square,
                                           accum_out=ssq[0:32, 16:17]), r=["xh"], w=["junk", "ssq"])
        emit_rstd(P, ssq, rstd, 17, "ssq", "rstd", rtmp)

        wcount = [0]

        def load_w(b):
            i = wcount[0] % 2
            wcount[0] += 1
            P.dma("pool", f"wup{i}", [
                (wv[i][:], wup_d[:, b * 256:(b + 1) * 256].rearrange("(k p) n -> p k n", p=128)),
                (wg[i][:], wup_d[:, DFF + b * 256:DFF + (b + 1) * 256].rearrange("(k p) n -> p k n", p=128)),
            ], w=[f"wup{i}"])
            return i

        nT = [0]
        for q in range(4):
            for tt in range(4):
                t = q * 4 + tt
                xb = xn[t % 2]
                P.op("dve", lambda e, t=t, xb=xb: e.scalar_tensor_tensor(
                    out=xb[:], in0=xres[:, t, :], scalar=rstd[:, t:t + 1], in1=g_b[:],
                    op0=ALU.mult, op1=ALU.mult), r=[f"x{t}", "rstd", "g_b"], w=[f"xn{t % 2}"])
                pt = pT[nT[0] % 2]
                pk = f"pT{nT[0] % 2}"
                nT[0] += 1
                for k in range(8):
                    P.op("pe", lambda e, k=k, pt=pt, xb=xb: e.transpose(pt[:, k, :], xb[:, k * 128:(k + 1) * 128],
                                                                       ident[:]),
                         r=[f"xn{t % 2}", "ident"], w=[pk])
                hh, cc = tt // 2, (tt % 2) * 128
                P.op("act", lambda e, pt=pt, hh=hh, cc=cc: e.copy(out=xnT[:, :, hh, cc:cc + 128], in_=pt[:, :, :]),
                     r=[pk], w=["xnT"])
            xb = xn[0]
            P.op("dve", lambda e, xb=xb: e.scalar_tensor_tensor(
                out=xb[0:32, :], in0=xh[:], scalar=rstd[0:32, 16:17], in1=g_b[0:32, :],
                op0=ALU.mult, op1=ALU.mult), r=["xh", "rstd", "g_b"], w=["xn0"])
            pt = pT[nT[0] % 2]
            pk = f"pT{nT[0] % 2}"
            nT[0] += 1
            for k in range(8):
                P.op("pe", lambda e, k=k, pt=pt, xb=xb: e.transpose(pt[:, k, 0:32], xb[0:32, k * 128:(k + 1) * 128],
                                                                   ident[0:32, 0:32]),
                     r=["xn0", "ident"], w=[pk])
            for hh in range(2):
                P.op("act", lambda e, pt=pt, hh=hh: e.copy(out=xnT[:, :, hh, 256:260],
                                                           in_=pt[:, :, 8 * q + 4 * hh:8 * q + 4 * hh + 4]),
                     r=[pk], w=["xnT"])

            for b in range(11):
                wi = load_w(b)
                for sub in range(2):
                    c = 2 * b + sub
                    for (wt, U, nm, pbase) in ((wv[wi], Uv, "Uv", 0), (wg[wi], Ug, "Ug", 2)):
                        for hh in range(2):
                            pu = pU[pbase + hh]
                            puk = f"pU{pbase + hh}"
                            for k in range(8):
                                P.op("pe", lambda e, pu=pu, wt=wt, k=k, hh=hh, sub=sub: e.matmul(
                                    pu[:, 0:260], wt[:, k, sub * 128:(sub + 1) * 128], xnT[:, k, hh, :],
                                    start=(k == 0), stop=(k == 7)),
                                     r=[f"wup{wi}", "xnT"], w=[puk])
                            P.op("act", lambda e, pu=pu, U=U, hh=hh: e.copy(
                                out=U[:, 2 * hh:2 * hh + 2, 2:130],
                                in_=pu[:, 0:256].rearrange("p (a b) -> p a b", a=2)), r=[puk], w=[nm])
                            P.op("dve", lambda e, pu=pu, U=U, hh=hh: e.tensor_copy(
                                out=U[:, 2 * hh:2 * hh + 2, 0:2],
                                in_=pu[:, 256:260].rearrange("p (a b) -> p a b", a=2)), r=[puk], w=[nm])
                    for (eng, U, nm, tb, tn, cidx) in (("dve", Uv, "Uv", tv, "tv", c), ("dve", Ug, "Ug", tg, "tg", 22 + c)):
                        P.op(eng, lambda e, U=U, tb=tb, cidx=cidx: e.tensor_scalar(
                            out=tb[0][:], in0=U[:, :, 0:128], scalar1=cw[:, cidx, 0:1], scalar2=cb[:, cidx:cidx + 1],
                            op0=ALU.mult, op1=ALU.add), r=[nm, "cw", "cb"], w=[tn + "0"])
                        P.op(eng, lambda e, U=U, tb=tb, cidx=cidx: e.tensor_scalar(
                            out=tb[1][:], in0=U[:, :, 1:129], scalar1=cw[:, cidx, 1:2], scalar2=None,
                            op0=ALU.mult), r=[nm, "cw"], w=[tn + "1"])
                        P.op(eng, lambda e, tb=tb: e.tensor_tensor(out=tb[0][:], in0=tb[0][:], in1=tb[1][:], op=ALU.add),
                             r=[tn + "0", tn + "1"], w=[tn + "0"])
                        P.op(eng, lambda e, U=U, tb=tb, cidx=cidx: e.tensor_scalar(
                            out=tb[1][:], in0=U[:, :, 2:130], scalar1=cw[:, cidx, 2:3], scalar2=None,
                            op0=ALU.mult), r=[nm, "cw"], w=[tn + "1"])
                        P.op(eng, lambda e, tb=tb: e.tensor_tensor(out=tb[0][:], in0=tb[0][:], in1=tb[1][:], op=ALU.add),
                             r=[tn + "0", tn + "1"], w=[tn + "0"])
                    P.op("act", lambda e: e.activation(out=sg[:], in_=tg[0][:], func=AF.Silu), r=["tg0"], w=["sg"])
                    P.op("dve", lambda e, c=c: e.tensor_tensor(
                        out=aT[:, c, :].rearrange("p (a b) -> p a b", a=4), in0=sg[:], in1=tv[0][:], op=ALU.mult),
                        r=["sg", "tv0"], w=["aT"])
            for tt in range(4):
                t = q * 4 + tt
                for cg in range(2):
                    pd = pD[cg]
                    for c in range(22):
                        P.op("pe", lambda e, pd=pd, c=c, tt=tt, cg=cg: e.matmul(
                            pd[:], aT[:, c, tt * 128:(tt + 1) * 128], wdn[:, c, cg * 512:(cg + 1) * 512],
                            start=(c == 0), stop=(c == 21)), r=["aT", "wdn"], w=[f"pD{cg}"])
                    P.op("dve", lambda e, pd=pd, t=t, cg=cg: e.tensor_tensor(
                        out=xres[:, t, cg * 512:(cg + 1) * 512], in0=xres[:, t, cg * 512:(cg + 1) * 512], in1=pd[:],
                        op=ALU.add), r=[f"pD{cg}", f"x{t}"], w=[f"x{t}"])
                P.dma("sp", f"y{t % 4}", [(y_d[t, :, :], xres[:, t, :])], r=[f"x{t}"], w=[f"yo{t}"])
        CX.done(P, [f"yo{t}" for t in range(16)])
        print("ffn instructions:", P.n_ins)
    return nc


def run_ffn(nc, xt, xh, g, wup, cw, cb, wdn):
    cwl = np.ascontiguousarray(cw.reshape(3, 44, 128).transpose(2, 1, 0))
    cbl = np.ascontiguousarray(cb.reshape(44, 128).T)
    gb = np.ascontiguousarray(np.broadcast_to(g[None, :], (128, D)))
    maps = [{"x": xt[c], "xh": xh[c], "g": gb, "wup": wup, "cw": cwl, "cb": cbl, "wdn": wdn, "ident": IDENT_BF}
            for c in range(NCORES)]
    res = run_bass_kernel_spmd(nc, maps, core_ids=list(range(NCORES)))
    return [r["y"] for r in res.results]


def halos_from(hfull, tile_ids):
    out = np.zeros((len(tile_ids) * 2, D), np.float32)
    for i, j in enumerate(tile_ids):
        if j > 0:
            out[2 * i:2 * i + 2] = hfull[j * 128 - 2:j * 128]
    return out


MIN_ = 3080


def build_mlstm(state_only, ctx=None):
    CX = ctx if ctx is not None else Ctx()
    nc = CX.nc
    x_d = CX.I("x", [16, 128, D])
    g_d = CX.I("g", [128, D])
    win_d = CX.I("win", [D, MIN_])
    bg_d = CX.I("bg", [128, 8])
    id_d = CX.I("ident", [128, 128], BF16)
    tri_d = CX.I("tri", [128, 128])
    if state_only:
        cst_o = CX.O("cst", [128, 4 * 257])
        fst_o = CX.O("fst", [128, 4])
    else:
        gh_d = CX.I("gh", [128, D])
        wout_d = CX.I("wout", [D, D])
        cst_d = CX.I("cst_in", [8, 128, 4 * 257])
        fst_d = CX.I("fst_in", [128, 8, 4])
        cm_d = CX.I("cmask", [128, 8])
        y_d = CX.O("y", [16, 128, D])
    with ExitStack() as es:
        P = CX.prog(es)
        make_eps(P)
        g_b = P.sb("g_b", [128, D], F32)
        ident = P.sb("ident_s", [128, 128], BF16)
        tri = P.sb("tri_s", [128, 128], F32)
        ones = P.sb("ones_s", [128, 128], F32)
        bg = P.sb("bg_s", [128, 8], F32)
        win = P.sb("win_s", [128, 8, MIN_], BF16)
        xin = [P.sb(f"xin{i}", [128, D], F32) for i in range(2)]
        junk = P.sb("junk", [128, D], BF16)
        ssq = P.sb("ssq", [128, 4], F32)
        rstd = P.sb("rstd", [128, 4], F32)
        rtmp = P.sb("rtmp", [128, 4], F32)
        xn = [P.sb(f"xn{i}", [128, D], BF16) for i in range(2)]
        xnT = P.sb("xnT", [128, 8, 512], BF16)
        qkT = P.sb("qkT", [128, 8, 512], BF16)
        ktok = P.sb("ktok", [128, 4, 512], BF16)
        vaug = P.sb("vaug", [128, 4, 4, 258], BF16)
        gts = P.sb("gts", [128, 4, 8], F32)
        C = P.sb("C", [128, 4, 257], F32)
        Cb = P.sb("Cb", [128, 4, 258], BF16)
        Facc = P.sb("Facc", [128, 4], F32)
        ig = P.sb("ig", [128, 4], F32)
        fp = P.sb("fp", [128, 4], F32)
        lf = P.sb("lf", [128, 4], F32)
        lfb = P.sb("lfb", [128, 4, 128], F32)
        bcs = P.sb("bcs", [128, 4], F32)
        av = P.sb("av", [128, 4], F32)
        eW = P.sb("eW", [128, 4], F32)
        eB = P.sb("eB", [128, 4], F32)
        E1 = P.sb("E1", [128, 128], F32)
        DT = P.sb("DT", [128, 128], F32)
        DTm = P.sb("DTm", [128, 128], F32)
        Sp = P.sb("Sp", [128, 128], BF16)
        qs = P.sb("qs", [128, 128], BF16)
        wk = P.sb("wk", [128, 128], BF16)
        pT = P.ps("pT", [128, 8, 128], BF16)
        pA = P.ps("pA", [128, 1024], F32)
        pB = P.ps("pB", [128, 512], F32)
        pM = P.ps("pM", [128, 512], F32)
        pPB = pM[:, 0:128]
        pG = pM[:, 128:144]
        pSd = P.ps("pSd", [128, 512], F32)
        pS = pSd[:, 0:128]
        pdC = pSd[:, 128:385]
        pOut = [P.ps(f"pOut{i}", [128, 512], F32) for i in range(2)]
        if not state_only:
            gh_b = P.sb("gh_b", [128, D], F32)
            wout = P.sb("wout_s", [128, 8, D], BF16)
            so = P.sb("so", [128, 4, D], BF16)
            gso = P.sb("gso", [128, D], F32)
            hg = P.sb("hg", [128, D], BF16)
            hgT = P.sb("hgT", [128, 8, 128], BF16)
            cstb = [P.sb(f"cstb{i}", [128, 4, 257], F32) for i in range(2)]
            fstb = P.sb("fstb", [128, 8, 4], F32)
            cm = P.sb("cm_s", [128, 8], F32)
            dec = P.sb("dec", [128, 4], F32)
            den = P.sb("den", [128, 4], F32)
            ssqh = P.sb("ssqh", [128, 4], F32)
            rsh = P.sb("rsh", [128, 4], F32)
            rth = P.sb("rth", [128, 4], F32)
            sc = P.sb("sc", [128, 4], F32)
            xres = [P.sb(f"xr{i}", [128, D], F32) for i in range(2)]

        P.dma("sp", "c0", [(g_b[:], g_d[:, :]), (ident[:], id_d[:, :]), (tri[:], tri_d[:, :]), (bg[:], bg_d[:, :])],
              w=["g_b", "ident", "tri", "bg"])
        P.op("pool", lambda e: e.memset(ones[:], 1.0), w=["ones"])
        P.op("pool", lambda e: e.memset(vaug[:], 1.0), w=["vaug"])
        P.op("pool", lambda e: e.memset(C[:], 0.0), w=["C"])
        P.op("pool", lambda e: e.memset(Cb[:], 0.0), w=["Cb"])
        P.op("pool", lambda e: e.memset(Facc[:], 0.0), w=["Facc"])
        for k in range(8):
            P.dma("pool", "win", [(win[:, k, :], win_d[k * 128:(k + 1) * 128, :])], w=[f"win{k}"])
        WIN = [f"win{k}" for k in range(8)]
        if not state_only:
            P.dma("sp", "c1", [(gh_b[:], gh_d[:, :]), (fstb[:], fst_d[:, :, :]), (cm[:], cm_d[:, :])],
                  w=["gh_b", "fstb", "cm"])
            P.dma("pool", "wout", [(wout[:], wout_d.rearrange("(k p) n -> p k n", p=128))], w=["wout"])
            for cp in range(8):
                cb_ = cstb[cp % 2]
                ck = f"cstb{cp % 2}"
                P.dma("sp", ck, [(cb_[:], cst_d[cp, :, :].rearrange("p (h e) -> p h e", h=4))], w=[ck])
                P.op("act", lambda e, cp=cp: e.activation(out=dec[:], in_=fstb[:, cp, :], func=AF.Exp), r=["fstb"], w=["dec"])
                P.op("dve", lambda e, cp=cp: e.tensor_scalar(out=dec[:], in0=dec[:], scalar1=-1.0, scalar2=cm[:, cp:cp + 1],
                                                             op0=ALU.add, op1=ALU.mult), r=["dec", "cm"], w=["dec"])
                P.op("dve", lambda e: e.tensor_scalar(out=dec[:], in0=dec[:], scalar1=1.0, scalar2=None, op0=ALU.add),
                     r=["dec"], w=["dec"])
                P.op("dve", lambda e, cp=cp, cb_=cb_: e.tensor_scalar(out=cb_[:], in0=cb_[:], scalar1=cm[:, cp:cp + 1], scalar2=None,
                                                                     op0=ALU.mult), r=[ck, "cm"], w=[ck])
                for h in range(4):
                    P.op("dve", lambda e, h=h, cb_=cb_: e.scalar_tensor_tensor(
                        out=C[:, h, :], in0=C[:, h, :], scalar=dec[:, h:h + 1], in1=cb_[:, h, :],
                        op0=ALU.mult, op1=ALU.add), r=["C", "dec", ck], w=["C"])
            P.op("act", lambda e: e.copy(out=Cb[:, :, 0:257], in_=C[:, :, :]), r=["C"], w=["Cb"])

        def proj_fm(gi_, col0, scale):
            for k in range(8):
                P.op("pe", lambda e, k=k: e.matmul(pB[:], win[:, k, col0:col0 + 128], xnT[:, k, :],
                                                  start=(k == 0), stop=(k == 7)), r=[f"win{k}", "xnT"], w=["pB"])
            if scale == 1.0:
                P.op("act", lambda e: e.copy(out=qkT[:, gi_, :], in_=pB[:]), r=["pB"], w=["qkT"])
            else:
                P.op("act", lambda e: e.mul(out=qkT[:, gi_, :], in_=pB[:], mul=scale), r=["pB"], w=["qkT"])

        for grp in range(4):
            for tt in range(4):
                t = grp * 4 + tt
                xi = xin[t % 2]
                xk = f"xin{t % 2}"
                P.dma("sp", xk, [(xi[:], x_d[t, :, :])], w=[xk])
                P.op("act", lambda e, xi=xi, tt=tt: e.activation(out=junk[:], in_=xi[:], func=AF.Square,
                                                                 accum_out=ssq[:, tt:tt + 1]), r=[xk], w=["junk", "ssq"])
                emit_rstd(P, ssq[:, tt:tt + 1], rstd[:, tt:tt + 1], 1, "ssq", "rstd", rtmp[:, tt:tt + 1])
                xb = xn[t % 2]
                P.op("dve", lambda e, xi=xi, xb=xb, tt=tt: e.scalar_tensor_tensor(
                    out=xb[:], in0=xi[:], scalar=rstd[:, tt:tt + 1], in1=g_b[:], op0=ALU.mult, op1=ALU.mult),
                    r=[xk, "rstd", "g_b"], w=[f"xn{t % 2}"])
                for k in range(8):
                    P.op("pe", lambda e, k=k, xb=xb: e.transpose(pT[:, k, :], xb[:, k * 128:(k + 1) * 128], ident[:]),
                         r=[f"xn{t % 2}", "ident"], w=["pT"])
                P.op("act", lambda e, tt=tt: e.copy(out=xnT[:, :, tt * 128:(tt + 1) * 128], in_=pT[:, :, :]),
                     r=["pT"], w=["xnT"])
            if not state_only:
                for h in range(4):
                    proj_fm(h, h * 128, 128.0 ** -0.5)
                for h in range(4):
                    proj_fm(4 + h, 512 + h * 128, 1.0)
            for tt in range(4):
                lhs = lambda k, tt=tt: xnT[:, k, tt * 128:(tt + 1) * 128]
                for k in range(8):
                    P.op("pe", lambda e, k=k: e.matmul(pB[:], lhs(k), win[:, k, 512:1024], start=(k == 0), stop=(k == 7)),
                         r=[f"win{k}", "xnT"], w=["pB"])
                P.op("act", lambda e, tt=tt: e.copy(out=ktok[:, tt, :], in_=pB[:]), r=["pB"], w=["ktok"])
                for cg in range(2):
                    for k in range(8):
                        P.op("pe", lambda e, k=k, cg=cg: e.matmul(pA[:, cg * 512:(cg + 1) * 512], lhs(k),
                                                                 win[:, k, 1024 + cg * 512:1536 + cg * 512],
                                                                 start=(k == 0), stop=(k == 7)), r=[f"win{k}", "xnT"], w=["pA"])
                P.op("act", lambda e, tt=tt: e.copy(out=vaug[:, tt, :, 0:256], in_=pA[:].rearrange("p (h e) -> p h e", h=4)),
                     r=["pA"], w=["vaug"])
                if not state_only:
                    for cg in range(2):
                        for k in range(8):
                            P.op("pe", lambda e, k=k, cg=cg: e.matmul(pA[:, cg * 512:(cg + 1) * 512], lhs(k),
                                                                     win[:, k, 2048 + cg * 512:2560 + cg * 512],
                                                                     start=(k == 0), stop=(k == 7)), r=[f"win{k}", "xnT"], w=["pA"])
                    P.op("act", lambda e, tt=tt: e.activation(out=so[:, tt, :], in_=pA[:], func=AF.Sigmoid), r=["pA"], w=["so"])
                for k in range(8):
                    P.op("pe", lambda e, k=k: e.matmul(pG[:, 0:8], lhs(k), win[:, k, 3072:3080], start=(k == 0), stop=(k == 7)),
                         r=[f"win{k}", "xnT"], w=["pM"])
                P.op("dve", lambda e, tt=tt: e.tensor_tensor(out=gts[:, tt, :], in0=pG[:, 0:8], in1=bg[:], op=ALU.add),
                     r=["pM", "bg"], w=["gts"])
            for tt in range(4):
                t = grp * 4 + tt
                P.op("act", lambda e, tt=tt: e.activation(out=fp[:], in_=gts[:, tt, 4:8], func=AF.Exp, scale=-1.0), r=["gts"], w=["fp"])
                P.op("act", lambda e: e.activation(out=fp[:], in_=fp[:], func=AF.Ln, scale=1.0, bias=ones[:, 0:1]), r=["fp", "ones"], w=["fp"])
                P.op("dve", lambda e: e.tensor_scalar(out=lf[:], in0=fp[:], scalar1=-1.0, scalar2=None, op0=ALU.mult), r=["fp"], w=["lf"])
                P.op("pe", lambda e: e.matmul(pG[:, 8:12], tri[:], lf[:], start=True, stop=True), r=["tri", "lf"], w=["pM"])
                P.op("pe", lambda e: e.matmul(pG[:, 12:16], ones[:], lf[:], start=True, stop=True), r=["ones", "lf"], w=["pM"])
                P.op("dve", lambda e, tt=tt: e.tensor_tensor(out=av[:], in0=gts[:, tt, 0:4], in1=pG[:, 8:12], op=ALU.subtract),
                     r=["gts", "pM"], w=["av"])
                P.op("dve", lambda e: e.tensor_tensor(out=eW[:], in0=av[:], in1=pG[:, 12:16], op=ALU.add), r=["av", "pM"], w=["eW"])
                P.op("act", lambda e: e.activation(out=eW[:], in_=eW[:], func=AF.Exp), r=["eW"], w=["eW"])
                P.op("act", lambda e: e.activation(out=eB[:], in_=pG[:, 12:16], func=AF.Exp), r=["pM"], w=["eB"])
                P.op("dve", lambda e: e.tensor_tensor(out=Facc[:], in0=Facc[:], in1=pG[:, 12:16], op=ALU.add), r=["Facc", "pM"], w=["Facc"])
                if not state_only:
                    P.op("dve", lambda e: e.tensor_copy(out=lfb[:], in_=lf[:, :].unsqueeze(2).to_broadcast([128, 4, 128])),
                         r=["lf"], w=["lfb"])
                    P.op("pool", lambda e, tt=tt: e.tensor_tensor(out=gso[:], in0=so[:, tt, :], in1=gh_b[:], op=ALU.mult),
                         r=["so", "gh_b"], w=["gso"])
                for h in range(4):
                    cs = slice(tt * 128, (tt + 1) * 128)
                    if not state_only:
                        P.op("pe", lambda e, h=h: e.matmul(pPB[:], lfb[:, h, :], tri[:], start=True, stop=True),
                             r=["lfb", "tri"], w=["pM"])
                        P.op("act", lambda e: e.activation(out=E1[:], in_=pPB[:], func=AF.Exp), r=["pM"], w=["E1"])
                        P.op("act", lambda e, h=h: e.activation(out=DT[:], in_=pPB[:], func=AF.Exp, bias=av[:, h:h + 1]),
                             r=["pM", "av"], w=["DT"])
                        P.op("pool", lambda e: e.tensor_tensor(out=DTm[:], in0=DT[:], in1=tri[:], op=ALU.mult), r=["DT", "tri"], w=["DTm"])
                        P.op("pool", lambda e, h=h, cs=cs: e.tensor_tensor(out=qs[:], in0=qkT[:, h, cs], in1=E1[:], op=ALU.mult),
                             r=["qkT", "E1"], w=["qs"])
                        P.op("pe", lambda e, h=h, cs=cs: e.matmul(pS[:], qkT[:, 4 + h, cs], qkT[:, h, cs], start=True, stop=True),
                             r=["qkT"], w=["pSd"])
                        P.op("dve", lambda e: e.tensor_tensor(out=Sp[:], in0=pS[:], in1=DTm[:], op=ALU.mult), r=["pSd", "DTm"], w=["Sp"])
                        po = pOut[h % 2]
                        pk = f"pOut{h % 2}"
                        P.op("pe", lambda e, h=h, po=po: e.matmul(po[:, 0:257], qs[:], Cb[:, h, 0:257], start=True, stop=False),
                             r=["qs", "Cb"], w=[pk])
                        P.op("pe", lambda e, h=h, po=po, tt=tt: e.matmul(po[:, 0:257], Sp[:], vaug[:, tt, h, 0:257], start=False, stop=True),
                             r=["Sp", "vaug"], w=[pk])
                    P.op("pool", lambda e, h=h, tt=tt: e.tensor_scalar(out=wk[:], in0=ktok[:, tt, h * 128:(h + 1) * 128],
                                                                       scalar1=eW[:, h:h + 1], scalar2=None, op0=ALU.mult),
                         r=["ktok", "eW"], w=["wk"])
                    P.op("pe", lambda e, h=h, tt=tt: e.matmul(pdC, wk[:], vaug[:, tt, h, 0:257], start=True, stop=True),
                         r=["wk", "vaug"], w=["pSd"])
                    P.op("dve", lambda e, h=h: e.scalar_tensor_tensor(out=C[:, h, :], in0=C[:, h, :], scalar=eB[:, h:h + 1],
                                                                      in1=pdC, op0=ALU.mult, op1=ALU.add),
                         r=["C", "eB", "pSd"], w=["C"])
                    if not state_only:
                        P.op("act", lambda e, h=h: e.copy(out=Cb[:, h, 0:257], in_=C[:, h, :]), r=["C"], w=["Cb"])
                        P.op("act", lambda e, h=h, po=po: e.activation(out=den[:, h:h + 1], in_=po[:, 256:257], func=AF.Abs), r=[pk], w=["den"])
                        P.op("dve", lambda e, h=h: e.tensor_scalar(out=den[:, h:h + 1], in0=den[:, h:h + 1], scalar1=1.0, scalar2=None,
                                                                   op0=ALU.max), r=["den"], w=["den"])
                        P.op("dve", lambda e, h=h: e.reciprocal(out=den[:, h:h + 1], in_=den[:, h:h + 1]), r=["den"], w=["den"])
                        P.op("act", lambda e, h=h, po=po: e.activation(out=junk[:, 0:256], in_=po[:, 0:256], func=AF.Square,
                                                                       scale=den[:, h:h + 1], accum_out=ssqh[:, h:h + 1]),
                             r=[pk, "den"], w=["junk", "ssqh"])
                        emit_rstd(P, ssqh[:, h:h + 1], rsh[:, h:h + 1], 1, "ssqh", "rsh", rth[:, h:h + 1], dim=256)
                        P.op("dve", lambda e, h=h: e.tensor_tensor(out=sc[:, h:h + 1], in0=den[:, h:h + 1], in1=rsh[:, h:h + 1], op=ALU.mult),
                             r=["den", "rsh"], w=["sc"])
                        P.op("dve", lambda e, h=h, po=po: e.scalar_tensor_tensor(
                            out=hg[:, h * 256:(h + 1) * 256], in0=po[:, 0:256], scalar=sc[:, h:h + 1],
                            in1=gso[:, h * 256:(h + 1) * 256], op0=ALU.mult, op1=ALU.mult), r=[pk, "sc", "gso"], w=["hg"])
                if not state_only:
                    for k in range(8):
                        P.op("pe", lambda e, k=k: e.transpose(pT[:, k, :], hg[:, k * 128:(k + 1) * 128], ident[:]),
                             r=["hg", "ident"], w=["pT"])
                    P.op("act", lambda e: e.copy(out=hgT[:], in_=pT[:, :, :]), r=["pT"], w=["hgT"])
                    xr = xres[t % 2]
                    xrk = f"xr{t % 2}"
                    P.dma("sp", xrk, [(xr[:], x_d[t, :, :])], w=[xrk])
                    for cg in range(2):
                        for k in range(8):
                            P.op("pe", lambda e, k=k, cg=cg: e.matmul(pA[:, cg * 512:(cg + 1) * 512], hgT[:, k, :],
                                                                     wout[:, k, cg * 512:(cg + 1) * 512],
                                                                     start=(k == 0), stop=(k == 7)), r=["hgT", "wout"], w=["pA"])
                    P.op("dve", lambda e, xr=xr: e.tensor_tensor(out=xr[:], in0=xr[:], in1=pA[:], op=ALU.add), r=[xrk, "pA"], w=[xrk])
                    P.dma("sp", f"y{t % 2}", [(y_d[t, :, :], xr[:])], r=[xrk], w=[f"yo{t}"])
        if state_only:
            P.dma("sp", "so", [(cst_o[:, :].rearrange("p (h e) -> p h e", h=4), C[:]), (fst_o[:, :], Facc[:])],
                  r=["C", "Facc"], w=["outs"])
            CX.done(P, ["outs"])
        else:
            CX.done(P, [f"yo{t}" for t in range(16)])
        print("mlstm instructions:", P.n_ins, "state_only", state_only)
    return nc


TRI = np.triu(np.ones((128, 128), np.float32))


def mlstm_common_inputs(xt, g, win, bgates):
    gb = np.ascontiguousarray(np.broadcast_to(g[None, :], (128, D)))
    bgb = np.ascontiguousarray(np.broadcast_to(bgates.reshape(1, 8), (128, 8)))
    return {"x": xt, "g": gb, "win": win, "bg": bgb, "ident": IDENT_BF, "tri": TRI}


def run_mlstm_state(nc, xts, g, win, bgates):
    maps = [mlstm_common_inputs(xts[c], g, win, bgates) for c in range(NCORES)]
    res = run_bass_kernel_spmd(nc, maps, core_ids=list(range(NCORES)))
    return [r["cst"] for r in res.results], [r["fst"] for r in res.results]


def run_mlstm_full(nc, xts, g, win, bgates, gh, wout, csts, fsts):
    cst_all = np.ascontiguousarray(np.stack(csts, 0))
    fst_all = np.ascontiguousarray(np.stack(fsts, 1))
    ghb = np.ascontiguousarray(np.broadcast_to(gh[None, :], (128, D)))
    maps = []
    for c in range(NCORES):
        m = mlstm_common_inputs(xts[c], g, win, bgates)
        cmask = np.zeros((128, 8), np.float32)
        cmask[:, :c] = 1.0
        m.update({"gh": ghb, "wout": wout, "cst_in": cst_all, "fst_in": fst_all, "cmask": cmask})
        maps.append(m)
    res = run_bass_kernel_spmd(nc, maps, core_ids=list(range(NCORES)))
    return [r["y"] for r in res.results]


def rope_tables_np():
    pos = np.arange(S, dtype=np.float32)
    inv = (np.float32(500000.0) ** (-(np.arange(0, 32, 2, dtype=np.float32) / np.float32(32)))).astype(np.float32)
    ang = (pos[:, None] * inv[None, :]).astype(np.float32)
    return np.cos(ang).astype(np.float32), np.sin(ang).astype(np.float32)


class NormT:
    def __init__(self, P, pT, ident, g_b):
        self.P, self.pT, self.ident, self.g_b = P, pT, ident, g_b
        self.xin = [P.sb(f"nt_xin{i}", [128, D], F32) for i in range(2)]
        self.xn = [P.sb(f"nt_xn{i}", [128, D], BF16) for i in range(2)]
        self.junk = P.sb("nt_junk", [128, D], BF16)
        self.ssq = P.sb("nt_ssq", [128, 2], F32)
        self.rstd = P.sb("nt_rstd", [128, 2], F32)
        self.rtmp = P.sb("nt_rtmp", [128, 2], F32)
        self.xnT = P.sb("nt_xnT", [128, 8, 128], BF16)
        self.n = 0

    def emit(self, x_ap):
        P = self.P
        i = self.n % 2
        self.n += 1
        xi, xb = self.xin[i], self.xn[i]
        xk, bk = f"nt_xin{i}", f"nt_xn{i}"
        P.dma("sp", xk, [(xi[:], x_ap)], w=[xk])
        P.op("act", lambda e: e.activation(out=self.junk[:], in_=xi[:], func=AF.Square, accum_out=self.ssq[:, i:i + 1]),
             r=[xk], w=["nt_junk", f"nt_ssq{i}"])
        emit_rstd(P, self.ssq[:, i:i + 1], self.rstd[:, i:i + 1], 1, f"nt_ssq{i}", f"nt_rstd{i}", self.rtmp[:, i:i + 1])
        P.op("dve", lambda e: e.scalar_tensor_tensor(out=xb[:], in0=xi[:], scalar=self.rstd[:, i:i + 1], in1=self.g_b[:],
                                                     op0=ALU.mult, op1=ALU.mult), r=[xk, f"nt_rstd{i}", "g_b"], w=[bk])
        for k in range(8):
            P.op("pe", lambda e, k=k: e.transpose(self.pT[:, k, :], xb[:, k * 128:(k + 1) * 128], self.ident[:]),
                 r=[bk, "ident"], w=["pT"])
        P.op("act", lambda e: e.copy(out=self.xnT[:], in_=self.pT[:, :, :]), r=["pT"], w=["nt_xnT"])
        return self.xnT


class HeadNormRope:
    def __init__(self, P, gh_b, cos, sin, scale):
        self.P, self.gh_b, self.cos, self.sin, self.scale = P, gh_b, cos, sin, scale
        self.sq = P.sb("hn_sq", [128, D], F32)
        self.ssq = P.sb("hn_ssq", [128, 8], F32)
        self.rstd = P.sb("hn_rstd", [128, 8], F32)
        self.rtmp = P.sb("hn_rtmp", [128, 8], F32)
        self.rt = P.sb("hn_rt", [128, 4, 8, 16], F32)

    def emit(self, ps, pkeys, kf, kfkey, t):
        P = self.P
        P.op("act", lambda e: e.activation(out=self.sq[:], in_=ps, func=AF.Square), r=pkeys, w=["hn_sq"])
        P.op("dve", lambda e: e.tensor_reduce(out=self.ssq[:], in_=self.sq[:].rearrange("p (h e) -> p h e", h=8),
                                              axis=AX.X, op=ALU.add), r=["hn_sq"], w=["hn_ssq"])
        emit_rstd(P, self.ssq, self.rstd, 8, "hn_ssq", "hn_rstd", self.rtmp, dim=128)
        kf3 = kf.rearrange("p (h e) -> p h e", h=8)
        P.op("dve", lambda e: e.scalar_tensor_tensor(out=kf3, in0=ps.rearrange("p (h e) -> p h e", h=8), scalar=self.scale,
                                                     in1=self.rstd[:, :].unsqueeze(2).to_broadcast([128, 8, 128]),
                                                     op0=ALU.mult, op1=ALU.mult),
             r=pkeys + ["hn_rstd"], w=[kfkey])
        P.op("pool", lambda e: e.tensor_tensor(out=kf3, in0=kf3, in1=self.gh_b[:, :].unsqueeze(1).to_broadcast([128, 8, 128]),
                                               op=ALU.mult), r=[kfkey, "gh_b"], w=[kfkey])
        t1, t2 = kf3[:, :, 0:16], kf3[:, :, 16:32]
        cb = self.cos[:, t, :].unsqueeze(1).to_broadcast([128, 8, 16])
        sb_ = self.sin[:, t, :].unsqueeze(1).to_broadcast([128, 8, 16])
        rt = self.rt
        for (j, a, b) in ((0, t1, cb), (1, t2, sb_), (2, t2, cb), (3, t1, sb_)):
            P.op("pool", lambda e, j=j, a=a, b=b: e.tensor_tensor(out=rt[:, j, :, :], in0=a, in1=b, op=ALU.mult),
                 r=[kfkey, "cs"], w=[f"hn_rt{j}"])
        P.op("pool", lambda e: e.tensor_tensor(out=t1, in0=rt[:, 0, :, :], in1=rt[:, 1, :, :], op=ALU.subtract),
             r=["hn_rt0", "hn_rt1"], w=[kfkey])
        P.op("pool", lambda e: e.tensor_tensor(out=t2, in0=rt[:, 2, :, :], in1=rt[:, 3, :, :], op=ALU.add),
             r=["hn_rt2", "hn_rt3"], w=[kfkey])


def build_kv(ctx=None):
    CX = ctx if ctx is not None else Ctx()
    nc = CX.nc
    x_d = CX.I("x", [16, 128, D])
    g_d = CX.I("g", [128, D])
    w_d = CX.I("wkv", [D, 2 * D])
    gk_d = CX.I("gk", [128, 128])
    cos_d = CX.I("cos", [128, 16, 16])
    sin_d = CX.I("sin", [128, 16, 16])
    id_d = CX.I("ident", [128, 128], BF16)
    kt_o = CX.O("kt", [8, 128, 2048], BF16)
    v_o = CX.O("v", [8, 128, 16, 130], BF16)
    km_o = CX.O("km", [128, 8, 8])
    with ExitStack() as es:
        P = CX.prog(es)
        make_eps(P)
        g_b = P.sb("g_b", [128, D], F32)
        gk_b = P.sb("gk_b", [128, 128], F32)
        cos = P.sb("cos_s", [128, 16, 16], F32)
        sin = P.sb("sin_s", [128, 16, 16], F32)
        ident = P.sb("ident_s", [128, 128], BF16)
        onesf = P.sb("onesf", [128, 1], F32)
        w = P.sb("w_s", [128, 8, 2 * D], BF16)
        kTall = P.sb("kTall", [128, 8, 2048], BF16)
        vall = P.sb("vall", [128, 8, 16, 130], BF16)
        kf = [P.sb(f"kf{i}", [128, D], F32) for i in range(2)]
        kb = P.sb("kb", [128, D], BF16)
        kms = P.sb("kms", [128, 8, 8], F32)
        pT = P.ps("pT", [128, 8, 128], BF16)
        pK = P.ps("pK", [128, 1024], F32)
        pV = P.ps("pV", [128, 1024], F32)
        pKM = P.ps("pKM", [128, 64], F32)
        P.dma("sp", "c0", [(g_b[:], g_d[:, :]), (gk_b[:], gk_d[:, :]), (cos[:], cos_d[:, :, :]), (sin[:], sin_d[:, :, :]),
                           (ident[:], id_d[:, :])], w=["g_b", "gh_b", "cs", "ident"])
        P.op("pool", lambda e: e.memset(onesf[:], 1.0), w=["onesf"])
        P.op("pool", lambda e: e.memset(vall[:], 1.0), w=["vall"])
        for k in range(8):
            P.dma("pool", "w", [(w[:, k, :], w_d[k * 128:(k + 1) * 128, :])], w=[f"w{k}"])
        nt = NormT(P, pT, ident, g_b)
        hn = HeadNormRope(P, gk_b, cos, sin, 1.0)
        for t in range(16):
            xnT = nt.emit(x_d[t, :, :])
            for (ps, pk, c0) in ((pK, "pK", 0), (pV, "pV", D)):
                for cg in range(2):
                    for k in range(8):
                        P.op("pe", lambda e, k=k, cg=cg, ps=ps, c0=c0: e.matmul(
                            ps[:, cg * 512:(cg + 1) * 512], xnT[:, k, :], w[:, k, c0 + cg * 512:c0 + (cg + 1) * 512],
                            start=(k == 0), stop=(k == 7)), r=["nt_xnT", f"w{k}"], w=[pk])
            P.op("act", lambda e, t=t: e.copy(out=vall[:, :, t, 0:128], in_=pV[:].rearrange("p (h e) -> p h e", h=8)),
                 r=["pV"], w=["vall"])
            kfi = kf[t % 2]
            kk = f"kf{t % 2}"
            hn.emit(pK[:], ["pK"], kfi[:], kk, t)
            P.op("act", lambda e, kfi=kfi: e.copy(out=kb[:], in_=kfi[:]), r=[kk], w=["kb"])
            for h in range(8):
                P.op("pe", lambda e, h=h: e.transpose(pT[:, h, :], kb[:, h * 128:(h + 1) * 128], ident[:]), r=["kb", "ident"], w=["pT"])
            P.op("act", lambda e, t=t: e.copy(out=kTall[:, :, t * 128:(t + 1) * 128], in_=pT[:, :, :]), r=["pT"], w=["kTall"])
            if t % 2 == 1:
                blk = t // 2
                for h in range(8):
                    for j in range(2):
                        P.op("pe", lambda e, h=h, j=j, blk=blk: e.matmul(
                            pKM[:, h * 8 + blk:h * 8 + blk + 1], kf[j][:, h * 128:(h + 1) * 128], onesf[:],
                            start=(j == 0), stop=(j == 1)), r=[f"kf{j}", "onesf"], w=["pKM"])
        P.op("act", lambda e: e.mul(out=kms[:].rearrange("p h b -> p (h b)"), in_=pKM[:], mul=1.0 / 256.0), r=["pKM"], w=["kms"])
        P.dma("sp", "o0", [(kt_o[h, :, :], kTall[:, h, :]) for h in range(8)], r=["kTall"], w=["o_kt"])
        P.dma("sp", "o1", [(v_o[h, :, :, :], vall[:, h, :, :]) for h in range(8)], r=["vall"], w=["o_v"])
        P.dma("sp", "o2", [(km_o[:, :, :], kms[:])], r=["kms"], w=["o_km"])
        CX.done(P, ["o_kt", "o_v", "o_km"])
        print("kv instructions:", P.n_ins)
    return nc


def run_kv(nc, xts, g, wkv, gk, cos_t, sin_t, tile_ids):
    gb = np.ascontiguousarray(np.broadcast_to(g[None, :], (128, D)))
    gkb = np.ascontiguousarray(np.broadcast_to(gk[None, :], (128, 128)))
    maps = []
    for c in range(NCORES):
        cs = np.stack([cos_t[j * 128:(j + 1) * 128] for j in tile_ids[c]], 1)
        sn = np.stack([sin_t[j * 128:(j + 1) * 128] for j in tile_ids[c]], 1)
        maps.append({"x": xts[c], "g": gb, "wkv": wkv, "gk": gkb, "cos": np.ascontiguousarray(cs),
                     "sin": np.ascontiguousarray(sn), "ident": IDENT_BF})
    res = run_bass_kernel_spmd(nc, maps, core_ids=list(range(NCORES)))
    kt = np.concatenate([r["kt"] for r in res.results], 2)
    v = np.concatenate([r["v"] for r in res.results], 2)
    km = np.concatenate([r["km"] for r in res.results], 2)
    return kt, v, km


def build_moba(ctx=None):
    CX = ctx if ctx is not None else Ctx()
    nc = CX.nc
    x_d = CX.I("x", [16, 128, D])
    g_d = CX.I("g", [128, D])
    wq_d = CX.I("wq", [D, D])
    wo_d = CX.I("wo", [D, D])
    gq_d = CX.I("gq", [128, 128])
    cos_d = CX.I("cos", [128, 16, 16])
    sin_d = CX.I("sin", [128, 16, 16])
    id_d = CX.I("ident", [128, 128], BF16)
    kt_d = CX.I("kt", [8, 8, 128, 2048], BF16)
    v_d = CX.I("v", [8, 8, 128, 16, 130], BF16)
    km_d = CX.I("km", [8, 128, 8, 8])
    gb_d = CX.I("gbias", [128, 16, 64])
    own_d = CX.I("own", [128, 16, 64])
    dg_d = CX.I("diag", [128, 8, 128], BF16)
    y_d = CX.O("y", [16, 128, D])
    with ExitStack() as es:
        P = CX.prog(es)
        make_eps(P)
        g_b = P.sb("g_b", [128, D], F32)
        gq_b = P.sb("gq_b", [128, 128], F32)
        cos = P.sb("cos_s", [128, 16, 16], F32)
        sin = P.sb("sin_s", [128, 16, 16], F32)
        ident = P.sb("ident_s", [128, 128], BF16)
        w = P.sb("w_s", [128, 8, D], BF16)
        gbias = P.sb("gbias_s", [128, 16, 64], F32)
        own = P.sb("own_s", [128, 16, 64], F32)
        diag = P.sb("diag_s", [128, 8, 128], BF16)
        kmf = P.sb("kmf", [128, 8, 64], F32)
        kmb = P.sb("kmb", [128, 8, 64], BF16)
        qT = P.sb("qT", [128, 8, 2048], BF16)
        Kh = P.sb("Kh", [128, S], BF16)
        Vh = P.sb("Vh", [128, 128, 130], BF16)
        Oall = P.sb("Oall", [128, 16, D], BF16)
        sbT = [P.sb(f"sbT{i}", [64, 2048], BF16) for i in range(2)]
        kf = P.sb("kf", [128, D], F32)
        qb = P.sb("qb", [128, D], BF16)
        top8 = P.sb("top8", [128, 16, 8], F32)
        thr = P.sb("thr", [128, 16], F32)
        PT = [P.sb(f"PT{i}", [128, 512], BF16) for i in range(2)]
        rden = P.sb("rden", [128, 4], F32)
        oT = P.sb("oT", [128, 8, 128], BF16)
        pT = P.ps("pT", [128, 8, 128], BF16)
        pQ = P.ps("pQ", [128, 1024], F32)
        pS = [P.ps(f"pS{i}", [128, 512], F32) for i in range(2)]
        pO23 = [P.ps(f"pO{i}", [128, 512], F32) for i in (2, 3)]
        pOb = [pQ[:, 0:512], pQ[:, 512:1024], pO23[0][:], pO23[1][:]]
        pOk = ["pQa", "pQb", "pO2", "pO3"]
        pGt = P.ps("pGt", [128, 512], F32)

        P.dma("sp", "c0", [(g_b[:], g_d[:, :]), (gq_b[:], gq_d[:, :]), (cos[:], cos_d[:, :, :]), (sin[:], sin_d[:, :, :]),
                           (ident[:], id_d[:, :]), (gbias[:], gb_d[:, :, :]), (own[:], own_d[:, :, :]), (diag[:], dg_d[:, :, :])] +
              [(kmf[:, :, r * 8:(r + 1) * 8], km_d[r, :, :, :]) for r in range(8)],
              w=["g_b", "gh_b", "cs", "ident", "gbias", "own", "diag", "kmf"])
        P.op("dve", lambda e: e.tensor_copy(out=kmb[:], in_=kmf[:]), r=["kmf"], w=["kmb"])
        P.dma("pool", "w", [(w[:], wq_d.rearrange("(k p) n -> p k n", p=128))], w=["w"])

        nt = NormT(P, pT, ident, g_b)
        hn = HeadNormRope(P, gq_b, cos, sin, 128.0 ** -0.5)
        for t in range(16):
            xnT = nt.emit(x_d[t, :, :])
            for cg in range(2):
                for k in range(8):
                    P.op("pe", lambda e, k=k, cg=cg: e.matmul(pQ[:, cg * 512:(cg + 1) * 512], xnT[:, k, :],
                                                             w[:, k, cg * 512:(cg + 1) * 512], start=(k == 0), stop=(k == 7)),
                         r=["nt_xnT", "w"], w=["pQa", "pQb"])
            hn.emit(pQ[:], ["pQa", "pQb"], kf[:], "kf", t)
            P.op("act", lambda e: e.copy(out=qb[:], in_=kf[:]), r=["kf"], w=["qb"])
            for h in range(8):
                P.op("pe", lambda e, h=h: e.transpose(pT[:, h, :], qb[:, h * 128:(h + 1) * 128], ident[:]), r=["qb", "ident"], w=["pT"])
            P.op("act", lambda e, t=t: e.copy(out=qT[:, :, t * 128:(t + 1) * 128], in_=pT[:, :, :]), r=["pT"], w=["qT"])
        P.dma("pool", "w", [(w[:], wo_d.rearrange("(k p) n -> p k n", p=128))], r=[], w=["w"])

        for h in range(8):
            for r in range(8):
                P.dma("sp", f"kh{r}", [(Kh[:, r * 2048:(r + 1) * 2048], kt_d[r, h, :, :])], w=[f"Kh{r}"])
                P.dma("sp", f"vh{r}", [(Vh[:, r * 16:(r + 1) * 16, :], v_d[r, h, :, :, :])], w=[f"Vh{r}"])
            sT = sbT[h % 2]
            sk = f"sbT{h % 2}"
            gm_all = kf[:].rearrange("p (i n) -> p i n", i=16)
            selb_all = qb[:].rearrange("p (i n) -> p i n", i=16)
            for half in range(2):
                for ii in range(8):
                    i = half * 8 + ii
                    P.op("pe", lambda e, i=i, ii=ii: e.matmul(pGt[:, ii * 64:(ii + 1) * 64], qT[:, h, i * 128:(i + 1) * 128], kmb[:, h, :],
                                                           start=True, stop=True), r=["qT", "kmb"], w=["pGt"])
                P.op("dve", lambda e, half=half: e.tensor_tensor(
                    out=gm_all[:, half * 8:(half + 1) * 8, :], in0=pGt[:].rearrange("p (i n) -> p i n", i=8),
                    in1=gbias[:, half * 8:(half + 1) * 8, :], op=ALU.add), r=["pGt", "gbias"], w=["kf"])
            for i in range(16):
                P.op("dve", lambda e, i=i: e.max(out=top8[:, i, :], in_=gm_all[:, i, :]), r=["kf"], w=["top8"])
            P.op("dve", lambda e: e.tensor_scalar(out=thr[:, :].unsqueeze(2), in0=top8[:, :, 2:3], scalar1=-1e29, scalar2=None, op0=ALU.max),
                 r=["top8"], w=["thr"])
            P.op("dve", lambda e: e.tensor_tensor(out=gm_all, in0=gm_all, in1=thr[:, :].unsqueeze(2).to_broadcast([128, 16, 64]), op=ALU.is_ge),
                 r=["kf", "thr"], w=["kf"])
            P.op("dve", lambda e: e.tensor_tensor(out=gm_all, in0=gm_all, in1=own[:], op=ALU.add), r=["kf", "own"], w=["kf"])
            P.op("dve", lambda e: e.tensor_scalar(out=selb_all, in0=gm_all, scalar1=-NEG, scalar2=NEG, op0=ALU.mult, op1=ALU.add),
                 r=["kf"], w=["qb"])
            pst = pGt[:].bitcast(BF16)
            for half in range(2):
                for ii in range(8):
                    i = half * 8 + ii
                    P.op("pe", lambda e, i=i, ii=ii: e.transpose(pst[0:64, ii * 128:(ii + 1) * 128], selb_all[:, i, :], ident[:]),
                         r=["qb", "ident"], w=["pGt"])
                P.op("act", lambda e, half=half: e.copy(out=sT[:, half * 1024:(half + 1) * 1024], in_=pst[0:64, :]), r=["pGt"], w=[sk])
            for g in range(4):
                qcols = slice(g * 512, (g + 1) * 512)
                first = [True] * 4
                nkt = 32 * (g + 1)

                def ncol0(kt, g=g):
                    nf = 0
                    for a in range(4):
                        if kt >= 8 * (4 * g + a) + 8:
                            nf += 1
                    return 128 * nf

                def emit_qk(kt, g=g):
                    ps = pS[kt % 2]
                    psk = f"pS{kt % 2}"
                    n = kt // 2
                    c0 = ncol0(kt)
                    qc = slice(g * 512 + c0, (g + 1) * 512)
                    dtile = None
                    for a in range(4):
                        i = 4 * g + a
                        if 8 * i <= kt < 8 * i + 8:
                            dtile = a
                    kk = f"Kh{kt // 16}"
                    P.op("pe", lambda e: e.matmul(ps[:, c0:512], Kh[:, kt * 128:(kt + 1) * 128], qT[:, h, qc],
                                                  start=True, stop=False), r=[kk, "qT"], w=[psk])
                    P.op("pe", lambda e: e.matmul(ps[:, c0:512], ident[0:64, n:n + 1].to_broadcast([64, 128]), sT[:, qc],
                                                  start=False, stop=(dtile is None)), r=["ident", sk], w=[psk])
                    if dtile is not None:
                        a = dtile
                        r_ = kt - 8 * (4 * g + a)
                        P.op("pe", lambda e: e.matmul(ps[:, a * 128:(a + 1) * 128], ident[:], diag[:, r_, :],
                                                      start=False, stop=True), r=["ident", "diag"], w=[psk])

                def emit_exp(kt):
                    c0 = ncol0(kt)
                    P.op("act", lambda e: e.activation(out=PT[kt % 2][:, c0:512], in_=pS[kt % 2][:, c0:512], func=AF.Exp),
                         r=[f"pS{kt % 2}"], w=[f"PT{kt % 2}"])

                def emit_pv(kt, g=g):
                    pt = PT[kt % 2]
                    for a in range(4):
                        i = 4 * g + a
                        if kt >= 8 * i + 8:
                            continue
                        last = (kt == 8 * i + 7)
                        bank = pOb[a]
                        P.op("pe", lambda e, bank=bank, a=a, f=first[a], last=last: e.matmul(
                            bank[:, 0:130], pt[:, a * 128:(a + 1) * 128], Vh[:, kt, :],
                            start=f, stop=last), r=[f"PT{kt % 2}", f"Vh{kt // 16}"], w=[pOk[a]])
                        first[a] = False

                emit_qk(0)
                for kt in range(nkt):
                    if kt + 1 < nkt:
                        emit_qk(kt + 1)
                    emit_exp(kt)
                    emit_pv(kt)
                for a in range(4):
                    i = 4 * g + a
                    bank = pOb[a]
                    c0 = 0
                    P.op("dve", lambda e, bank=bank, c0=c0, a=a: e.reciprocal(out=rden[:, a:a + 1], in_=bank[:, c0 + 128:c0 + 129]),
                         r=[pOk[a]], w=["rden"])
                    P.op("dve", lambda e, bank=bank, c0=c0, a=a, i=i, h=h: e.tensor_scalar(
                        out=Oall[:, i, h * 128:(h + 1) * 128], in0=bank[:, c0:c0 + 128], scalar1=rden[:, a:a + 1], scalar2=None,
                        op0=ALU.mult), r=[pOk[a], "rden"], w=[f"Oall{i}"])
        for t in range(16):
            for k in range(8):
                P.op("pe", lambda e, k=k, t=t: e.transpose(pT[:, k, :], Oall[:, t, k * 128:(k + 1) * 128], ident[:]),
                     r=[f"Oall{t}", "ident"], w=["pT"])
            P.op("act", lambda e: e.copy(out=oT[:], in_=pT[:, :, :]), r=["pT"], w=["oT"])
            xi = nt.xin[t % 2]
            xk = f"nt_xin{t % 2}"
            P.dma("sp", xk, [(xi[:], x_d[t, :, :])], w=[xk])
            for cg in range(2):
                for k in range(8):
                    P.op("pe", lambda e, k=k, cg=cg: e.matmul(pQ[:, cg * 512:(cg + 1) * 512], oT[:, k, :],
                                                             w[:, k, cg * 512:(cg + 1) * 512], start=(k == 0), stop=(k == 7)),
                         r=["oT", "w"], w=["pQa", "pQb"])
            P.op("dve", lambda e, xi=xi: e.tensor_tensor(out=xi[:], in0=xi[:], in1=pQ[:], op=ALU.add), r=[xk, "pQa", "pQb"], w=[xk])
            P.dma("sp", f"y{t % 2}", [(y_d[t, :, :], xi[:])], r=[xk], w=[f"yo{t}"])
        CX.done(P, [f"yo{t}" for t in range(16)])
        print("moba instructions:", P.n_ins)
    return nc


def moba_consts(c):
    gb = np.full((16, 64), -1e30, np.float32)
    own = np.zeros((16, 64), np.float32)
    for i in range(16):
        cur = (8 * i + c) // 2
        gb[i, :cur] = 0.0
        own[i, cur] = 1.0
    diag = np.zeros((8, 128, 128), np.float32)
    r0 = c - (c % 2)
    kk = np.arange(128)[:, None]
    qq = np.arange(128)[None, :]
    for r in (r0, r0 + 1):
        kpos = r * 128 + kk
        qpos = c * 128 + qq
        diag[r] = np.where(kpos <= qpos, 0.0, NEG)
    gbb = np.ascontiguousarray(np.broadcast_to(gb[None], (128, 16, 64)))
    ownb = np.ascontiguousarray(np.broadcast_to(own[None], (128, 16, 64)))
    diagb = np.ascontiguousarray(diag.transpose(1, 0, 2)).astype(ml_dtypes.bfloat16)
    return gbb, ownb, diagb


def moba_inputs(c, xt, g, wq, wo, gq, cos_t, sin_t, kt, v, km):
    tiles = [8 * i + c for i in range(16)]
    gbb, ownb, diagb = moba_consts(c)
    return {"x": xt, "g": np.ascontiguousarray(np.broadcast_to(g[None, :], (128, D))), "wq": wq, "wo": wo,
            "gq": np.ascontiguousarray(np.broadcast_to(gq[None, :], (128, 128))),
            "cos": np.ascontiguousarray(np.stack([cos_t[j * 128:(j + 1) * 128] for j in tiles], 1)),
            "sin": np.ascontiguousarray(np.stack([sin_t[j * 128:(j + 1) * 128] for j in tiles], 1)),
            "ident": IDENT_BF, "kt": kt, "v": v, "km": km, "gbias": gbb, "own": ownb, "diag": diagb}


def run_moba(nc, xts, g, wq, wo, gq, cos_t, sin_t, kt, v, km):
    maps = [moba_inputs(c, xts[c], g, wq, wo, gq, cos_t, sin_t, kt, v, km) for c in range(NCORES)]
    res = run_bass_kernel_spmd(nc, maps, core_ids=list(range(NCORES)))
    return [r["y"] for r in res.results]


def emit_tails(ctx):
    CX = ctx
    x_d = CX.I("x", [16, 128, D])
    t_d = CX.O("tails", [32, D])
    with ExitStack() as es:
        P = CX.prog(es)
        P.dma("sp", "tl", [(t_d.rearrange("(t r) n -> t r n", r=2), x_d[:, 126:128, :])], w=["tails"])
        CX.done(P, ["tails"])


def emit_halo(ctx, lay):
    CX = ctx
    tg_d = CX.I("tails_g", [8 * 32, D])
    hs_d = CX.I("hsel", [128, 6, 8])
    xh_d = CX.O("xh", [32, D])
    with ExitStack() as es:
        P = CX.prog(es)
        G = P.sb("G", [32, 3, 8, D], F32)
        hs = P.sb("hs", [128, 6, 8], F32)
        acc = P.sb("acc", [32, D], F32)
        tg3 = tg_d.rearrange("(r q) n -> q r n", q=32)
        P.op("pool", lambda e: e.memset(G[:, 1:3, :, :], 0.0), w=["G"])
        P.dma("sp", "hl", [(hs[:], hs_d[:, :, :]), (G[:, 0, :, :], tg3),
                           (G[2:32, 1, :, :], tg3[0:30, :, :]), (G[0:2, 2, :, :], tg3[30:32, :, :])], w=["G", "hs"])
        first = True
        for v in range(3):
            for r in range(8):
                sc = hs[0:32, lay * 3 + v, r:r + 1]
                if first:
                    P.op("dve", lambda e, v=v, r=r, sc=sc: e.tensor_scalar(out=acc[:], in0=G[:, v, r, :], scalar1=sc, scalar2=None,
                                                                           op0=ALU.mult), r=["G", "hs"], w=["acc"])
                    first = False
                else:
                    P.op("dve", lambda e, v=v, r=r, sc=sc: e.scalar_tensor_tensor(out=acc[:], in0=G[:, v, r, :], scalar=sc, in1=acc[:],
                                                                                  op0=ALU.mult, op1=ALU.add), r=["G", "hs", "acc"], w=["acc"])
        P.dma("sp", "ho", [(xh_d[:, :], acc[:])], r=["acc"], w=["xh"])
        CX.done(P, ["xh"])


def emit_relayout(ctx):
    CX = ctx
    hg_d = CX.I("hgath", [8 * 2048, D])
    sc_d = CX.I("selc", [128, 8])
    y_d = CX.O("y", [16, 128, D])
    with ExitStack() as es:
        P = CX.prog(es)
        cand = [P.sb(f"cand{i}", [128, 8, D], F32) for i in range(2)]
        acc = [P.sb(f"acc{i}", [128, D], F32) for i in range(2)]
        sc = P.sb("sc", [128, 8], F32)
        P.dma("sp", "rs", [(sc[:], sc_d[:, :])], w=["sc"])
        for i in range(16):
            cb, ck = cand[i % 2], f"cand{i % 2}"
            ab, ak = acc[i % 2], f"acc{i % 2}"
            row0 = ((i // 2) * 16 + 8 * (i % 2)) * 128
            P.dma("sp", ck, [(cb[:], hg_d[row0:row0 + 1024, :].rearrange("(c p) n -> p c n", p=128))], w=[ck])
            P.op("dve", lambda e, cb=cb, ab=ab: e.tensor_scalar(out=ab[:], in0=cb[:, 0, :], scalar1=sc[:, 0:1], scalar2=None, op0=ALU.mult),
                 r=[ck, "sc"], w=[ak])
            for c2 in range(1, 8):
                P.op("dve", lambda e, cb=cb, ab=ab, c2=c2: e.scalar_tensor_tensor(out=ab[:], in0=cb[:, c2, :], scalar=sc[:, c2:c2 + 1], in1=ab[:],
                                                                                  op0=ALU.mult, op1=ALU.add), r=[ck, "sc", ak], w=[ak])
            P.dma("sp", f"ro{i % 2}", [(y_d[i, :, :], ab[:])], r=[ak], w=[f"yo{i}"])
        CX.done(P, [f"yo{i}" for i in range(16)])


def build_fused():
    nc = new_nc()
    E = lambda name, shape, dt=F32: din(nc, name, shape, dt)
    x_d = E("x", [16, 128, D])
    a_norm = E("a_norm_b", [2, 128, D]); a_win = E("a_w_in", [2, D, MIN_]); a_bg = E("a_bg_b", [2, 128, 8])
    a_hn = E("a_h_norm_b", [2, 128, D]); a_wout = E("a_w_out", [2, D, D])
    kvn = E("kv_norm_b", [128, D]); wkv = E("w_kv", [D, 2 * D]); kn = E("k_norm_b", [128, 128])
    b_norm = E("b_norm_b", [2, 128, D]); b_wq = E("b_w_q", [2, D, D]); b_qn = E("b_q_norm_b", [2, 128, 128]); b_wo = E("b_w_o", [2, D, D])
    f_norm = E("f_norm_b", [4, 128, D]); f_wup = E("f_w_up", [4, D, 2 * DFF]); f_cw = E("f_cw", [4, 128, 44, 3])
    f_cb = E("f_cb", [4, 128, 44]); f_wdn = E("f_w_down", [4, DFF, D])
    ident = E("ident", [128, 128], BF16); tri = E("tri", [128, 128])
    cmask = E("cmask", [128, 8]); hsel = E("hsel", [128, 6, 8]); selc = E("selc", [128, 8])
    cosC = E("cosC", [128, 16, 16]); sinC = E("sinC", [128, 16, 16]); cosI = E("cosI", [128, 16, 16]); sinI = E("sinI", [128, 16, 16])
    gbias = E("gbias", [128, 16, 64]); own = E("own", [128, 16, 64]); diag = E("diag", [128, 8, 128], BF16)
    y_d = dout(nc, "y", [16, 128, D])
    T = lambda name, shape, dt=F32: nc.dram_tensor(name, list(shape), dt)
    hA = T("hA", [2048, D]); hB = T("hB", [2048, D])
    st_src = T("st_src", [128, 1032]); st_g = T("st_g", [8 * 128, 1032])
    tl_src = T("tl_src", [32, D]); tl_g = T("tl_g", [8 * 32, D]); xh = T("xh_d", [32, D])
    kt_src = T("kt_src", [8 * 128, 2048], BF16); kt_g = T("kt_g", [64 * 128, 2048], BF16)
    v_src = T("v_src", [8 * 128, 16 * 130], BF16); v_g = T("v_g", [64 * 128, 16 * 130], BF16)
    km_src = T("km_src", [128, 64]); km_g = T("km_g", [8 * 128, 64])
    hgath = T("hgath", [8 * 2048, D])
    t3 = lambda t: t.ap().rearrange("(t p) n -> t p n", p=128)
    hA3, hB3 = t3(hA), t3(hB)
    with ExitStack() as es_sem:
        P = Prog(nc, None, es_sem)
        ctx = lambda aps: Ctx(nc, P, aps)

        def gather(src, dst):
            P.begin_phase(None)
            P.coll("AllGather", src.ap().opt(), dst.ap().opt())
            P.end_phase()

        def ffn(l, lay, xin, yout):
            emit_tails(ctx({"x": xin, "tails": tl_src.ap()}))
            gather(tl_src, tl_g)
            emit_halo(ctx({"tails_g": tl_g.ap(), "hsel": hsel, "xh": xh.ap()}), lay)
            build_ffn(ctx({"x": xin, "xh": xh.ap(), "g": f_norm[l, :, :], "wup": f_wup[l, :, :], "cw": f_cw[l, :, :, :],
                           "cb": f_cb[l, :, :], "wdn": f_wdn[l, :, :], "ident": ident, "y": yout}))

        cur = x_d
        for l in range(2):
            base = {"x": cur, "g": a_norm[l, :, :], "win": a_win[l, :, :], "bg": a_bg[l, :, :], "ident": ident, "tri": tri}
            build_mlstm(True, ctx(dict(base, cst=st_src.ap()[:, 0:1028], fst=st_src.ap()[:, 1028:1032])))
            gather(st_src, st_g)
            build_mlstm(False, ctx(dict(base, gh=a_hn[l, :, :], wout=a_wout[l, :, :],
                                        cst_in=st_g.ap().rearrange("(r p) n -> r p n", p=128)[:, :, 0:1028],
                                        fst_in=st_g.ap().rearrange("(r p) n -> p r n", p=128)[:, :, 1028:1032],
                                        cmask=cmask, y=hA3)))
            ffn(l, 0, hA3, hB3)
            cur = hB3
        build_kv(ctx({"x": hB3, "g": kvn, "wkv": wkv, "gk": kn, "cos": cosC, "sin": sinC, "ident": ident,
                      "kt": kt_src.ap().rearrange("(h p) n -> h p n", p=128),
                      "v": v_src.ap().rearrange("(h p) (t e) -> h p t e", p=128, e=130),
                      "km": km_src.ap().rearrange("p (h b) -> p h b", b=8)}))
        gather(kt_src, kt_g)
        gather(v_src, v_g)
        gather(km_src, km_g)
        gather(hB, hgath)
        emit_relayout(ctx({"hgath": hgath.ap(), "selc": selc, "y": hA3}))
        for j in range(2):
            build_moba(ctx({"x": hA3, "g": b_norm[j, :, :], "wq": b_wq[j, :, :], "wo": b_wo[j, :, :], "gq": b_qn[j, :, :],
                            "cos": cosI, "sin": sinI, "ident": ident,
                            "kt": kt_g.ap().rearrange("(r h p) n -> r h p n", h=8, p=128),
                            "v": v_g.ap().rearrange("(r h p) (t e) -> r h p t e", h=8, p=128, e=130),
                            "km": km_g.ap().rearrange("(r p) (h b) -> r p h b", p=128, b=8),
                            "gbias": gbias, "own": own, "diag": diag, "y": hB3}))
            ffn(2 + j, 1, hB3, hA3 if j == 0 else y_d)
        print("fused instructions:", P.n_ins)
    return nc


_NC = [None]


def fused_inputs(c, x, a_norm, a_w_in, a_b_gates, a_h_norm, a_w_out, kv_norm, w_kv, k_norm,
                 b_norm, b_w_q, b_q_norm, b_w_o, f_norm, f_w_up, f_conv_w, f_conv_b, f_w_down, cos_t, sin_t):
    rep = lambda a, n=128: np.ascontiguousarray(np.broadcast_to(a[..., None, :], a.shape[:-1] + (n, a.shape[-1])))
    contig = list(range(16 * c, 16 * c + 16))
    inter = [8 * i + c for i in range(16)]
    tab = lambda t, ids: np.ascontiguousarray(np.stack([t[j * 128:(j + 1) * 128] for j in ids], 1))
    cmask = np.zeros((128, 8), np.float32); cmask[:, :c] = 1.0
    hsel = np.zeros((128, 6, 8), np.float32)
    hsel[:, 1, c] = 1.0
    if c >= 1:
        hsel[:, 2, c - 1] = 1.0
        hsel[:, 3, c - 1] = 1.0
    else:
        hsel[:, 4, 7] = 1.0
    selc = np.zeros((128, 8), np.float32); selc[:, c] = 1.0
    gbb, ownb, diagb = moba_consts(c)
    return {
        "x": np.ascontiguousarray(x[c * 2048:(c + 1) * 2048].reshape(16, 128, D)),
        "a_norm_b": rep(a_norm), "a_w_in": a_w_in, "a_bg_b": rep(a_b_gates.reshape(2, 8)), "a_h_norm_b": rep(a_h_norm), "a_w_out": a_w_out,
        "kv_norm_b": rep(kv_norm), "w_kv": w_kv, "k_norm_b": rep(k_norm),
        "b_norm_b": rep(b_norm), "b_w_q": b_w_q, "b_q_norm_b": rep(b_q_norm), "b_w_o": b_w_o,
        "f_norm_b": rep(f_norm), "f_w_up": f_w_up,
        "f_cw": np.ascontiguousarray(f_conv_w.reshape(4, 3, 44, 128).transpose(0, 3, 2, 1)),
        "f_cb": np.ascontiguousarray(f_conv_b.reshape(4, 44, 128).transpose(0, 2, 1)), "f_w_down": f_w_down,
        "ident": IDENT_BF, "tri": TRI, "cmask": cmask, "hsel": hsel, "selc": selc,
        "cosC": tab(cos_t, contig), "sinC": tab(sin_t, contig), "cosI": tab(cos_t, inter), "sinI": tab(sin_t, inter),
        "gbias": gbb, "own": ownb, "diag": diagb,
    }


def kernel(x, a_norm, a_w_in, a_b_gates, a_h_norm, a_w_out, kv_norm, w_kv, k_norm,
           b_norm, b_w_q, b_q_norm, b_w_o, f_norm, f_w_up, f_conv_w, f_conv_b, f_w_down):
    f32 = lambda a: np.ascontiguousarray(np.asarray(a, dtype=np.float32))
    args = [f32(a) for a in (x, a_norm, a_w_in, a_b_gates, a_h_norm, a_w_out, kv_norm, w_kv, k_norm,
                             b_norm, b_w_q, b_q_norm, b_w_o, f_norm, f_w_up, f_conv_w, f_conv_b, f_w_down)]
    args[0] = args[0].reshape(S, D)
    cos_t, sin_t = rope_tables_np()
    if _NC[0] is None:
        _NC[0] = build_fused()
    maps = [fused_inputs(c, *args, cos_t, sin_t) for c in range(NCORES)]
    res = run_bass_kernel_spmd(_NC[0], maps, core_ids=list(range(NCORES)))
    out = np.empty((S, D), np.float32)
    for c in range(NCORES):
        y = res.results[c]["y"]
        for i in range(16):
            j = 8 * i + c
            out[j * 128:(j + 1) * 128] = y[i]
    return out.reshape(1, S, D)
```

```python
import numpy as np
from contextlib import ExitStack
import ml_dtypes
import concourse.bass as bass
import concourse.mybir as mybir
from concourse.bass_utils import run_bass_kernel_spmd

F32 = mybir.dt.float32
BF16 = mybir.dt.bfloat16
AF = mybir.ActivationFunctionType
ALU = mybir.AluOpType
AX = mybir.AxisListType

NCORES = 8
S = 16384
D = 1024
EPS = 1e-6
DFF = 2816
NEG = -30000.0


class Prog:
    def __init__(self, nc, es, es_sem=None):
        self.nc = nc
        self.es = es
        es_sem = es if es_sem is None else es_sem
        self.es_sem = es_sem
        self.eng = {"pe": nc.tensor, "act": nc.scalar, "dve": nc.vector, "pool": nc.gpsimd, "sp": nc.sync}
        self.esem = {}
        self.ecnt = {}
        for e in ("pe", "act", "dve", "pool"):
            self.esem[e] = es_sem.enter_context(nc.semaphore("s_" + e))
            self.ecnt[e] = 0
        self.waited = {}
        self.lastw = {}
        self.readers = {}
        self.dsem = {}
        self.dcnt = {}
        self.n_ins = 0
        self.phase = 0

    def sb(self, name, shape, dt):
        return self.es.enter_context(self.nc.sbuf_tensor(f"{name}_p{self.phase}", list(shape), dt))

    def ps(self, name, shape, dt):
        return self.es.enter_context(self.nc.psum_tensor(f"{name}_p{self.phase}", list(shape), dt))

    def _waits(self, e, r, w, dma=False):
        deps = []
        for k in r:
            t = self.lastw.get(k)
            if t is not None:
                deps.append((t, True))
        for k in w:
            t = self.lastw.get(k)
            if t is not None:
                deps.append((t, False))
            for t in self.readers.get(k, ()):
                deps.append((t, False))
        eng = self.eng[e]
        for (t, raw) in deps:
            sem, val, owner = t
            if owner == e and e == "pe" and not raw and not dma:
                continue
            key = (e, sem.name)
            if self.waited.get(key, 0) >= val:
                continue
            eng.wait_ge(sem, val)
            self.waited[key] = val

    def _commit(self, tok, r, w):
        for k in w:
            self.lastw[k] = tok
            self.readers[k] = []
        for k in r:
            if k in w:
                continue
            self.readers.setdefault(k, []).append(tok)
            if len(self.readers[k]) > 24:
                best = {}
                for t in self.readers[k]:
                    if t[0].name not in best or best[t[0].name][1] < t[1]:
                        best[t[0].name] = t
                self.readers[k] = list(best.values())

    def op(self, e, fn, r=(), w=()):
        self._waits(e, r, w)
        ins = fn(self.eng[e])
        self.ecnt[e] += 1
        ins.then_inc(self.esem[e], 1)
        self._commit((self.esem[e], self.ecnt[e], e), r, w)
        self.n_ins += 1

    def dma(self, q, chan, pairs, r=(), w=()):
        if chan not in self.dsem:
            self.dsem[chan] = self.es_sem.enter_context(self.nc.semaphore("d_" + chan))
            self.dcnt[chan] = 0
        sem = self.dsem[chan]
        prev = self.dcnt[chan]
        self._waits(q, r, w, dma=True)
        if prev and self.waited.get((q, sem.name), 0) < prev:
            self.eng[q].wait_ge(sem, prev)
            self.waited[(q, sem.name)] = prev
        for (o, i) in pairs:
            self.eng[q].dma_start(out=o, in_=i).then_inc(sem, 16)
            self.dcnt[chan] += 16
            self.n_ins += 1
        self._commit((sem, self.dcnt[chan], "dma"), r, w)

    def coll(self, kind, src_ap, dst_ap, r=(), w=()):
        self.dma_like_wait("pool", r, w)
        if "cc" not in self.dsem:
            self.dsem["cc"] = self.es_sem.enter_context(self.nc.semaphore("d_cc"))
            self.dcnt["cc"] = 0
        sem = self.dsem["cc"]
        self.nc.gpsimd.collective_compute(kind, ALU.bypass, replica_groups=[list(range(NCORES))],
                                          ins=[src_ap], outs=[dst_ap]).then_inc(sem, 1)
        self.dcnt["cc"] += 1
        self.n_ins += 1
        self._commit((sem, self.dcnt["cc"], "dma"), r, w)

    def dma_like_wait(self, q, r, w):
        self._waits(q, r, w, dma=True)

    def begin_phase(self, es):
        self.es = es
        self.phase += 1
        self.lastw = {}
        self.readers = {}
        return self

    def barrier(self):
        for e in ("pe", "act", "dve", "pool", "sp"):
            for o in ("pe", "act", "dve", "pool"):
                if o != e and self.ecnt[o] and self.waited.get((e, self.esem[o].name), 0) < self.ecnt[o]:
                    self.eng[e].wait_ge(self.esem[o], self.ecnt[o])
                    self.waited[(e, self.esem[o].name)] = self.ecnt[o]
            for ch, sem in self.dsem.items():
                if self.dcnt[ch] and self.waited.get((e, sem.name), 0) < self.dcnt[ch]:
                    self.eng[e].wait_ge(sem, self.dcnt[ch])
                    self.waited[(e, sem.name)] = self.dcnt[ch]

    def end_phase(self):
        self.barrier()
        self.lastw = {}
        self.readers = {}

    def finish(self, keys):
        self._waits("sp", keys, ())
        for e in ("pe", "act", "dve", "pool"):
            if self.ecnt[e]:
                self.eng["sp"].wait_ge(self.esem[e], self.ecnt[e])


def new_nc():
    return bass.Bass("TRN2", target_bir_lowering=False)


def din(nc, name, shape, dt=F32):
    return nc.dram_tensor(name, list(shape), dt, kind="ExternalInput").ap()


def dout(nc, name, shape, dt=F32):
    return nc.dram_tensor(name, list(shape), dt, kind="ExternalOutput").ap()


class Ctx:
    def __init__(self, nc=None, P=None, aps=None):
        self.fused = nc is not None
        self.nc = nc if self.fused else new_nc()
        self.P = P
        self.aps = aps

    def I(self, name, shape, dt=F32):
        if self.fused:
            ap = self.aps[name]
            assert list(ap.shape) == list(shape), (name, ap.shape, shape)
            return ap
        return din(self.nc, name, shape, dt)

    def O(self, name, shape, dt=F32):
        if self.fused:
            ap = self.aps[name]
            assert list(ap.shape) == list(shape), (name, ap.shape, shape)
            return ap
        return dout(self.nc, name, shape, dt)

    def prog(self, es):
        if self.fused:
            return self.P.begin_phase(es)
        return Prog(self.nc, es)

    def done(self, P, keys):
        if self.fused:
            P.end_phase()
        else:
            P.finish(keys)


def bf16_np(a):
    return a.astype(ml_dtypes.bfloat16)


IDENT_BF = np.eye(128, dtype=np.float32).astype(ml_dtypes.bfloat16)
IDENT_F = np.eye(128, dtype=np.float32)


def emit_rstd(P, ssq, rstd, n, key_ssq, key_rstd, tmp, dim=D):
    P.op("act", lambda e: e.activation(out=tmp[:, 0:n], in_=ssq[:, 0:n], func=AF.Ln, scale=1.0 / dim, bias=EPS_AP[0][:, 0:1]),
         r=[key_ssq, "epsc"], w=[key_rstd + "_t"])
    P.op("act", lambda e: e.activation(out=rstd[:, 0:n], in_=tmp[:, 0:n], func=AF.Exp, scale=-0.5),
         r=[key_rstd + "_t"], w=[key_rstd])


EPS_AP = [None]


def make_eps(P):
    t = P.sb("epsc", [128, 1], F32)
    P.op("pool", lambda e: e.memset(t[:], EPS), w=["epsc"])
    EPS_AP[0] = t


def build_ffn(ctx=None):
    CX = ctx if ctx is not None else Ctx()
    nc = CX.nc
    x_d = CX.I("x", [16, 128, D])
    xh_d = CX.I("xh", [32, D])
    g_d = CX.I("g", [128, D])
    wup_d = CX.I("wup", [D, 2 * DFF])
    cw_d = CX.I("cw", [128, 44, 3])
    cb_d = CX.I("cb", [128, 44])
    wdn_d = CX.I("wdn", [DFF, D])
    id_d = CX.I("ident", [128, 128], BF16)
    y_d = CX.O("y", [16, 128, D])
    with ExitStack() as es:
        P = CX.prog(es)
        xres = P.sb("xres", [128, 16, D], F32)
        g_b = P.sb("g_b", [128, D], F32)
        ident = P.sb("ident_s", [128, 128], BF16)
        cw = P.sb("cw_s", [128, 44, 3], F32)
        cb = P.sb("cb_s", [128, 44], F32)
        wdn = P.sb("wdn_s", [128, 22, D], BF16)
        ssq = P.sb("ssq", [128, 20], F32)
        rstd = P.sb("rstd", [128, 20], F32)
        rtmp = P.sb("rtmp", [128, 20], F32)
        xh = P.sb("xh_s", [32, D], F32)
        junk = P.sb("junk", [128, D], BF16)
        xn = [P.sb(f"xn{i}", [128, D], BF16) for i in range(2)]
        xnT = P.sb("xnT", [128, 8, 2, 260], BF16)
        wv = [P.sb(f"wv{i}", [128, 8, 256], BF16) for i in range(2)]
        wg = [P.sb(f"wg{i}", [128, 8, 256], BF16) for i in range(2)]
        Uv = P.sb("Uv", [128, 4, 130], F32)
        Ug = P.sb("Ug", [128, 4, 130], F32)
        tv = [P.sb(f"tv{i}", [128, 4, 128], F32) for i in range(2)]
        tg = [P.sb(f"tg{i}", [128, 4, 128], F32) for i in range(2)]
        sg = P.sb("sg", [128, 4, 128], F32)
        aT = P.sb("aT", [128, 22, 512], BF16)
        pT = [P.ps(f"pT{i}", [128, 8, 128], BF16) for i in range(2)]
        pU = [P.ps(f"pU{i}", [128, 512], F32) for i in range(4)]
        pD = [P.ps(f"pD{i}", [128, 512], F32) for i in range(2)]

        P.dma("sp", "c0", [(g_b[:], g_d[:, :]), (ident[:], id_d[:, :]), (cw[:], cw_d[:, :, :]),
                           (cb[:], cb_d[:, :]), (xh[:], xh_d[:, :])], w=["g_b", "ident", "cw", "cb", "xh"])
        for t in range(16):
            P.dma("sp", f"x{t % 4}", [(xres[:, t, :], x_d[t, :, :])], w=[f"x{t}"])
        P.dma("pool", "wdn", [(wdn[:, c * 11:(c + 1) * 11, :],
                               wdn_d[c * 11 * 128:(c + 1) * 11 * 128, :].rearrange("(c p) n -> p c n", p=128))
                              for c in range(2)], w=["wdn"])

        make_eps(P)
        P.op("pool", lambda e: e.memset(ssq[:], 1.0), w=["ssq"])
        for t in range(16):
            P.op("act", lambda e, t=t: e.activation(out=junk[:], in_=xres[:, t, :], func=AF.Square,
                                                    accum_out=ssq[:, t:t + 1]), r=[f"x{t}"], w=["junk", "ssq"])
        P.op("act", lambda e: e.activation(out=junk[0:32, :], in_=xh[:], func=AF.Square,
                                           accum_out=ssq[0:32, 16:17]), r=["xh"], w=["junk", "ssq"])
        emit_rstd(P, ssq, rstd, 17, "ssq", "rstd", rtmp)

        wcount = [0]

        def load_w(b):
            i = wcount[0] % 2
            wcount[0] += 1
            P.dma("pool", f"wup{i}", [
                (wv[i][:], wup_d[:, b * 256:(b + 1) * 256].rearrange("(k p) n -> p k n", p=128)),
                (wg[i][:], wup_d[:, DFF + b * 256:DFF + (b + 1) * 256].rearrange("(k p) n -> p k n", p=128)),
            ], w=[f"wup{i}"])
            return i

        nT = [0]
        for q in range(4):
            for tt in range(4):
                t = q * 4 + tt
                xb = xn[t % 2]
                P.op("dve", lambda e, t=t, xb=xb: e.scalar_tensor_tensor(
                    out=xb[:], in0=xres[:, t, :], scalar=rstd[:, t:t + 1], in1=g_b[:],
                    op0=ALU.mult, op1=ALU.mult), r=[f"x{t}", "rstd", "g_b"], w=[f"xn{t % 2}"])
                pt = pT[nT[0] % 2]
                pk = f"pT{nT[0] % 2}"
                nT[0] += 1
                for k in range(8):
                    P.op("pe", lambda e, k=k, pt=pt, xb=xb: e.transpose(pt[:, k, :], xb[:, k * 128:(k + 1) * 128],
                                                                       ident[:]),
                         r=[f"xn{t % 2}", "ident"], w=[pk])
                hh, cc = tt // 2, (tt % 2) * 128
                P.op("act", lambda e, pt=pt, hh=hh, cc=cc: e.copy(out=xnT[:, :, hh, cc:cc + 128], in_=pt[:, :, :]),
                     r=[pk], w=["xnT"])
            xb = xn[0]
            P.op("dve", lambda e, xb=xb: e.scalar_tensor_tensor(
                out=xb[0:32, :], in0=xh[:], scalar=rstd[0:32, 16:17], in1=g_b[0:32, :],
                op0=ALU.mult, op1=ALU.mult), r=["xh", "rstd", "g_b"], w=["xn0"])
            pt = pT[nT[0] % 2]
            pk = f"pT{nT[0] % 2}"
            nT[0] += 1
            for k in range(8):
                P.op("pe", lambda e, k=k, pt=pt, xb=xb: e.transpose(pt[:, k, 0:32], xb[0:32, k * 128:(k + 1) * 128],
                                                                   ident[0:32, 0:32]),
                     r=["xn0", "ident"], w=[pk])
            for hh in range(2):
                P.op("act", lambda e, pt=pt, hh=hh: e.copy(out=xnT[:, :, hh, 256:260],
                                                           in_=pt[:, :, 8 * q + 4 * hh:8 * q + 4 * hh + 4]),
                     r=[pk], w=["xnT"])

            for b in range(11):
                wi = load_w(b)
                for sub in range(2):
                    c = 2 * b + sub
                    for (wt, U, nm, pbase) in ((wv[wi], Uv, "Uv", 0), (wg[wi], Ug, "Ug", 2)):
                        for hh in range(2):
                            pu = pU[pbase + hh]
                            puk = f"pU{pbase + hh}"
                            for k in range(8):
                                P.op("pe", lambda e, pu=pu, wt=wt, k=k, hh=hh, sub=sub: e.matmul(
                                    pu[:, 0:260], wt[:, k, sub * 128:(sub + 1) * 128], xnT[:, k, hh, :],
                                    start=(k == 0), stop=(k == 7)),
                                     r=[f"wup{wi}", "xnT"], w=[puk])
                            P.op("act", lambda e, pu=pu, U=U, hh=hh: e.copy(
                                out=U[:, 2 * hh:2 * hh + 2, 2:130],
                                in_=pu[:, 0:256].rearrange("p (a b) -> p a b", a=2)), r=[puk], w=[nm])
                            P.op("dve", lambda e, pu=pu, U=U, hh=hh: e.tensor_copy(
                                out=U[:, 2 * hh:2 * hh + 2, 0:2],
                                in_=pu[:, 256:260].rearrange("p (a b) -> p a b", a=2)), r=[puk], w=[nm])
                    for (eng, U, nm, tb, tn, cidx) in (("dve", Uv, "Uv", tv, "tv", c), ("dve", Ug, "Ug", tg, "tg", 22 + c)):
                        P.op(eng, lambda e, U=U, tb=tb, cidx=cidx: e.tensor_scalar(
                            out=tb[0][:], in0=U[:, :, 0:128], scalar1=cw[:, cidx, 0:1], scalar2=cb[:, cidx:cidx + 1],
                            op0=ALU.mult, op1=ALU.add), r=[nm, "cw", "cb"], w=[tn + "0"])
                        for j in (1, 2):
                            P.op(eng, lambda e, U=U, tb=tb, cidx=cidx, j=j: e.scalar_tensor_tensor(
                                out=tb[0][:], in0=U[:, :, j:j + 128], scalar=cw[:, cidx, j:j + 1], in1=tb[0][:],
                                op0=ALU.mult, op1=ALU.add), r=[nm, "cw", tn + "0"], w=[tn + "0"])
                    P.op("act", lambda e: e.activation(out=sg[:], in_=tg[0][:], func=AF.Silu), r=["tg0"], w=["sg"])
                    P.op("dve", lambda e, c=c: e.tensor_tensor(
                        out=aT[:, c, :].rearrange("p (a b) -> p a b", a=4), in0=sg[:], in1=tv[0][:], op=ALU.mult),
                        r=["sg", "tv0"], w=["aT"])
            for tt in range(4):
                t = q * 4 + tt
                for cg in range(2):
                    pd = pD[cg]
                    for c in range(22):
                        P.op("pe", lambda e, pd=pd, c=c, tt=tt, cg=cg: e.matmul(
                            pd[:], aT[:, c, tt * 128:(tt + 1) * 128], wdn[:, c, cg * 512:(cg + 1) * 512],
                            start=(c == 0), stop=(c == 21)), r=["aT", "wdn"], w=[f"pD{cg}"])
                    P.op("dve", lambda e, pd=pd, t=t, cg=cg: e.tensor_tensor(
                        out=xres[:, t, cg * 512:(cg + 1) * 512], in0=xres[:, t, cg * 512:(cg + 1) * 512], in1=pd[:],
                        op=ALU.add), r=[f"pD{cg}", f"x{t}"], w=[f"x{t}"])
                P.dma("sp", f"y{t % 4}", [(y_d[t, :, :], xres[:, t, :])], r=[f"x{t}"], w=[f"yo{t}"])
        CX.done(P, [f"yo{t}" for t in range(16)])
        print("ffn instructions:", P.n_ins)
    return nc


def run_ffn(nc, xt, xh, g, wup, cw, cb, wdn):
    cwl = np.ascontiguousarray(cw.reshape(3, 44, 128).transpose(2, 1, 0))
    cbl = np.ascontiguousarray(cb.reshape(44, 128).T)
    gb = np.ascontiguousarray(np.broadcast_to(g[None, :], (128, D)))
    maps = [{"x": xt[c], "xh": xh[c], "g": gb, "wup": wup, "cw": cwl, "cb": cbl, "wdn": wdn, "ident": IDENT_BF}
            for c in range(NCORES)]
    res = run_bass_kernel_spmd(nc, maps, core_ids=list(range(NCORES)))
    return [r["y"] for r in res.results]


def halos_from(hfull, tile_ids):
    out = np.zeros((len(tile_ids) * 2, D), np.float32)
    for i, j in enumerate(tile_ids):
        if j > 0:
            out[2 * i:2 * i + 2] = hfull[j * 128 - 2:j * 128]
    return out


MIN_ = 3080


def build_mlstm(state_only, ctx=None):
    CX = ctx if ctx is not None else Ctx()
    nc = CX.nc
    x_d = CX.I("x", [16, 128, D])
    g_d = CX.I("g", [128, D])
    win_d = CX.I("win", [D, MIN_])
    bg_d = CX.I("bg", [128, 8])
    id_d = CX.I("ident", [128, 128], BF16)
    tri_d = CX.I("tri", [128, 128])
    if state_only:
        cst_o = CX.O("cst", [128, 4 * 257])
        fst_o = CX.O("fst", [128, 4])
    else:
        gh_d = CX.I("gh", [128, D])
        wout_d = CX.I("wout", [D, D])
        cst_d = CX.I("cst_in", [8, 128, 4 * 257])
        fst_d = CX.I("fst_in", [128, 8, 4])
        cm_d = CX.I("cmask", [128, 8])
        y_d = CX.O("y", [16, 128, D])
    with ExitStack() as es:
        P = CX.prog(es)
        make_eps(P)
        g_b = P.sb("g_b", [128, D], F32)
        ident = P.sb("ident_s", [128, 128], BF16)
        tri = P.sb("tri_s", [128, 128], F32)
        ones = P.sb("ones_s", [128, 128], F32)
        bg = P.sb("bg_s", [128, 8], F32)
        win = P.sb("win_s", [128, 8, MIN_], BF16)
        xin = [P.sb(f"xin{i}", [128, D], F32) for i in range(2)]
        junk = P.sb("junk", [128, D], BF16)
        ssq = P.sb("ssq", [128, 4], F32)
        rstd = P.sb("rstd", [128, 4], F32)
        rtmp = P.sb("rtmp", [128, 4], F32)
        xn = [P.sb(f"xn{i}", [128, D], BF16) for i in range(2)]
        xnT = P.sb("xnT", [128, 8, 512], BF16)
        qkT = P.sb("qkT", [128, 8, 512], BF16)
        ktok = P.sb("ktok", [128, 4, 512], BF16)
        vaug = P.sb("vaug", [128, 4, 4, 258], BF16)
        gts = P.sb("gts", [128, 4, 8], F32)
        C = P.sb("C", [128, 4, 257], F32)
        Cb = P.sb("Cb", [128, 4, 258], BF16)
        Facc = P.sb("Facc", [128, 4], F32)
        ig = P.sb("ig", [128, 4], F32)
        fp = P.sb("fp", [128, 4], F32)
        lf = P.sb("lf", [128, 4], F32)
        lfb = P.sb("lfb", [128, 4, 128], F32)
        bcs = P.sb("bcs", [128, 4], F32)
        av = P.sb("av", [128, 4], F32)
        eW = P.sb("eW", [128, 4], F32)
        eB = P.sb("eB", [128, 4], F32)
        E1 = P.sb("E1", [128, 128], F32)
        DT = P.sb("DT", [128, 128], F32)
        DTm = P.sb("DTm", [128, 128], F32)
        Sp = P.sb("Sp", [128, 128], BF16)
        qs = P.sb("qs", [128, 128], BF16)
        wk = P.sb("wk", [128, 128], BF16)
        pT = P.ps("pT", [128, 8, 128], BF16)
        pA = P.ps("pA", [128, 1024], F32)
        pB = P.ps("pB", [128, 512], F32)
        pM = P.ps("pM", [128, 512], F32)
        pPB = pM[:, 0:128]
        pG = pM[:, 128:144]
        pSd = P.ps("pSd", [128, 512], F32)
        pS = pSd[:, 0:128]
        pdC = pSd[:, 128:385]
        pOut = [P.ps(f"pOut{i}", [128, 512], F32) for i in range(2)]
        if not state_only:
            gh_b = P.sb("gh_b", [128, D], F32)
            wout = P.sb("wout_s", [128, 8, D], BF16)
            so = P.sb("so", [128, 4, D], BF16)
            gso = P.sb("gso", [128, D], F32)
            hg = P.sb("hg", [128, D], BF16)
            hgT = P.sb("hgT", [128, 8, 128], BF16)
            cstb = [P.sb(f"cstb{i}", [128, 4, 257], F32) for i in range(2)]
            fstb = P.sb("fstb", [128, 8, 4], F32)
            cm = P.sb("cm_s", [128, 8], F32)
            dec = P.sb("dec", [128, 4], F32)
            den = P.sb("den", [128, 4], F32)
            ssqh = P.sb("ssqh", [128, 4], F32)
            rsh = P.sb("rsh", [128, 4], F32)
            rth = P.sb("rth", [128, 4], F32)
            sc = P.sb("sc", [128, 4], F32)
            xres = [P.sb(f"xr{i}", [128, D], F32) for i in range(2)]

        P.dma("sp", "c0", [(g_b[:], g_d[:, :]), (ident[:], id_d[:, :]), (tri[:], tri_d[:, :]), (bg[:], bg_d[:, :])],
              w=["g_b", "ident", "tri", "bg"])
        P.op("pool", lambda e: e.memset(ones[:], 1.0), w=["ones"])
        P.op("pool", lambda e: e.memset(vaug[:], 1.0), w=["vaug"])
        P.op("pool", lambda e: e.memset(C[:], 0.0), w=["C"])
        P.op("pool", lambda e: e.memset(Cb[:], 0.0), w=["Cb"])
        P.op("pool", lambda e: e.memset(Facc[:], 0.0), w=["Facc"])
        for k in range(8):
            P.dma("pool", "win", [(win[:, k, :], win_d[k * 128:(k + 1) * 128, :])], w=[f"win{k}"])
        WIN = [f"win{k}" for k in range(8)]
        if not state_only:
            P.dma("sp", "c1", [(gh_b[:], gh_d[:, :]), (fstb[:], fst_d[:, :, :]), (cm[:], cm_d[:, :])],
                  w=["gh_b", "fstb", "cm"])
            P.dma("pool", "wout", [(wout[:], wout_d.rearrange("(k p) n -> p k n", p=128))], w=["wout"])
            for cp in range(8):
                cb_ = cstb[cp % 2]
                ck = f"cstb{cp % 2}"
                P.dma("sp", ck, [(cb_[:], cst_d[cp, :, :].rearrange("p (h e) -> p h e", h=4))], w=[ck])
                P.op("act", lambda e, cp=cp: e.activation(out=dec[:], in_=fstb[:, cp, :], func=AF.Exp), r=["fstb"], w=["dec"])
                P.op("dve", lambda e, cp=cp: e.tensor_scalar(out=dec[:], in0=dec[:], scalar1=-1.0, scalar2=cm[:, cp:cp + 1],
                                                             op0=ALU.add, op1=ALU.mult), r=["dec", "cm"], w=["dec"])
                P.op("dve", lambda e: e.tensor_scalar(out=dec[:], in0=dec[:], scalar1=1.0, scalar2=None, op0=ALU.add),
                     r=["dec"], w=["dec"])
                P.op("dve", lambda e, cp=cp, cb_=cb_: e.tensor_scalar(out=cb_[:], in0=cb_[:], scalar1=cm[:, cp:cp + 1], scalar2=None,
                                                                     op0=ALU.mult), r=[ck, "cm"], w=[ck])
                for h in range(4):
                    P.op("dve", lambda e, h=h, cb_=cb_: e.scalar_tensor_tensor(
                        out=C[:, h, :], in0=C[:, h, :], scalar=dec[:, h:h + 1], in1=cb_[:, h, :],
                        op0=ALU.mult, op1=ALU.add), r=["C", "dec", ck], w=["C"])
            P.op("act", lambda e: e.copy(out=Cb[:, :, 0:257], in_=C[:, :, :]), r=["C"], w=["Cb"])

        def proj_fm(gi_, col0, scale):
            for k in range(8):
                P.op("pe", lambda e, k=k: e.matmul(pB[:], win[:, k, col0:col0 + 128], xnT[:, k, :],
                                                  start=(k == 0), stop=(k == 7)), r=[f"win{k}", "xnT"], w=["pB"])
            if scale == 1.0:
                P.op("act", lambda e: e.copy(out=qkT[:, gi_, :], in_=pB[:]), r=["pB"], w=["qkT"])
            else:
                P.op("act", lambda e: e.mul(out=qkT[:, gi_, :], in_=pB[:], mul=scale), r=["pB"], w=["qkT"])

        for grp in range(4):
            for tt in range(4):
                t = grp * 4 + tt
                xi = xin[t % 2]
                xk = f"xin{t % 2}"
                P.dma("sp", xk, [(xi[:], x_d[t, :, :])], w=[xk])
                P.op("act", lambda e, xi=xi, tt=tt: e.activation(out=junk[:], in_=xi[:], func=AF.Square,
                                                                 accum_out=ssq[:, tt:tt + 1]), r=[xk], w=["junk", "ssq"])
                emit_rstd(P, ssq[:, tt:tt + 1], rstd[:, tt:tt + 1], 1, "ssq", "rstd", rtmp[:, tt:tt + 1])
                xb = xn[t % 2]
                P.op("dve", lambda e, xi=xi, xb=xb, tt=tt: e.scalar_tensor_tensor(
                    out=xb[:], in0=xi[:], scalar=rstd[:, tt:tt + 1], in1=g_b[:], op0=ALU.mult, op1=ALU.mult),
                    r=[xk, "rstd", "g_b"], w=[f"xn{t % 2}"])
                for k in range(8):
                    P.op("pe", lambda e, k=k, xb=xb: e.transpose(pT[:, k, :], xb[:, k * 128:(k + 1) * 128], ident[:]),
                         r=[f"xn{t % 2}", "ident"], w=["pT"])
                P.op("act", lambda e, tt=tt: e.copy(out=xnT[:, :, tt * 128:(tt + 1) * 128], in_=pT[:, :, :]),
                     r=["pT"], w=["xnT"])
            if not state_only:
                for h in range(4):
                    proj_fm(h, h * 128, 128.0 ** -0.5)
                for h in range(4):
                    proj_fm(4 + h, 512 + h * 128, 1.0)
            for tt in range(4):
                lhs = lambda k, tt=tt: xnT[:, k, tt * 128:(tt + 1) * 128]
                for k in range(8):
                    P.op("pe", lambda e, k=k: e.matmul(pB[:], lhs(k), win[:, k, 512:1024], start=(k == 0), stop=(k == 7)),
                         r=[f"win{k}", "xnT"], w=["pB"])
                P.op("act", lambda e, tt=tt: e.copy(out=ktok[:, tt, :], in_=pB[:]), r=["pB"], w=["ktok"])
                for cg in range(2):
                    for k in range(8):
                        P.op("pe", lambda e, k=k, cg=cg: e.matmul(pA[:, cg * 512:(cg + 1) * 512], lhs(k),
                                                                 win[:, k, 1024 + cg * 512:1536 + cg * 512],
                                                                 start=(k == 0), stop=(k == 7)), r=[f"win{k}", "xnT"], w=["pA"])
                P.op("act", lambda e, tt=tt: e.copy(out=vaug[:, tt, :, 0:256], in_=pA[:].rearrange("p (h e) -> p h e", h=4)),
                     r=["pA"], w=["vaug"])
                if not state_only:
                    for cg in range(2):
                        for k in range(8):
                            P.op("pe", lambda e, k=k, cg=cg: e.matmul(pA[:, cg * 512:(cg + 1) * 512], lhs(k),
                                                                     win[:, k, 2048 + cg * 512:2560 + cg * 512],
                                                                     start=(k == 0), stop=(k == 7)), r=[f"win{k}", "xnT"], w=["pA"])
                    P.op("act", lambda e, tt=tt: e.activation(out=so[:, tt, :], in_=pA[:], func=AF.Sigmoid), r=["pA"], w=["so"])
                for k in range(8):
                    P.op("pe", lambda e, k=k: e.matmul(pG[:, 0:8], lhs(k), win[:, k, 3072:3080], start=(k == 0), stop=(k == 7)),
                         r=[f"win{k}", "xnT"], w=["pM"])
                P.op("dve", lambda e, tt=tt: e.tensor_tensor(out=gts[:, tt, :], in0=pG[:, 0:8], in1=bg[:], op=ALU.add),
                     r=["pM", "bg"], w=["gts"])
            for tt in range(4):
                t = grp * 4 + tt
                P.op("act", lambda e, tt=tt: e.activation(out=fp[:], in_=gts[:, tt, 4:8], func=AF.Exp, scale=-1.0), r=["gts"], w=["fp"])
                P.op("act", lambda e: e.activation(out=fp[:], in_=fp[:], func=AF.Ln, scale=1.0, bias=ones[:, 0:1]), r=["fp", "ones"], w=["fp"])
                P.op("dve", lambda e: e.tensor_scalar(out=lf[:], in0=fp[:], scalar1=-1.0, scalar2=None, op0=ALU.mult), r=["fp"], w=["lf"])
                P.op("pe", lambda e: e.matmul(pG[:, 8:12], tri[:], lf[:], start=True, stop=True), r=["tri", "lf"], w=["pM"])
                P.op("pe", lambda e: e.matmul(pG[:, 12:16], ones[:], lf[:], start=True, stop=True), r=["ones", "lf"], w=["pM"])
                P.op("dve", lambda e, tt=tt: e.tensor_tensor(out=av[:], in0=gts[:, tt, 0:4], in1=pG[:, 8:12], op=ALU.subtract),
                     r=["gts", "pM"], w=["av"])
                P.op("dve", lambda e: e.tensor_tensor(out=eW[:], in0=av[:], in1=pG[:, 12:16], op=ALU.add), r=["av", "pM"], w=["eW"])
                P.op("act", lambda e: e.activation(out=eW[:], in_=eW[:], func=AF.Exp), r=["eW"], w=["eW"])
                P.op("act", lambda e: e.activation(out=eB[:], in_=pG[:, 12:16], func=AF.Exp), r=["pM"], w=["eB"])
                P.op("dve", lambda e: e.tensor_tensor(out=Facc[:], in0=Facc[:], in1=pG[:, 12:16], op=ALU.add), r=["Facc", "pM"], w=["Facc"])
                if not state_only:
                    P.op("dve", lambda e: e.tensor_copy(out=lfb[:], in_=lf[:, :].unsqueeze(2).to_broadcast([128, 4, 128])),
                         r=["lf"], w=["lfb"])
                    P.op("pool", lambda e, tt=tt: e.tensor_tensor(out=gso[:], in0=so[:, tt, :], in1=gh_b[:], op=ALU.mult),
                         r=["so", "gh_b"], w=["gso"])
                for h in range(4):
                    cs = slice(tt * 128, (tt + 1) * 128)
                    if not state_only:
                        P.op("pe", lambda e, h=h: e.matmul(pPB[:], lfb[:, h, :], tri[:], start=True, stop=True),
                             r=["lfb", "tri"], w=["pM"])
                        P.op("act", lambda e: e.activation(out=E1[:], in_=pPB[:], func=AF.Exp), r=["pM"], w=["E1"])
                        P.op("act", lambda e, h=h: e.activation(out=DT[:], in_=pPB[:], func=AF.Exp, bias=av[:, h:h + 1]),
                             r=["pM", "av"], w=["DT"])
                        P.op("pool", lambda e: e.tensor_tensor(out=DTm[:], in0=DT[:], in1=tri[:], op=ALU.mult), r=["DT", "tri"], w=["DTm"])
                        P.op("pool", lambda e, h=h, cs=cs: e.tensor_tensor(out=qs[:], in0=qkT[:, h, cs], in1=E1[:], op=ALU.mult),
                             r=["qkT", "E1"], w=["qs"])
                        P.op("pe", lambda e, h=h, cs=cs: e.matmul(pS[:], qkT[:, 4 + h, cs], qkT[:, h, cs], start=True, stop=True),
                             r=["qkT"], w=["pSd"])
                        P.op("dve", lambda e: e.tensor_tensor(out=Sp[:], in0=pS[:], in1=DTm[:], op=ALU.mult), r=["pSd", "DTm"], w=["Sp"])
                        po = pOut[h % 2]
                        pk = f"pOut{h % 2}"
                        P.op("pe", lambda e, h=h, po=po: e.matmul(po[:, 0:257], qs[:], Cb[:, h, 0:257], start=True, stop=False),
                             r=["qs", "Cb"], w=[pk])
                        P.op("pe", lambda e, h=h, po=po, tt=tt: e.matmul(po[:, 0:257], Sp[:], vaug[:, tt, h, 0:257], start=False, stop=True),
                             r=["Sp", "vaug"], w=[pk])
                    P.op("pool", lambda e, h=h, tt=tt: e.tensor_scalar(out=wk[:], in0=ktok[:, tt, h * 128:(h + 1) * 128],
                                                                       scalar1=eW[:, h:h + 1], scalar2=None, op0=ALU.mult),
                         r=["ktok", "eW"], w=["wk"])
                    P.op("pe", lambda e, h=h, tt=tt: e.matmul(pdC, wk[:], vaug[:, tt, h, 0:257], start=True, stop=True),
                         r=["wk", "vaug"], w=["pSd"])
                    P.op("dve", lambda e, h=h: e.scalar_tensor_tensor(out=C[:, h, :], in0=C[:, h, :], scalar=eB[:, h:h + 1],
                                                                      in1=pdC, op0=ALU.mult, op1=ALU.add),
                         r=["C", "eB", "pSd"], w=["C"])
                    if not state_only:
                        P.op("act", lambda e, h=h: e.copy(out=Cb[:, h, 0:257], in_=C[:, h, :]), r=["C"], w=["Cb"])
                        P.op("act", lambda e, h=h, po=po: e.activation(out=den[:, h:h + 1], in_=po[:, 256:257], func=AF.Abs), r=[pk], w=["den"])
                        P.op("dve", lambda e, h=h: e.tensor_scalar(out=den[:, h:h + 1], in0=den[:, h:h + 1], scalar1=1.0, scalar2=None,
                                                                   op0=ALU.max), r=["den"], w=["den"])
                        P.op("dve", lambda e, h=h: e.reciprocal(out=den[:, h:h + 1], in_=den[:, h:h + 1]), r=["den"], w=["den"])
                        P.op("act", lambda e, h=h, po=po: e.activation(out=junk[:, 0:256], in_=po[:, 0:256], func=AF.Square,
                                                                       scale=den[:, h:h + 1], accum_out=ssqh[:, h:h + 1]),
                             r=[pk, "den"], w=["junk", "ssqh"])
                        emit_rstd(P, ssqh[:, h:h + 1], rsh[:, h:h + 1], 1, "ssqh", "rsh", rth[:, h:h + 1], dim=256)
                        P.op("dve", lambda e, h=h: e.tensor_tensor(out=sc[:, h:h + 1], in0=den[:, h:h + 1], in1=rsh[:, h:h + 1], op=ALU.mult),
                             r=["den", "rsh"], w=["sc"])
                        P.op("dve", lambda e, h=h, po=po: e.scalar_tensor_tensor(
                            out=hg[:, h * 256:(h + 1) * 256], in0=po[:, 0:256], scalar=sc[:, h:h + 1],
                            in1=gso[:, h * 256:(h + 1) * 256], op0=ALU.mult, op1=ALU.mult), r=[pk, "sc", "gso"], w=["hg"])
                if not state_only:
                    for k in range(8):
                        P.op("pe", lambda e, k=k: e.transpose(pT[:, k, :], hg[:, k * 128:(k + 1) * 128], ident[:]),
                             r=["hg", "ident"], w=["pT"])
                    P.op("act", lambda e: e.copy(out=hgT[:], in_=pT[:, :, :]), r=["pT"], w=["hgT"])
                    xr = xres[t % 2]
                    xrk = f"xr{t % 2}"
                    P.dma("sp", xrk, [(xr[:], x_d[t, :, :])], w=[xrk])
                    for cg in range(2):
                        for k in range(8):
                            P.op("pe", lambda e, k=k, cg=cg: e.matmul(pA[:, cg * 512:(cg + 1) * 512], hgT[:, k, :],
                                                                     wout[:, k, cg * 512:(cg + 1) * 512],
                                                                     start=(k == 0), stop=(k == 7)), r=["hgT", "wout"], w=["pA"])
                    P.op("dve", lambda e, xr=xr: e.tensor_tensor(out=xr[:], in0=xr[:], in1=pA[:], op=ALU.add), r=[xrk, "pA"], w=[xrk])
                    P.dma("sp", f"y{t % 2}", [(y_d[t, :, :], xr[:])], r=[xrk], w=[f"yo{t}"])
        if state_only:
            P.dma("sp", "so", [(cst_o[:, :].rearrange("p (h e) -> p h e", h=4), C[:]), (fst_o[:, :], Facc[:])],
                  r=["C", "Facc"], w=["outs"])
            CX.done(P, ["outs"])
        else:
            CX.done(P, [f"yo{t}" for t in range(16)])
        print("mlstm instructions:", P.n_ins, "state_only", state_only)
    return nc


TRI = np.triu(np.ones((128, 128), np.float32))


def mlstm_common_inputs(xt, g, win, bgates):
    gb = np.ascontiguousarray(np.broadcast_to(g[None, :], (128, D)))
    bgb = np.ascontiguousarray(np.broadcast_to(bgates.reshape(1, 8), (128, 8)))
    return {"x": xt, "g": gb, "win": win, "bg": bgb, "ident": IDENT_BF, "tri": TRI}


def run_mlstm_state(nc, xts, g, win, bgates):
    maps = [mlstm_common_inputs(xts[c], g, win, bgates) for c in range(NCORES)]
    res = run_bass_kernel_spmd(nc, maps, core_ids=list(range(NCORES)))
    return [r["cst"] for r in res.results], [r["fst"] for r in res.results]


def run_mlstm_full(nc, xts, g, win, bgates, gh, wout, csts, fsts):
    cst_all = np.ascontiguousarray(np.stack(csts, 0))
    fst_all = np.ascontiguousarray(np.stack(fsts, 1))
    ghb = np.ascontiguousarray(np.broadcast_to(gh[None, :], (128, D)))
    maps = []
    for c in range(NCORES):
        m = mlstm_common_inputs(xts[c], g, win, bgates)
        cmask = np.zeros((128, 8), np.float32)
        cmask[:, :c] = 1.0
        m.update({"gh": ghb, "wout": wout, "cst_in": cst_all, "fst_in": fst_all, "cmask": cmask})
        maps.append(m)
    res = run_bass_kernel_spmd(nc, maps, core_ids=list(range(NCORES)))
    return [r["y"] for r in res.results]


def rope_tables_np():
    pos = np.arange(S, dtype=np.float32)
    inv = (np.float32(500000.0) ** (-(np.arange(0, 32, 2, dtype=np.float32) / np.float32(32)))).astype(np.float32)
    ang = (pos[:, None] * inv[None, :]).astype(np.float32)
    return np.cos(ang).astype(np.float32), np.sin(ang).astype(np.float32)


class NormT:
    def __init__(self, P, pT, ident, g_b):
        self.P, self.pT, self.ident, self.g_b = P, pT, ident, g_b
        self.xin = [P.sb(f"nt_xin{i}", [128, D], F32) for i in range(2)]
        self.xn = [P.sb(f"nt_xn{i}", [128, D], BF16) for i in range(2)]
        self.junk = P.sb("nt_junk", [128, D], BF16)
        self.ssq = P.sb("nt_ssq", [128, 2], F32)
        self.rstd = P.sb("nt_rstd", [128, 2], F32)
        self.rtmp = P.sb("nt_rtmp", [128, 2], F32)
        self.xnT = P.sb("nt_xnT", [128, 8, 128], BF16)
        self.n = 0

    def emit(self, x_ap):
        P = self.P
        i = self.n % 2
        self.n += 1
        xi, xb = self.xin[i], self.xn[i]
        xk, bk = f"nt_xin{i}", f"nt_xn{i}"
        P.dma("sp", xk, [(xi[:], x_ap)], w=[xk])
        P.op("act", lambda e: e.activation(out=self.junk[:], in_=xi[:], func=AF.Square, accum_out=self.ssq[:, i:i + 1]),
             r=[xk], w=["nt_junk", f"nt_ssq{i}"])
        emit_rstd(P, self.ssq[:, i:i + 1], self.rstd[:, i:i + 1], 1, f"nt_ssq{i}", f"nt_rstd{i}", self.rtmp[:, i:i + 1])
        P.op("dve", lambda e: e.scalar_tensor_tensor(out=xb[:], in0=xi[:], scalar=self.rstd[:, i:i + 1], in1=self.g_b[:],
                                                     op0=ALU.mult, op1=ALU.mult), r=[xk, f"nt_rstd{i}", "g_b"], w=[bk])
        for k in range(8):
            P.op("pe", lambda e, k=k: e.transpose(self.pT[:, k, :], xb[:, k * 128:(k + 1) * 128], self.ident[:]),
                 r=[bk, "ident"], w=["pT"])
        P.op("act", lambda e: e.copy(out=self.xnT[:], in_=self.pT[:, :, :]), r=["pT"], w=["nt_xnT"])
        return self.xnT


class HeadNormRope:
    def __init__(self, P, gh_b, cos, sin, scale):
        self.P, self.gh_b, self.cos, self.sin, self.scale = P, gh_b, cos, sin, scale
        self.sq = P.sb("hn_sq", [128, D], F32)
        self.ssq = P.sb("hn_ssq", [128, 8], F32)
        self.rstd = P.sb("hn_rstd", [128, 8], F32)
        self.rtmp = P.sb("hn_rtmp", [128, 8], F32)
        self.rt = P.sb("hn_rt", [128, 4, 8, 16], F32)

    def emit(self, ps, pkeys, kf, kfkey, t):
        P = self.P
        P.op("act", lambda e: e.activation(out=self.sq[:], in_=ps, func=AF.Square), r=pkeys, w=["hn_sq"])
        P.op("dve", lambda e: e.tensor_reduce(out=self.ssq[:], in_=self.sq[:].rearrange("p (h e) -> p h e", h=8),
                                              axis=AX.X, op=ALU.add), r=["hn_sq"], w=["hn_ssq"])
        emit_rstd(P, self.ssq, self.rstd, 8, "hn_ssq", "hn_rstd", self.rtmp, dim=128)
        kf3 = kf.rearrange("p (h e) -> p h e", h=8)
        P.op("dve", lambda e: e.scalar_tensor_tensor(out=kf3, in0=ps.rearrange("p (h e) -> p h e", h=8), scalar=self.scale,
                                                     in1=self.rstd[:, :].unsqueeze(2).to_broadcast([128, 8, 128]),
                                                     op0=ALU.mult, op1=ALU.mult),
             r=pkeys + ["hn_rstd"], w=[kfkey])
        P.op("pool", lambda e: e.tensor_tensor(out=kf3, in0=kf3, in1=self.gh_b[:, :].unsqueeze(1).to_broadcast([128, 8, 128]),
                                               op=ALU.mult), r=[kfkey, "gh_b"], w=[kfkey])
        t1, t2 = kf3[:, :, 0:16], kf3[:, :, 16:32]
        cb = self.cos[:, t, :].unsqueeze(1).to_broadcast([128, 8, 16])
        sb_ = self.sin[:, t, :].unsqueeze(1).to_broadcast([128, 8, 16])
        rt = self.rt
        for (j, a, b) in ((0, t1, cb), (1, t2, sb_), (2, t2, cb), (3, t1, sb_)):
            P.op("pool", lambda e, j=j, a=a, b=b: e.tensor_tensor(out=rt[:, j, :, :], in0=a, in1=b, op=ALU.mult),
                 r=[kfkey, "cs"], w=[f"hn_rt{j}"])
        P.op("pool", lambda e: e.tensor_tensor(out=t1, in0=rt[:, 0, :, :], in1=rt[:, 1, :, :], op=ALU.subtract),
             r=["hn_rt0", "hn_rt1"], w=[kfkey])
        P.op("pool", lambda e: e.tensor_tensor(out=t2, in0=rt[:, 2, :, :], in1=rt[:, 3, :, :], op=ALU.add),
             r=["hn_rt2", "hn_rt3"], w=[kfkey])


def build_kv(ctx=None):
    CX = ctx if ctx is not None else Ctx()
    nc = CX.nc
    x_d = CX.I("x", [16, 128, D])
    g_d = CX.I("g", [128, D])
    w_d = CX.I("wkv", [D, 2 * D])
    gk_d = CX.I("gk", [128, 128])
    cos_d = CX.I("cos", [128, 16, 16])
    sin_d = CX.I("sin", [128, 16, 16])
    id_d = CX.I("ident", [128, 128], BF16)
    kt_o = CX.O("kt", [8, 128, 2048], BF16)
    v_o = CX.O("v", [8, 128, 16, 130], BF16)
    km_o = CX.O("km", [128, 8, 8])
    with ExitStack() as es:
        P = CX.prog(es)
        make_eps(P)
        g_b = P.sb("g_b", [128, D], F32)
        gk_b = P.sb("gk_b", [128, 128], F32)
        cos = P.sb("cos_s", [128, 16, 16], F32)
        sin = P.sb("sin_s", [128, 16, 16], F32)
        ident = P.sb("ident_s", [128, 128], BF16)
        onesf = P.sb("onesf", [128, 1], F32)
        w = P.sb("w_s", [128, 8, 2 * D], BF16)
        kTall = P.sb("kTall", [128, 8, 2048], BF16)
        vall = P.sb("vall", [128, 8, 16, 130], BF16)
        kf = [P.sb(f"kf{i}", [128, D], F32) for i in range(2)]
        kb = P.sb("kb", [128, D], BF16)
        kms = P.sb("kms", [128, 8, 8], F32)
        pT = P.ps("pT", [128, 8, 128], BF16)
        pK = P.ps("pK", [128, 1024], F32)
        pV = P.ps("pV", [128, 1024], F32)
        pKM = P.ps("pKM", [128, 64], F32)
        P.dma("sp", "c0", [(g_b[:], g_d[:, :]), (gk_b[:], gk_d[:, :]), (cos[:], cos_d[:, :, :]), (sin[:], sin_d[:, :, :]),
                           (ident[:], id_d[:, :])], w=["g_b", "gh_b", "cs", "ident"])
        P.op("pool", lambda e: e.memset(onesf[:], 1.0), w=["onesf"])
        P.op("pool", lambda e: e.memset(vall[:], 1.0), w=["vall"])
        for k in range(8):
            P.dma("pool", "w", [(w[:, k, :], w_d[k * 128:(k + 1) * 128, :])], w=[f"w{k}"])
        nt = NormT(P, pT, ident, g_b)
        hn = HeadNormRope(P, gk_b, cos, sin, 1.0)
        for t in range(16):
            xnT = nt.emit(x_d[t, :, :])
            for (ps, pk, c0) in ((pK, "pK", 0), (pV, "pV", D)):
                for cg in range(2):
                    for k in range(8):
                        P.op("pe", lambda e, k=k, cg=cg, ps=ps, c0=c0: e.matmul(
                            ps[:, cg * 512:(cg + 1) * 512], xnT[:, k, :], w[:, k, c0 + cg * 512:c0 + (cg + 1) * 512],
                            start=(k == 0), stop=(k == 7)), r=["nt_xnT", f"w{k}"], w=[pk])
            P.op("act", lambda e, t=t: e.copy(out=vall[:, :, t, 0:128], in_=pV[:].rearrange("p (h e) -> p h e", h=8)),
                 r=["pV"], w=["vall"])
            kfi = kf[t % 2]
            kk = f"kf{t % 2}"
            hn.emit(pK[:], ["pK"], kfi[:], kk, t)
            P.op("act", lambda e, kfi=kfi: e.copy(out=kb[:], in_=kfi[:]), r=[kk], w=["kb"])
            for h in range(8):
                P.op("pe", lambda e, h=h: e.transpose(pT[:, h, :], kb[:, h * 128:(h + 1) * 128], ident[:]), r=["kb", "ident"], w=["pT"])
            P.op("act", lambda e, t=t: e.copy(out=kTall[:, :, t * 128:(t + 1) * 128], in_=pT[:, :, :]), r=["pT"], w=["kTall"])
            if t % 2 == 1:
                blk = t // 2
                for h in range(8):
                    for j in range(2):
                        P.op("pe", lambda e, h=h, j=j, blk=blk: e.matmul(
                            pKM[:, h * 8 + blk:h * 8 + blk + 1], kf[j][:, h * 128:(h + 1) * 128], onesf[:],
                            start=(j == 0), stop=(j == 1)), r=[f"kf{j}", "onesf"], w=["pKM"])
        P.op("act", lambda e: e.mul(out=kms[:].rearrange("p h b -> p (h b)"), in_=pKM[:], mul=1.0 / 256.0), r=["pKM"], w=["kms"])
        P.dma("sp", "o0", [(kt_o[h, :, :], kTall[:, h, :]) for h in range(8)], r=["kTall"], w=["o_kt"])
        P.dma("sp", "o1", [(v_o[h, :, :, :], vall[:, h, :, :]) for h in range(8)], r=["vall"], w=["o_v"])
        P.dma("sp", "o2", [(km_o[:, :, :], kms[:])], r=["kms"], w=["o_km"])
        CX.done(P, ["o_kt", "o_v", "o_km"])
        print("kv instructions:", P.n_ins)
    return nc


def run_kv(nc, xts, g, wkv, gk, cos_t, sin_t, tile_ids):
    gb = np.ascontiguousarray(np.broadcast_to(g[None, :], (128, D)))
    gkb = np.ascontiguousarray(np.broadcast_to(gk[None, :], (128, 128)))
    maps = []
    for c in range(NCORES):
        cs = np.stack([cos_t[j * 128:(j + 1) * 128] for j in tile_ids[c]], 1)
        sn = np.stack([sin_t[j * 128:(j + 1) * 128] for j in tile_ids[c]], 1)
        maps.append({"x": xts[c], "g": gb, "wkv": wkv, "gk": gkb, "cos": np.ascontiguousarray(cs),
                     "sin": np.ascontiguousarray(sn), "ident": IDENT_BF})
    res = run_bass_kernel_spmd(nc, maps, core_ids=list(range(NCORES)))
    kt = np.concatenate([r["kt"] for r in res.results], 2)
    v = np.concatenate([r["v"] for r in res.results], 2)
    km = np.concatenate([r["km"] for r in res.results], 2)
    return kt, v, km


def build_moba(ctx=None):
    CX = ctx if ctx is not None else Ctx()
    nc = CX.nc
    x_d = CX.I("x", [16, 128, D])
    g_d = CX.I("g", [128, D])
    wq_d = CX.I("wq", [D, D])
    wo_d = CX.I("wo", [D, D])
    gq_d = CX.I("gq", [128, 128])
    cos_d = CX.I("cos", [128, 16, 16])
    sin_d = CX.I("sin", [128, 16, 16])
    id_d = CX.I("ident", [128, 128], BF16)
    kt_d = CX.I("kt", [8, 8, 128, 2048], BF16)
    v_d = CX.I("v", [8, 8, 128, 16, 130], BF16)
    km_d = CX.I("km", [8, 128, 8, 8])
    gb_d = CX.I("gbias", [128, 16, 64])
    own_d = CX.I("own", [128, 16, 64])
    dg_d = CX.I("diag", [128, 8, 128], BF16)
    y_d = CX.O("y", [16, 128, D])
    with ExitStack() as es:
        P = CX.prog(es)
        make_eps(P)
        g_b = P.sb("g_b", [128, D], F32)
        gq_b = P.sb("gq_b", [128, 128], F32)
        cos = P.sb("cos_s", [128, 16, 16], F32)
        sin = P.sb("sin_s", [128, 16, 16], F32)
        ident = P.sb("ident_s", [128, 128], BF16)
        w = P.sb("w_s", [128, 8, D], BF16)
        gbias = P.sb("gbias_s", [128, 16, 64], F32)
        own = P.sb("own_s", [128, 16, 64], F32)
        diag = P.sb("diag_s", [128, 8, 128], BF16)
        kmf = P.sb("kmf", [128, 8, 64], F32)
        kmb = P.sb("kmb", [128, 8, 64], BF16)
        qT = P.sb("qT", [128, 8, 2048], BF16)
        Kh = P.sb("Kh", [128, S], BF16)
        Vh = P.sb("Vh", [128, 128, 130], BF16)
        Oall = P.sb("Oall", [128, 16, D], BF16)
        sbT = [P.sb(f"sbT{i}", [128, 2048], BF16) for i in range(2)]
        kf = P.sb("kf", [128, D], F32)
        qb = P.sb("qb", [128, D], BF16)
        top8 = P.sb("top8", [128, 16, 8], F32)
        thr = P.sb("thr", [128, 16], F32)
        PT = [P.sb(f"PT{i}", [128, 512], BF16) for i in range(2)]
        rden = P.sb("rden", [128, 4], F32)
        oT = P.sb("oT", [128, 8, 128], BF16)
        pT = P.ps("pT", [128, 8, 128], BF16)
        pQ = P.ps("pQ", [128, 1024], F32)
        pS = [P.ps(f"pS{i}", [128, 512], F32) for i in range(2)]
        pO23 = [P.ps(f"pO{i}", [128, 512], F32) for i in (2, 3)]
        pOb = [pQ[:, 0:512], pQ[:, 512:1024], pO23[0][:], pO23[1][:]]
        pOk = ["pQa", "pQb", "pO2", "pO3"]
        pGt = P.ps("pGt", [128, 512], F32)

        P.dma("sp", "c0", [(g_b[:], g_d[:, :]), (gq_b[:], gq_d[:, :]), (cos[:], cos_d[:, :, :]), (sin[:], sin_d[:, :, :]),
                           (ident[:], id_d[:, :]), (gbias[:], gb_d[:, :, :]), (own[:], own_d[:, :, :]), (diag[:], dg_d[:, :, :])] +
              [(kmf[:, :, r * 8:(r + 1) * 8], km_d[r, :, :, :]) for r in range(8)],
              w=["g_b", "gh_b", "cs", "ident", "gbias", "own", "diag", "kmf"])
        P.op("dve", lambda e: e.tensor_copy(out=kmb[:], in_=kmf[:]), r=["kmf"], w=["kmb"])
        for i in range(2):
            P.op("pool", lambda e, i=i: e.memset(sbT[i][64:128, :], 0.0), w=[f"sbT{i}"])
        P.dma("pool", "w", [(w[:], wq_d.rearrange("(k p) n -> p k n", p=128))], w=["w"])

        nt = NormT(P, pT, ident, g_b)
        hn = HeadNormRope(P, gq_b, cos, sin, 128.0 ** -0.5)
        for t in range(16):
            xnT = nt.emit(x_d[t, :, :])
            for cg in range(2):
                for k in range(8):
                    P.op("pe", lambda e, k=k, cg=cg: e.matmul(pQ[:, cg * 512:(cg + 1) * 512], xnT[:, k, :],
                                                             w[:, k, cg * 512:(cg + 1) * 512], start=(k == 0), stop=(k == 7)),
                         r=["nt_xnT", "w"], w=["pQa", "pQb"])
            hn.emit(pQ[:], ["pQa", "pQb"], kf[:], "kf", t)
            P.op("act", lambda e: e.copy(out=qb[:], in_=kf[:]), r=["kf"], w=["qb"])
            for h in range(8):
                P.op("pe", lambda e, h=h: e.transpose(pT[:, h, :], qb[:, h * 128:(h + 1) * 128], ident[:]), r=["qb", "ident"], w=["pT"])
            P.op("act", lambda e, t=t: e.copy(out=qT[:, :, t * 128:(t + 1) * 128], in_=pT[:, :, :]), r=["pT"], w=["qT"])
        P.dma("pool", "w", [(w[:], wo_d.rearrange("(k p) n -> p k n", p=128))], r=[], w=["w"])

        for h in range(8):
            for r in range(8):
                P.dma("sp", f"kh{r}", [(Kh[:, r * 2048:(r + 1) * 2048], kt_d[r, h, :, :])], w=[f"Kh{r}"])
                P.dma("sp", f"vh{r}", [(Vh[:, r * 16:(r + 1) * 16, :], v_d[r, h, :, :, :])], w=[f"Vh{r}"])
            sT = sbT[h % 2]
            sk = f"sbT{h % 2}"
            gm_all = kf[:].rearrange("p (i n) -> p i n", i=16)
            selb_all = qb[:].rearrange("p (i n) -> p i n", i=16)
            for half in range(2):
                for ii in range(8):
                    i = half * 8 + ii
                    P.op("pe", lambda e, i=i, ii=ii: e.matmul(pGt[:, ii * 64:(ii + 1) * 64], qT[:, h, i * 128:(i + 1) * 128], kmb[:, h, :],
                                                           start=True, stop=True), r=["qT", "kmb"], w=["pGt"])
                P.op("dve", lambda e, half=half: e.tensor_tensor(
                    out=gm_all[:, half * 8:(half + 1) * 8, :], in0=pGt[:].rearrange("p (i n) -> p i n", i=8),
                    in1=gbias[:, half * 8:(half + 1) * 8, :], op=ALU.add), r=["pGt", "gbias"], w=["kf"])
            for i in range(16):
                P.op("dve", lambda e, i=i: e.max(out=top8[:, i, :], in_=gm_all[:, i, :]), r=["kf"], w=["top8"])
            P.op("dve", lambda e: e.tensor_scalar(out=thr[:, :].unsqueeze(2), in0=top8[:, :, 2:3], scalar1=-1e29, scalar2=None, op0=ALU.max),
                 r=["top8"], w=["thr"])
            P.op("dve", lambda e: e.tensor_tensor(out=gm_all, in0=gm_all, in1=thr[:, :].unsqueeze(2).to_broadcast([128, 16, 64]), op=ALU.is_ge),
                 r=["kf", "thr"], w=["kf"])
            P.op("dve", lambda e: e.tensor_tensor(out=gm_all, in0=gm_all, in1=own[:], op=ALU.add), r=["kf", "own"], w=["kf"])
            P.op("dve", lambda e: e.tensor_scalar(out=selb_all, in0=gm_all, scalar1=-NEG, scalar2=NEG, op0=ALU.mult, op1=ALU.add),
                 r=["kf"], w=["qb"])
            pst = pGt[:].bitcast(BF16)
            for half in range(2):
                for ii in range(8):
                    i = half * 8 + ii
                    P.op("pe", lambda e, i=i, ii=ii: e.transpose(pst[0:64, ii * 128:(ii + 1) * 128], selb_all[:, i, :], ident[:]),
                         r=["qb", "ident"], w=["pGt"])
                P.op("act", lambda e, half=half: e.copy(out=sT[0:64, half * 1024:(half + 1) * 1024], in_=pst[0:64, :]), r=["pGt"], w=[sk])
            for g in range(4):
                qcols = slice(g * 512, (g + 1) * 512)
                first = [True] * 4
                nkt = 32 * (g + 1)

                def ncol0(kt, g=g):
                    nf = 0
                    for a in range(4):
                        if kt >= 8 * (4 * g + a) + 8:
                            nf += 1
                    return 128 * nf

                def emit_qk(kt, g=g):
                    ps = pS[kt % 2]
                    psk = f"pS{kt % 2}"
                    n = kt // 2
                    c0 = ncol0(kt)
                    qc = slice(g * 512 + c0, (g + 1) * 512)
                    dtile = None
                    for a in range(4):
                        i = 4 * g + a
                        if 8 * i <= kt < 8 * i + 8:
                            dtile = a
                    kk = f"Kh{kt // 16}"
                    P.op("pe", lambda e: e.matmul(ps[:, c0:512], Kh[:, kt * 128:(kt + 1) * 128], qT[:, h, qc],
                                                  start=True, stop=False), r=[kk, "qT"], w=[psk])
                    P.op("pe", lambda e: e.matmul(ps[:, c0:512], ident[:, n:n + 1].to_broadcast([128, 128]), sT[:, qc],
                                                  start=False, stop=(dtile is None)), r=["ident", sk], w=[psk])
                    if dtile is not None:
                        a = dtile
                        r_ = kt - 8 * (4 * g + a)
                        P.op("pe", lambda e: e.matmul(ps[:, a * 128:(a + 1) * 128], ident[:], diag[:, r_, :],
                                                      start=False, stop=True), r=["ident", "diag"], w=[psk])

                def emit_exp(kt):
                    c0 = ncol0(kt)
                    P.op("act", lambda e: e.activation(out=PT[kt % 2][:, c0:512], in_=pS[kt % 2][:, c0:512], func=AF.Exp),
                         r=[f"pS{kt % 2}"], w=[f"PT{kt % 2}"])

                def emit_pv(kt, g=g):
                    pt = PT[kt % 2]
                    for a in range(4):
                        i = 4 * g + a
                        if kt >= 8 * i + 8:
                            continue
                        last = (kt == 8 * i + 7)
                        bank = pOb[a]
                        P.op("pe", lambda e, bank=bank, a=a, f=first[a], last=last: e.matmul(
                            bank[:, 0:130], pt[:, a * 128:(a + 1) * 128], Vh[:, kt, :],
                            start=f, stop=last), r=[f"PT{kt % 2}", f"Vh{kt // 16}"], w=[pOk[a]])
                        first[a] = False

                emit_qk(0)
                for kt in range(nkt):
                    if kt + 1 < nkt:
                        emit_qk(kt + 1)
                    emit_exp(kt)
                    emit_pv(kt)
                for a in range(4):
                    i = 4 * g + a
                    bank = pOb[a]
                    c0 = 0
                    P.op("dve", lambda e, bank=bank, c0=c0, a=a: e.reciprocal(out=rden[:, a:a + 1], in_=bank[:, c0 + 128:c0 + 129]),
                         r=[pOk[a]], w=["rden"])
                    P.op("dve", lambda e, bank=bank, c0=c0, a=a, i=i, h=h: e.tensor_scalar(
                        out=Oall[:, i, h * 128:(h + 1) * 128], in0=bank[:, c0:c0 + 128], scalar1=rden[:, a:a + 1], scalar2=None,
                        op0=ALU.mult), r=[pOk[a], "rden"], w=[f"Oall{i}"])
        for t in range(16):
            for k in range(8):
                P.op("pe", lambda e, k=k, t=t: e.transpose(pT[:, k, :], Oall[:, t, k * 128:(k + 1) * 128], ident[:]),
                     r=[f"Oall{t}", "ident"], w=["pT"])
            P.op("act", lambda e: e.copy(out=oT[:], in_=pT[:, :, :]), r=["pT"], w=["oT"])
            xi = nt.xin[t % 2]
            xk = f"nt_xin{t % 2}"
            P.dma("sp", xk, [(xi[:], x_d[t, :, :])], w=[xk])
            for cg in range(2):
                for k in range(8):
                    P.op("pe", lambda e, k=k, cg=cg: e.matmul(pQ[:, cg * 512:(cg + 1) * 512], oT[:, k, :],
                                                             w[:, k, cg * 512:(cg + 1) * 512], start=(k == 0), stop=(k == 7)),
                         r=["oT", "w"], w=["pQa", "pQb"])
            P.op("dve", lambda e, xi=xi: e.tensor_tensor(out=xi[:], in0=xi[:], in1=pQ[:], op=ALU.add), r=[xk, "pQa", "pQb"], w=[xk])
            P.dma("sp", f"y{t % 2}", [(y_d[t, :, :], xi[:])], r=[xk], w=[f"yo{t}"])
        CX.done(P, [f"yo{t}" for t in range(16)])
        print("moba instructions:", P.n_ins)
    return nc


def moba_consts(c):
    gb = np.full((16, 64), -1e30, np.float32)
    own = np.zeros((16, 64), np.float32)
    for i in range(16):
        cur = (8 * i + c) // 2
        gb[i, :cur] = 0.0
        own[i, cur] = 1.0
    diag = np.zeros((8, 128, 128), np.float32)
    r0 = c - (c % 2)
    kk = np.arange(128)[:, None]
    qq = np.arange(128)[None, :]
    for r in (r0, r0 + 1):
        kpos = r * 128 + kk
        qpos = c * 128 + qq
        diag[r] = np.where(kpos <= qpos, 0.0, NEG)
    gbb = np.ascontiguousarray(np.broadcast_to(gb[None], (128, 16, 64)))
    ownb = np.ascontiguousarray(np.broadcast_to(own[None], (128, 16, 64)))
    diagb = np.ascontiguousarray(diag.transpose(1, 0, 2)).astype(ml_dtypes.bfloat16)
    return gbb, ownb, diagb


def moba_inputs(c, xt, g, wq, wo, gq, cos_t, sin_t, kt, v, km):
    tiles = [8 * i + c for i in range(16)]
    gbb, ownb, diagb = moba_consts(c)
    return {"x": xt, "g": np.ascontiguousarray(np.broadcast_to(g[None, :], (128, D))), "wq": wq, "wo": wo,
            "gq": np.ascontiguousarray(np.broadcast_to(gq[None, :], (128, 128))),
            "cos": np.ascontiguousarray(np.stack([cos_t[j * 128:(j + 1) * 128] for j in tiles], 1)),
            "sin": np.ascontiguousarray(np.stack([sin_t[j * 128:(j + 1) * 128] for j in tiles], 1)),
            "ident": IDENT_BF, "kt": kt, "v": v, "km": km, "gbias": gbb, "own": ownb, "diag": diagb}


def run_moba(nc, xts, g, wq, wo, gq, cos_t, sin_t, kt, v, km):
    maps = [moba_inputs(c, xts[c], g, wq, wo, gq, cos_t, sin_t, kt, v, km) for c in range(NCORES)]
    res = run_bass_kernel_spmd(nc, maps, core_ids=list(range(NCORES)))
    return [r["y"] for r in res.results]


def emit_tails(ctx):
    CX = ctx
    x_d = CX.I("x", [16, 128, D])
    t_d = CX.O("tails", [32, D])
    with ExitStack() as es:
        P = CX.prog(es)
        P.dma("sp", "tl", [(t_d.rearrange("(t r) n -> t r n", r=2), x_d[:, 126:128, :])], w=["tails"])
        CX.done(P, ["tails"])


def emit_halo(ctx, lay):
    CX = ctx
    tg_d = CX.I("tails_g", [8 * 32, D])
    hs_d = CX.I("hsel", [128, 6, 8])
    xh_d = CX.O("xh", [32, D])
    with ExitStack() as es:
        P = CX.prog(es)
        G = P.sb("G", [32, 3, 8, D], F32)
        hs = P.sb("hs", [128, 6, 8], F32)
        acc = P.sb("acc", [32, D], F32)
        tg3 = tg_d.rearrange("(r q) n -> q r n", q=32)
        P.op("pool", lambda e: e.memset(G[:, 1:3, :, :], 0.0), w=["G"])
        P.dma("sp", "hl", [(hs[:], hs_d[:, :, :]), (G[:, 0, :, :], tg3),
                           (G[2:32, 1, :, :], tg3[0:30, :, :]), (G[0:2, 2, :, :], tg3[30:32, :, :])], w=["G", "hs"])
        first = True
        for v in range(3):
            for r in range(8):
                sc = hs[0:32, lay * 3 + v, r:r + 1]
                if first:
                    P.op("dve", lambda e, v=v, r=r, sc=sc: e.tensor_scalar(out=acc[:], in0=G[:, v, r, :], scalar1=sc, scalar2=None,
                                                                           op0=ALU.mult), r=["G", "hs"], w=["acc"])
                    first = False
                else:
                    P.op("dve", lambda e, v=v, r=r, sc=sc: e.scalar_tensor_tensor(out=acc[:], in0=G[:, v, r, :], scalar=sc, in1=acc[:],
                                                                                  op0=ALU.mult, op1=ALU.add), r=["G", "hs", "acc"], w=["acc"])
        P.dma("sp", "ho", [(xh_d[:, :], acc[:])], r=["acc"], w=["xh"])
        CX.done(P, ["xh"])


def emit_relayout(ctx):
    CX = ctx
    hg_d = CX.I("hgath", [8 * 2048, D])
    sc_d = CX.I("selc", [128, 8])
    y_d = CX.O("y", [16, 128, D])
    with ExitStack() as es:
        P = CX.prog(es)
        cand = [P.sb(f"cand{i}", [128, 8, D], F32) for i in range(2)]
        acc = [P.sb(f"acc{i}", [128, D], F32) for i in range(2)]
        sc = P.sb("sc", [128, 8], F32)
        P.dma("sp", "rs", [(sc[:], sc_d[:, :])], w=["sc"])
        for i in range(16):
            cb, ck = cand[i % 2], f"cand{i % 2}"
            ab, ak = acc[i % 2], f"acc{i % 2}"
            row0 = ((i // 2) * 16 + 8 * (i % 2)) * 128
            P.dma("sp", ck, [(cb[:], hg_d[row0:row0 + 1024, :].rearrange("(c p) n -> p c n", p=128))], w=[ck])
            P.op("dve", lambda e, cb=cb, ab=ab: e.tensor_scalar(out=ab[:], in0=cb[:, 0, :], scalar1=sc[:, 0:1], scalar2=None, op0=ALU.mult),
                 r=[ck, "sc"], w=[ak])
            for c2 in range(1, 8):
                P.op("dve", lambda e, cb=cb, ab=ab, c2=c2: e.scalar_tensor_tensor(out=ab[:], in0=cb[:, c2, :], scalar=sc[:, c2:c2 + 1], in1=ab[:],
                                                                                  op0=ALU.mult, op1=ALU.add), r=[ck, "sc", ak], w=[ak])
            P.dma("sp", f"ro{i % 2}", [(y_d[i, :, :], ab[:])], r=[ak], w=[f"yo{i}"])
        CX.done(P, [f"yo{i}" for i in range(16)])


def build_fused():
    nc = new_nc()
    E = lambda name, shape, dt=F32: din(nc, name, shape, dt)
    x_d = E("x", [16, 128, D])
    a_norm = E("a_norm_b", [2, 128, D]); a_win = E("a_w_in", [2, D, MIN_]); a_bg = E("a_bg_b", [2, 128, 8])
    a_hn = E("a_h_norm_b", [2, 128, D]); a_wout = E("a_w_out", [2, D, D])
    kvn = E("kv_norm_b", [128, D]); wkv = E("w_kv", [D, 2 * D]); kn = E("k_norm_b", [128, 128])
    b_norm = E("b_norm_b", [2, 128, D]); b_wq = E("b_w_q", [2, D, D]); b_qn = E("b_q_norm_b", [2, 128, 128]); b_wo = E("b_w_o", [2, D, D])
    f_norm = E("f_norm_b", [4, 128, D]); f_wup = E("f_w_up", [4, D, 2 * DFF]); f_cw = E("f_cw", [4, 128, 44, 3])
    f_cb = E("f_cb", [4, 128, 44]); f_wdn = E("f_w_down", [4, DFF, D])
    ident = E("ident", [128, 128], BF16); tri = E("tri", [128, 128])
    cmask = E("cmask", [128, 8]); hsel = E("hsel", [128, 6, 8]); selc = E("selc", [128, 8])
    cosC = E("cosC", [128, 16, 16]); sinC = E("sinC", [128, 16, 16]); cosI = E("cosI", [128, 16, 16]); sinI = E("sinI", [128, 16, 16])
    gbias = E("gbias", [128, 16, 64]); own = E("own", [128, 16, 64]); diag = E("diag", [128, 8, 128], BF16)
    y_d = dout(nc, "y", [16, 128, D])
    T = lambda name, shape, dt=F32: nc.dram_tensor(name, list(shape), dt)
    hA = T("hA", [2048, D]); hB = T("hB", [2048, D])
    st_src = T("st_src", [128, 1032]); st_g = T("st_g", [8 * 128, 1032])
    tl_src = T("tl_src", [32, D]); tl_g = T("tl_g", [8 * 32, D]); xh = T("xh_d", [32, D])
    kt_src = T("kt_src", [8 * 128, 2048], BF16); kt_g = T("kt_g", [64 * 128, 2048], BF16)
    v_src = T("v_src", [8 * 128, 16 * 130], BF16); v_g = T("v_g", [64 * 128, 16 * 130], BF16)
    km_src = T("km_src", [128, 64]); km_g = T("km_g", [8 * 128, 64])
    hgath = T("hgath", [8 * 2048, D])
    t3 = lambda t: t.ap().rearrange("(t p) n -> t p n", p=128)
    hA3, hB3 = t3(hA), t3(hB)
    with ExitStack() as es_sem:
        P = Prog(nc, None, es_sem)
        ctx = lambda aps: Ctx(nc, P, aps)

        def gather(src, dst):
            P.begin_phase(None)
            P.coll("AllGather", src.ap().opt(), dst.ap().opt())
            P.end_phase()

        def ffn(l, lay, xin, yout):
            emit_tails(ctx({"x": xin, "tails": tl_src.ap()}))
            gather(tl_src, tl_g)
            emit_halo(ctx({"tails_g": tl_g.ap(), "hsel": hsel, "xh": xh.ap()}), lay)
            build_ffn(ctx({"x": xin, "xh": xh.ap(), "g": f_norm[l, :, :], "wup": f_wup[l, :, :], "cw": f_cw[l, :, :, :],
                           "cb": f_cb[l, :, :], "wdn": f_wdn[l, :, :], "ident": ident, "y": yout}))

        cur = x_d
        for l in range(2):
            base = {"x": cur, "g": a_norm[l, :, :], "win": a_win[l, :, :], "bg": a_bg[l, :, :], "ident": ident, "tri": tri}
            build_mlstm(True, ctx(dict(base, cst=st_src.ap()[:, 0:1028], fst=st_src.ap()[:, 1028:1032])))
            gather(st_src, st_g)
            build_mlstm(False, ctx(dict(base, gh=a_hn[l, :, :], wout=a_wout[l, :, :],
                                        cst_in=st_g.ap().rearrange("(r p) n -> r p n", p=128)[:, :, 0:1028],
                                        fst_in=st_g.ap().rearrange("(r p) n -> p r n", p=128)[:, :, 1028:1032],
                                        cmask=cmask, y=hA3)))
            ffn(l, 0, hA3, hB3)
            cur = hB3
        build_kv(ctx({"x": hB3, "g": kvn, "wkv": wkv, "gk": kn, "cos": cosC, "sin": sinC, "ident": ident,
                      "kt": kt_src.ap().rearrange("(h p) n -> h p n", p=128),
                      "v": v_src.ap().rearrange("(h p) (t e) -> h p t e", p=128, e=130),
                      "km": km_src.ap().rearrange("p (h b) -> p h b", b=8)}))
        gather(kt_src, kt_g)
        gather(v_src, v_g)
        gather(km_src, km_g)
        gather(hB, hgath)
        emit_relayout(ctx({"hgath": hgath.ap(), "selc": selc, "y": hA3}))
        for j in range(2):
            build_moba(ctx({"x": hA3, "g": b_norm[j, :, :], "wq": b_wq[j, :, :], "wo": b_wo[j, :, :], "gq": b_qn[j, :, :],
                            "cos": cosI, "sin": sinI, "ident": ident,
                            "kt": kt_g.ap().rearrange("(r h p) n -> r h p n", h=8, p=128),
                            "v": v_g.ap().rearrange("(r h p) (t e) -> r h p t e", h=8, p=128, e=130),
                            "km": km_g.ap().rearrange("(r p) (h b) -> r p h b", p=128, b=8),
                            "gbias": gbias, "own": own, "diag": diag, "y": hB3}))
            ffn(2 + j, 1, hB3, hA3 if j == 0 else y_d)
        print("fused instructions:", P.n_ins)
    return nc


_NC = [None]


def fused_inputs(c, x, a_norm, a_w_in, a_b_gates, a_h_norm, a_w_out, kv_norm, w_kv, k_norm,
                 b_norm, b_w_q, b_q_norm, b_w_o, f_norm, f_w_up, f_conv_w, f_conv_b, f_w_down, cos_t, sin_t):
    rep = lambda a, n=128: np.ascontiguousarray(np.broadcast_to(a[..., None, :], a.shape[:-1] + (n, a.shape[-1])))
    contig = list(range(16 * c, 16 * c + 16))
    inter = [8 * i + c for i in range(16)]
    tab = lambda t, ids: np.ascontiguousarray(np.stack([t[j * 128:(j + 1) * 128] for j in ids], 1))
    cmask = np.zeros((128, 8), np.float32); cmask[:, :c] = 1.0
    hsel = np.zeros((128, 6, 8), np.float32)
    hsel[:, 1, c] = 1.0
    if c >= 1:
        hsel[:, 2, c - 1] = 1.0
        hsel[:, 3, c - 1] = 1.0
    else:
        hsel[:, 4, 7] = 1.0
    selc = np.zeros((128, 8), np.float32); selc[:, c] = 1.0
    gbb, ownb, diagb = moba_consts(c)
    return {
        "x": np.ascontiguousarray(x[c * 2048:(c + 1) * 2048].reshape(16, 128, D)),
        "a_norm_b": rep(a_norm), "a_w_in": a_w_in, "a_bg_b": rep(a_b_gates.reshape(2, 8)), "a_h_norm_b": rep(a_h_norm), "a_w_out": a_w_out,
        "kv_norm_b": rep(kv_norm), "w_kv": w_kv, "k_norm_b": rep(k_norm),
        "b_norm_b": rep(b_norm), "b_w_q": b_w_q, "b_q_norm_b": rep(b_q_norm), "b_w_o": b_w_o,
        "f_norm_b": rep(f_norm), "f_w_up": f_w_up,
        "f_cw": np.ascontiguousarray(f_conv_w.reshape(4, 3, 44, 128).transpose(0, 3, 2, 1)),
        "f_cb": np.ascontiguousarray(f_conv_b.reshape(4, 44, 128).transpose(0, 2, 1)), "f_w_down": f_w_down,
        "ident": IDENT_BF, "tri": TRI, "cmask": cmask, "hsel": hsel, "selc": selc,
        "cosC": tab(cos_t, contig), "sinC": tab(sin_t, contig), "cosI": tab(cos_t, inter), "sinI": tab(sin_t, inter),
        "gbias": gbb, "own": ownb, "diag": diagb,
    }


def kernel(x, a_norm, a_w_in, a_b_gates, a_h_norm, a_w_out, kv_norm, w_kv, k_norm,
           b_norm, b_w_q, b_q_norm, b_w_o, f_norm, f_w_up, f_conv_w, f_conv_b, f_w_down):
    f32 = lambda a: np.ascontiguousarray(np.asarray(a, dtype=np.float32))
    args = [f32(a) for a in (x, a_norm, a_w_in, a_b_gates, a_h_norm, a_w_out, kv_norm, w_kv, k_norm,
                             b_norm, b_w_q, b_q_norm, b_w_o, f_norm, f_w_up, f_conv_w, f_conv_b, f_w_down)]
    args[0] = args[0].reshape(S, D)
    cos_t, sin_t = rope_tables_np()
    if _NC[0] is None:
        _NC[0] = build_fused()
    maps = [fused_inputs(c, *args, cos_t, sin_t) for c in range(NCORES)]
    res = run_bass_kernel_spmd(_NC[0], maps, core_ids=list(range(NCORES)))
    out = np.empty((S, D), np.float32)
    for c in range(NCORES):
        y = res.results[c]["y"]
        for i in range(16):
            j = 8 * i + c
            out[j * 128:(j + 1) * 128] = y[i]
    return out.reshape(1, S, D)
```
